# Optimizing a Trainium2 kernel written in Bass

```python
import math
import jax, jax.numpy as jnp
from jax import lax
import numpy as np

D_MODEL = 1024
BATCH = 16
SEQ = 256
DEPTH = 4
DEC_BATCH = 8
DEC_SEQ = 4096
PAST_LEN = 256

GRID_W = 64
FOU_GROUPS = 4
FOU_GROUP_W = 128
FOU_W = FOU_GROUPS * FOU_GROUP_W
GLA_HEADS = 4
GLA_DK = 64
GLA_DV = 128
GLA_K = GLA_HEADS * GLA_DK
GLA_V = GLA_HEADS * GLA_DV
GATE_RANK = 16
GATE_TEMP = 16.0
GLA_CHUNK = 64
DIFF_HEADS = 4
DIFF_HEAD_DIM = 64
DIFF_V_HEAD = 2 * DIFF_HEAD_DIM
DIFF_QK = DIFF_HEADS * 2 * DIFF_HEAD_DIM
DIFF_V = DIFF_HEADS * DIFF_V_HEAD
Q_BLOCK = 128
ROPE_AXIS_DIM = DIFF_HEAD_DIM // 2
ROPE_FREQS = ROPE_AXIS_DIM // 2
ROPE_BASE = 10000.0
N_BRANCH = 3
D_FF = ((8 * D_MODEL // 3 + 255) // 256) * 256
PROJ_SIZES = (FOU_W, GLA_K, GLA_K, GLA_V, GLA_V, 2 * GATE_RANK, DIFF_QK, DIFF_QK, DIFF_V, N_BRANCH * D_MODEL)
IN_COLS = FOU_W + 2 * GLA_K + 2 * GLA_V + 2 * GATE_RANK + 2 * DIFF_QK + DIFF_V + N_BRANCH * D_MODEL

kernel_name = 'hybrid_fnet_gla_diffattn_prefix_dit_step'


def rms_norm(x, gain, eps=1e-6):
    xf = x.astype(jnp.float32)
    y = xf * lax.rsqrt(jnp.mean(xf * xf, axis=-1, keepdims=True) + eps)
    return (y * gain.astype(jnp.float32)).astype(x.dtype)


def split_projection(proj):
    outs, start = [], 0
    for size in PROJ_SIZES:
        outs.append(proj[..., start:start + size])
        start += size
    return outs


def fourier_mix(u):
    b, t, _ = u.shape
    ug = u.astype(jnp.float32).reshape(b, t, FOU_GROUPS, FOU_GROUP_W)
    f = jnp.fft.fft2(ug, axes=(1, 3), norm='ortho').real
    return f.reshape(b, t, FOU_W).astype(u.dtype)


def gla_log_gates(a_lr, w_a2, b_a):
    b, t = a_lr.shape[:2]
    z = jnp.einsum('btdr,drk->btdk', a_lr.reshape(b, t, 2, GATE_RANK), w_a2) + b_a
    return (jax.nn.log_sigmoid(z.astype(jnp.float32)) / GATE_TEMP).reshape(b, t, 2, GLA_HEADS, GLA_DK)


def gla_scan(q, k, v, log_a, s0):
    b, t, h, dk = q.shape
    dv = v.shape[-1]
    n = t // GLA_CHUNK
    def chunks(a):
        return jnp.moveaxis(a.astype(jnp.float32).reshape(b, n, GLA_CHUNK, *a.shape[2:]), 1, 0)
    causal = jnp.tril(jnp.ones((GLA_CHUNK, GLA_CHUNK), dtype=bool))
    def step(s, inp):
        qc, kc, vc, gc = inp
        cum = jnp.cumsum(gc, axis=1)
        o_inter = jnp.einsum('bihk,bhkv->bihv', qc * jnp.exp(cum), s)
        rel = cum[:, :, None] - cum[:, None, :]
        decay = jnp.exp(jnp.where(causal[None, :, :, None, None], rel, -jnp.inf))
        att = jnp.einsum('bihk,bjhk,bijhk->bhij', qc, kc, decay)
        o = o_inter + jnp.einsum('bhij,bjhv->bihv', att, vc)
        last = cum[:, -1]
        k_dec = kc * jnp.exp(last[:, None] - cum)
        s_new = jnp.exp(last)[..., None] * s + jnp.einsum('bjhk,bjhv->bhkv', k_dec, vc)
        return s_new, o
    s_fin, o = lax.scan(step, s0.astype(jnp.float32), (chunks(q), chunks(k), chunks(v), chunks(log_a)))
    o = jnp.moveaxis(o, 0, 1).reshape(b, t, h, dv)
    return o, s_fin


def gla_bidirectional(q, k, v, log_a, s0):
    o_f, s_f = gla_scan(q, k, v, log_a[:, :, 0], s0[:, 0])
    flip = lambda a: jnp.flip(a, axis=1)
    o_b, s_b = gla_scan(flip(q), flip(k), flip(v), flip(log_a[:, :, 1]), s0[:, 1])
    return o_f + flip(o_b), jnp.stack([s_f, s_b], axis=1)


def axial_rope(n_tokens):
    rows = n_tokens // GRID_W
    row = jnp.repeat(jnp.arange(rows), GRID_W).astype(jnp.float32)
    col = jnp.tile(jnp.arange(GRID_W), rows).astype(jnp.float32)
    inv = ROPE_BASE ** (-jnp.arange(ROPE_FREQS, dtype=jnp.float32) * 2.0 / ROPE_AXIS_DIM)
    ang_r = row[:, None] * inv
    ang_c = col[:, None] * inv
    ang = jnp.concatenate([ang_r, ang_r, ang_c, ang_c], axis=-1)
    return jnp.cos(ang), jnp.sin(ang)


def rotate_half_axial(x):
    xs = x.reshape(*x.shape[:-1], 2, 2, ROPE_FREQS)
    return jnp.stack([-xs[..., 1, :], xs[..., 0, :]], axis=-2).reshape(x.shape)


def apply_rope(x, cos, sin):
    xf = x.astype(jnp.float32)
    cs = cos[None, :, None, None, :]
    sn = sin[None, :, None, None, :]
    return (xf * cs + rotate_half_axial(xf) * sn).astype(x.dtype)


def diff_attend(q, k, v, lam):
    b, tq, h, _, d = q.shape
    dv = v.shape[-1]
    nb = tq // Q_BLOCK
    qb = jnp.moveaxis(q.reshape(b, nb, Q_BLOCK, h, 2, d), 1, 0)
    kf = k.astype(jnp.float32)
    vf = v.astype(jnp.float32)
    scale = d ** -0.5
    def block(qblk):
        s = jnp.einsum('bqhmd,bkhmd->bhmqk', qblk.astype(jnp.float32), kf) * scale
        p = jax.nn.softmax(s, axis=-1)
        w = p[:, :, 0] - lam * p[:, :, 1]
        return jnp.einsum('bhqk,bkhv->bqhv', w, vf)
    o = lax.map(block, qb)
    return jnp.moveaxis(o, 0, 1).reshape(b, tq, h, dv).astype(v.dtype)


def trunk_layer(x, cond, layer_idx, p, rope, ctx):
    b, t, _ = x.shape
    mod = jnp.einsum('bd,de->be', jax.nn.silu(cond), p['w_mod']) + p['b_mod']
    sh1, sc1, g1, sh2, sc2, g2 = jnp.split(mod[:, None, :], 6, axis=-1)
    h = rms_norm(x, p['norm1']) * (1 + sc1) + sh1
    u_f, q_g, k_g, v_g, r_g, a_g, q_d, k_d, v_d, gates = split_projection(h @ p['w_in'])

    y_f = fourier_mix(u_f) @ p['w_fou']

    q_g = q_g.reshape(b, t, GLA_HEADS, GLA_DK) * (GLA_DK ** -0.5)
    k_g = k_g.reshape(b, t, GLA_HEADS, GLA_DK)
    v_g = v_g.reshape(b, t, GLA_HEADS, GLA_DV)
    log_a = gla_log_gates(a_g, p['w_gla_a2'], p['b_gla_a'])
    s0 = jnp.zeros((b, 2, GLA_HEADS, GLA_DK, GLA_DV), jnp.float32) if ctx is None else ctx[0]
    o_g, s_fin = gla_bidirectional(q_g, k_g, v_g, log_a, s0)
    o_g = rms_norm(o_g.astype(x.dtype), p['gla_norm']) * jax.nn.silu(r_g.reshape(b, t, GLA_HEADS, GLA_DV))
    y_g = o_g.reshape(b, t, GLA_V) @ p['w_gla_o']

    q_d = rms_norm(q_d.reshape(b, t, DIFF_HEADS, 2, DIFF_HEAD_DIM), p['diff_qk_norm'][0])
    k_d = rms_norm(k_d.reshape(b, t, DIFF_HEADS, 2, DIFF_HEAD_DIM), p['diff_qk_norm'][1])
    v_d = v_d.reshape(b, t, DIFF_HEADS, DIFF_V_HEAD)
    lp = p['diff_lambda'].astype(jnp.float32)
    lam_init = 0.8 - 0.6 * math.exp(-0.3 * layer_idx)
    lam = jnp.exp(jnp.sum(lp[0] * lp[1])) - jnp.exp(jnp.sum(lp[2] * lp[3])) + lam_init
    if ctx is None:
        o_d = diff_attend(q_d, k_d, v_d, lam)
    else:
        cos, sin = rope
        keys = jnp.concatenate([apply_rope(k_d, cos, sin), ctx[1].astype(k_d.dtype)], axis=1)
        vals = jnp.concatenate([v_d, ctx[2].astype(v_d.dtype)], axis=1)
        o_d = diff_attend(apply_rope(q_d, cos, sin), keys, vals, lam)
    o_d = rms_norm(o_d, p['diff_norm']) * (1.0 - lam_init)
    y_d = o_d.reshape(b, t, DIFF_V) @ p['w_diff_o']

    gf, gg, gd = jnp.split(jax.nn.sigmoid(gates), N_BRANCH, axis=-1)
    merged = gf * y_f + gg * y_g + gd * y_d
    x = x + g1 * (merged @ p['w_out'])

    h2 = rms_norm(x, p['norm2']) * (1 + sc2) + sh2
    ff = (jax.nn.silu(h2 @ p['w_ff_gate']) * (h2 @ p['w_ff_up'])) @ p['w_ff_down']
    x = x + g2 * ff
    return x, s_fin, k_d, v_d


def setup_inputs(seed: int = 0) -> dict:
    key = jax.random.key(seed)
    ks = jax.random.split(key, 25)
    def nrm(k, shape, s=1.0):
        return jax.random.normal(k, shape, jnp.float32) * s
    D = D_MODEL
    return {
        'x_prompt': nrm(ks[0], (BATCH, SEQ, D)),
        'x_sample': nrm(ks[1], (DEC_BATCH, DEC_SEQ, D)),
        'c': nrm(ks[2], (DEC_BATCH, D)),
        'cache_diff_k': nrm(ks[3], (DEC_BATCH, DEPTH, PAST_LEN, DIFF_HEADS, 2, DIFF_HEAD_DIM)),
        'cache_diff_v': nrm(ks[4], (DEC_BATCH, DEPTH, PAST_LEN, DIFF_HEADS, DIFF_V_HEAD)),
        'state_gla': nrm(ks[5], (DEC_BATCH, DEPTH, 2, GLA_HEADS, GLA_DK, GLA_DV), 0.5),
        'c_ctx': nrm(ks[6], (D,)),
        'w_mod': nrm(ks[7], (DEPTH, D, 6 * D), 0.5 * D ** -0.5),
        'b_mod': nrm(ks[8], (DEPTH, 6 * D), 0.02),
        'norm1': 1.0 + nrm(ks[9], (DEPTH, D), 0.02),
        'norm2': 1.0 + nrm(ks[10], (DEPTH, D), 0.02),
        'w_in': nrm(ks[11], (DEPTH, D, IN_COLS), D ** -0.5),
        'w_gla_a2': nrm(ks[12], (DEPTH, 2, GATE_RANK, GLA_K), GATE_RANK ** -0.5),
        'b_gla_a': nrm(ks[13], (DEPTH, 2, GLA_K), 0.1),
        'gla_norm': 1.0 + nrm(ks[14], (DEPTH, GLA_DV), 0.02),
        'diff_qk_norm': 1.0 + nrm(ks[15], (DEPTH, 2, DIFF_HEAD_DIM), 0.02),
        'diff_lambda': nrm(ks[16], (DEPTH, 4, DIFF_HEAD_DIM), 0.1),
        'diff_norm': 1.0 + nrm(ks[17], (DEPTH, DIFF_V_HEAD), 0.02),
        'w_fou': nrm(ks[18], (DEPTH, FOU_W, D), FOU_W ** -0.5),
        'w_gla_o': nrm(ks[19], (DEPTH, GLA_V, D), GLA_V ** -0.5),
        'w_diff_o': nrm(ks[20], (DEPTH, DIFF_V, D), DIFF_V ** -0.5),
        'w_out': nrm(ks[21], (DEPTH, D, D), D ** -0.5),
        'w_ff_gate': nrm(ks[22], (DEPTH, D, D_FF), D ** -0.5),
        'w_ff_up': nrm(ks[23], (DEPTH, D, D_FF), D ** -0.5),
        'w_ff_down': nrm(ks[24], (DEPTH, D_FF, D), D_FF ** -0.5),
    }


def reference(x_prompt, x_sample, c, cache_diff_k, cache_diff_v, state_gla, c_ctx,
              w_mod, b_mod, norm1, norm2, w_in, w_gla_a2, b_gla_a, gla_norm,
              diff_qk_norm, diff_lambda, diff_norm, w_fou, w_gla_o, w_diff_o, w_out,
              w_ff_gate, w_ff_up, w_ff_down):
    def layer_params(l):
        return {
            'w_mod': w_mod[l], 'b_mod': b_mod[l], 'norm1': norm1[l], 'norm2': norm2[l],
            'w_in': w_in[l], 'w_gla_a2': w_gla_a2[l], 'b_gla_a': b_gla_a[l],
            'gla_norm': gla_norm[l], 'diff_qk_norm': diff_qk_norm[l],
            'diff_lambda': diff_lambda[l], 'diff_norm': diff_norm[l], 'w_fou': w_fou[l],
            'w_gla_o': w_gla_o[l], 'w_diff_o': w_diff_o[l], 'w_out': w_out[l],
            'w_ff_gate': w_ff_gate[l], 'w_ff_up': w_ff_up[l], 'w_ff_down': w_ff_down[l],
        }

    cond_ctx = c_ctx[None, :]
    y_prompt = x_prompt
    k_list, v_list, s_list = [], [], []
    for l in range(DEPTH):
        y_prompt, s_l, k_l, v_l = trunk_layer(y_prompt, cond_ctx, l, layer_params(l), None, None)
        s_list.append(s_l)
        k_list.append(k_l)
        v_list.append(v_l)

    rope = axial_rope(x_sample.shape[1])
    y_sample = x_sample
    for l in range(DEPTH):
        ctx = (state_gla[:, l], cache_diff_k[:, l], cache_diff_v[:, l])
        y_sample, _, _, _ = trunk_layer(y_sample, c, l, layer_params(l), rope, ctx)

    new_cache_diff_k = jnp.stack(k_list, axis=1)
    new_cache_diff_v = jnp.stack(v_list, axis=1)
    new_state_gla = jnp.stack(s_list, axis=1)
    return (y_prompt, y_sample, new_cache_diff_k, new_cache_diff_v, new_state_gla)
```

```python
import math
from contextlib import ExitStack
import numpy as np
import ml_dtypes
import concourse.bass as bass
import concourse.mybir as mybir
from concourse.bass_utils import run_bass_kernel_spmd

F32 = mybir.dt.float32
BF16 = mybir.dt.bfloat16
AF = mybir.ActivationFunctionType
ALU = mybir.AluOpType
AX = mybir.AxisListType

D = 1024
NKC = 8
INC = 6688
DFF = 2816
NFF = 22
SPL = 256
PAST = 256
EPS = 1e-6
C_UF, C_QG, C_KG, C_VG, C_RG, C_AG, C_QD, C_KD, C_VD, C_GT = 0, 512, 768, 1024, 1536, 2048, 2080, 2592, 3104, 3616


class Ev:
    __slots__ = ("sem", "sid", "val", "owner", "kind")

    def __init__(self, sem, sid, val, owner, kind):
        self.sem, self.sid, self.val, self.owner, self.kind = sem, sid, val, owner, kind


class Buf:
    __slots__ = ("w", "r")

    def __init__(self):
        self.w = None
        self.r = {}


class MK:
    CE = ("pe", "act", "dve", "pool")
    KD = 6

    def __init__(self, nc):
        self.nc = nc
        self.E = dict(pe=nc.tensor, act=nc.scalar, dve=nc.vector, pool=nc.gpsimd, sp=nc.sync)
        self.nsem = 0
        self.csem = {}
        self.cnt = {}
        for e in self.CE:
            self.csem[e] = self._newsem("c_" + e)
            self.cnt[e] = 0
        self.waited = {e: {} for e in self.E}
        self.dq = {}
        for q in ("sp", "pool", "act"):
            self.dq[q] = dict(sems=[self._newsem("d_" + q) for _ in range(self.KD)], vals=[0] * self.KD, n=0)
        self.same_sync = True
        self.nins = 0

    def _newsem(self, name):
        self.nsem += 1
        return (self.nc.alloc_semaphore(name=f"{name}_{self.nsem}"), self.nsem)

    def _wait(self, e, ev):
        w = self.waited[e]
        if w.get(ev.sid, 0) >= ev.val:
            return
        self.E[e].wait_ge(ev.sem, ev.val)
        w[ev.sid] = ev.val

    def _deps(self, reads, writes):
        evs = []
        for b in reads:
            if b.w is not None:
                evs.append(b.w)
        for b in writes:
            if b.w is not None:
                evs.append(b.w)
            evs.extend(b.r.values())
        return evs

    def _record(self, ev, reads, writes):
        for b in reads:
            b.r[ev.sid] = ev
        for b in writes:
            b.w = ev
            b.r = {}

    def op(self, e, fn, reads=(), writes=()):
        for ev in self._deps(reads, writes):
            if ev.kind == "c" and ev.owner == e and (e == "pe" or not self.same_sync):
                continue
            self._wait(e, ev)
        ins = fn(self.E[e])
        self.cnt[e] += 1
        self.nins += 1
        sem, sid = self.csem[e]
        ins.then_inc(sem, 1)
        ev = Ev(sem, sid, self.cnt[e], e, "c")
        self._record(ev, reads, writes)
        return ev

    def dma(self, q, out, in_, reads=(), writes=(), **kw):
        for ev in self._deps(reads, writes):
            self._wait(q, ev)
        d = self.dq[q]
        slot = d["n"] % self.KD
        d["n"] += 1
        sem, sid = d["sems"][slot]
        if d["vals"][slot] > 0:
            self._wait(q, Ev(sem, sid, d["vals"][slot], q, "d"))
        ins = self.E[q].dma_start(out=out, in_=in_, **kw)
        d["vals"][slot] += 16
        ins.then_inc(sem, 16)
        self.nins += 1
        ev = Ev(sem, sid, d["vals"][slot], q, "d")
        self._record(ev, reads, writes)
        return ev

    def barrier(self):
        evs = []
        for e in self.CE:
            if self.cnt[e] > 0:
                sem, sid = self.csem[e]
                evs.append(Ev(sem, sid, self.cnt[e], e, "c"))
        for q, d in self.dq.items():
            for (sem, sid), v in zip(d["sems"], d["vals"]):
                if v > 0:
                    evs.append(Ev(sem, sid, v, q, "d"))
        for e in self.E:
            for ev in evs:
                self._wait(e, ev)
        for e in self.CE:
            if self.cnt[e] > 12000:
                self.csem[e] = self._newsem("c_" + e)
                self.cnt[e] = 0
        for q, d in self.dq.items():
            for i in range(self.KD):
                if d["vals"][i] > 12000:
                    d["sems"][i] = self._newsem("d_" + q)
                    d["vals"][i] = 0


class Pool_:
    def __init__(self, tiles):
        self.tiles = tiles
        self.bufs = [Buf() for _ in tiles]
        self.i = 0

    def next(self):
        k = self.i % len(self.tiles)
        self.i += 1
        return self.tiles[k], self.bufs[k]


def run_pipeline(gens, depth):
    active = []
    it = iter(gens)
    done = False
    while True:
        if not done and len(active) < depth:
            try:
                active.append(next(it))
            except StopIteration:
                done = True
        if not active:
            break
        nxt = []
        for g in active:
            try:
                next(g)
                nxt.append(g)
            except StopIteration:
                pass
        active = nxt


class Builder:
    def __init__(self, L=4, TS=4096, stop_after=None, debug=()):
        self.L, self.TS = L, TS
        self.NTOK = 2 * SPL + TS
        self.NT = self.NTOK // 128
        self.NB = self.NTOK // 512
        self.seqs = [(0, SPL), (SPL, SPL), (2 * SPL, TS)]
        self.stop_after = stop_after
        self.debug = set(debug)
        self.nc = bass.Bass("TRN2", target_bir_lowering=False)
        self.mk = MK(self.nc)
        self.dram = {}

    def din(self, name, shape, dt=F32):
        self.dram[name] = self.nc.dram_tensor(name, list(shape), dt, kind="ExternalInput").ap()
        return self.dram[name]

    def dout(self, name, shape, dt=F32):
        self.dram[name] = self.nc.dram_tensor(name, list(shape), dt, kind="ExternalOutput").ap()
        return self.dram[name]

    def dscr(self, name, shape, dt):
        kind = "ExternalOutput" if name in self.debug else "Internal"
        self.dram[name] = self.nc.dram_tensor(name, list(shape), dt, kind=kind).ap()
        return self.dram[name]

    def dbg(self, name, ap, reads):
        if "dbg_" + name in self.debug:
            o = self.nc.dram_tensor("dbg_" + name, list(ap.shape), ap.dtype, kind="ExternalOutput").ap()
            self.mk.dma("sp", o, ap, reads=reads)

    def sb(self, st, name, shape, dt):
        self.uid = getattr(self, "uid", 0) + 1
        return st.enter_context(self.nc.sbuf_tensor(f"{name}_u{self.uid}", list(shape), dt))

    def ps(self, st, name, shape, dt=F32):
        self.uid = getattr(self, "uid", 0) + 1
        return st.enter_context(self.nc.psum_tensor(f"{name}_u{self.uid}", list(shape), dt))

    def sbpool(self, st, name, shape, dt, n):
        return Pool_([self.sb(st, f"{name}{i}", shape, dt) for i in range(n)])

    def pspool(self, st, name, shape, dt, n):
        return Pool_([self.ps(st, f"{name}{i}", shape, dt) for i in range(n)])

    def filler_setup(self, st):
        self.fz = self.sb(st, "fill_z", [128, 512], BF16)
        self.fp = self.ps(st, "fill_p", [128, 512], F32)
        self.fb = Buf()
        self.mk.op("pool", lambda e: e.memset(self.fz[:], 0.0), writes=[self.fb])

    def filler(self, n):
        def mm(e):
            for _ in range(n):
                ins = e.matmul(self.fp[:], lhsT=self.fz[:, 0:128], rhs=self.fz[:], start=True, stop=True)
            return ins
        self.mk.op("pe", mm, reads=[self.fb], writes=[])

    def cond_of_tile(self, t):
        return 0 if t < (2 * SPL) // 128 else 1

    def declare(self):
        L, TS, NTOK = self.L, self.TS, self.NTOK
        di = self.din
        di("xp", [2 * SPL, D]); di("xs", [TS, D]); di("cvec", [2, D])
        di("cache_k", [L, PAST, 512]); di("cache_v", [L, PAST, 512]); di("state", [L, 2, 4, 64, 128])
        di("w_mod", [L, D, 6 * D]); di("b_mod", [L, 6 * D]); di("norm1", [L, D]); di("norm2", [L, D])
        di("w_in", [L, D, INC]); di("w_gla_a2", [L, 2, 16, 256]); di("b_gla_a", [L, 2, 256])
        di("gla_norm", [L, 128]); di("diff_qk_norm", [L, 2, 64]); di("diff_lambda", [L, 4, 64]); di("diff_norm", [L, 128])
        di("w_fou", [L, 512, D]); di("w_gla_o", [L, 512, D]); di("w_diff_o", [L, 512, D]); di("w_out", [L, D, D])
        di("w_ff_gate", [L, D, DFF]); di("w_ff_up", [L, D, DFF]); di("w_ff_down", [L, DFF, D])
        di("k_ident", [128, 128]); di("k_csc", [128, 256])
        di("k_ctP", [SPL, SPL], BF16); di("k_nstP", [SPL, SPL], BF16)
        di("k_ctS", [TS, TS], BF16); di("k_nstS", [TS, TS], BF16)
        di("k_tri", [64, 6, 64])
        di("k_rope", [TS, 2, 64])
        do = self.dout
        do("yp", [2 * SPL, D]); do("ys", [TS, D])
        do("nk", [2, L, SPL, 512]); do("nv", [2, L, SPL, 512]); do("ns", [2, L, 2, 4, 64, 128])
        ds = self.dscr
        ds("X", [NTOK, D], F32); ds("MOD", [L, 2, 6 * D], F32)
        ds("UFT", [512, NTOK], BF16); ds("QGT", [256, NTOK], BF16); ds("KGT", [256, NTOK], BF16)
        ds("KG", [NTOK, 256], BF16); ds("VG", [NTOK, 512], BF16); ds("RG", [NTOK, 512], BF16)
        ds("LG", [NTOK, 512], F32); ds("QDT", [512, NTOK], BF16); ds("KDT", [512, NTOK], BF16)
        ds("VD", [NTOK, 512], BF16); ds("GT", [3072, NTOK], BF16)
        ds("FT", [512, NTOK], BF16); ds("OGT", [512, NTOK], BF16); ds("ODT", [512, NTOK], BF16)
        ds("OF", [NTOK, 512], F32); ds("OB", [NTOK, 512], F32)
        ds("MT", [D, NTOK], BF16); ds("AT", [DFF, NTOK], BF16)

    def build(self):
        self.declare()
        nc, mk, dr = self.nc, self.mk, self.dram
        L = self.L
        with ExitStack() as st:
            self.hT = self.sb(st, "hT", [128, NKC, self.NTOK], BF16)
            self.hT_b = [Buf() for _ in range(self.NT)]
            self.ident_f = self.sb(st, "ident_f", [128, 128], F32)
            self.ident_b = self.sb(st, "ident_b", [128, 128], BF16)
            self.AB = self.sb(st, "AB", [128, L, 2, 4, 8], F32)
            self.lam = self.sb(st, "lam", [128, L, 2], F32)
            self.cbuf = Buf()
            mk.dma("sp", self.ident_f[:], dr["k_ident"], writes=[self.cbuf])
            mk.op("dve", lambda e: e.tensor_copy(out=self.ident_b[:], in_=self.ident_f[:]), reads=[self.cbuf], writes=[self.cbuf])
            for r0 in range(0, self.NTOK, 512):
                src = dr["xp"][r0:r0 + 512, :] if r0 < 2 * SPL else dr["xs"][r0 - 2 * SPL:r0 - 2 * SPL + 512, :]
                mk.dma("sp", dr["X"][r0:r0 + 512, :], src)
            self.prologue()
            mk.barrier()
            if self.stop_after == "prologue":
                return self.finish()
            self.phase_A(0)
            for l in range(L):
                for ph in (self.phase_B, self.phase_C, self.phase_D, self.phase_E, self.phase_F1, self.phase_F2,
                           self.phase_H, self.phase_I):
                    mk.barrier()
                    ph(l)
                    if self.stop_after == (ph.__name__[6:], l):
                        return self.finish()
            return self.finish()

    def finish(self):
        self.mk.barrier()
        return self.nc

    def prologue(self):
        nc, mk, dr, L = self.nc, self.mk, self.dram, self.L
        with ExitStack() as st:
            c16 = self.sb(st, "c16", [16, 128], F32)
            sT = self.sb(st, "sT", [128, 16], F32)
            bm = self.sb(st, "bm", [2, 6 * D], F32)
            mrow = self.sb(st, "mrow", [2, 6 * D], F32)
            wm = self.sbpool(st, "wm", [128, NKC, 512], F32, 3)
            pm = self.pspool(st, "pm", [128, 512], F32, 2)
            pt = self.ps(st, "pt", [128, 128], F32)
            VR = self.sb(st, "VR", [112, 128], F32)
            VC = self.sb(st, "VC", [128, 112], F32)
            dl = self.sb(st, "dl", [128, 4, 64], F32)
            pr = self.sb(st, "pr", [128, 2, 64], F32)
            sm = self.sb(st, "sm", [128, 2], F32)
            b_c16, b_sT, b_bm, b_mrow, b_pt, b_VR, b_VC, b_dl = (Buf() for _ in range(8))
            b_VRs = (Buf(), Buf(), Buf())
            mk.dma("sp", c16[:], dr["cvec"].rearrange("r (kc p) -> (r kc) p", p=128), writes=[b_c16])
            mk.op("pe", lambda e: e.transpose(out=pt[:, 0:16], in_=c16[:], identity=self.ident_f[0:16, 0:16]),
                  reads=[b_c16, self.cbuf], writes=[b_pt])
            mk.op("act", lambda e: e.activation(out=sT[:], in_=pt[:, 0:16], func=AF.Silu), reads=[b_pt], writes=[b_sT])
            sT2 = self.sb(st, "sT2", [128, NKC, 2], F32)
            mk.op("dve", lambda e: e.tensor_copy(out=sT2[:], in_=sT[:].rearrange("p (r kc) -> p kc r", r=2)), reads=[b_sT], writes=[b_sT])
            sTv = sT2[:]
            self.dbg("sT", sT[:], [b_sT])
            for l in range(L):
                mk.dma("sp", bm[:], dr["b_mod"][l].partition_broadcast(2), writes=[b_bm])
                for cc in range(12):
                    wt, wb_ = wm.next()
                    mk.dma("sp", wt[:], dr["w_mod"][l][:, cc * 512:(cc + 1) * 512].rearrange("(kc p) n -> p kc n", p=128),
                           writes=[wb_])
                    pmt, pmb = pm.next()

                    def mm(e, wt=wt, pmt=pmt):
                        for kc in range(NKC):
                            ins = e.matmul(pmt[0:2, :], lhsT=sTv[:, kc, :], rhs=wt[:, kc, :], start=(kc == 0), stop=(kc == NKC - 1))
                        return ins
                    mk.op("pe", mm, reads=[wb_, b_sT], writes=[pmb])
                    mk.op("dve", lambda e, pmt=pmt, cc=cc: e.tensor_tensor(out=mrow[:, cc * 512:(cc + 1) * 512], in0=pmt[0:2, :],
                                                                          in1=bm[:, cc * 512:(cc + 1) * 512], op=ALU.add),
                          reads=[pmb, b_bm], writes=[b_mrow])
                b_MOD = Buf()
                mk.dma("sp", dr["MOD"][l], mrow[:], reads=[b_mrow], writes=[b_MOD])
                b_V0, b_V1, b_V2 = b_VRs
                mk.dma("sp", VR[0:96, :], dr["MOD"][l].rearrange("r (j p) -> (r j) p", p=128), reads=[b_MOD], writes=[b_V0])
                mk.dma("sp", VR[96:104, :], dr["norm1"][l].rearrange("(j p) -> j p", p=128), writes=[b_V1])
                mk.dma("sp", VR[104:112, :], dr["norm2"][l].rearrange("(j p) -> j p", p=128), writes=[b_V2])
                mk.op("pe", lambda e: e.transpose(out=pt[:, 0:112], in_=VR[:], identity=self.ident_f[0:112, 0:112]),
                      reads=[b_V0, b_V1, b_V2], writes=[b_pt])
                mk.op("dve", lambda e: e.tensor_copy(out=VC[:], in_=pt[:, 0:112]), reads=[b_pt], writes=[b_VC])
                for r in range(2):
                    c0 = r * 48
                    mk.op("dve", lambda e, r=r, c0=c0: e.scalar_tensor_tensor(out=self.AB[:, l, r, 0, :], in0=VC[:, c0 + 8:c0 + 16], scalar=1.0,
                                                                            in1=VC[:, 96:104], op0=ALU.add, op1=ALU.mult),
                          reads=[b_VC], writes=[self.cbuf])
                    mk.op("dve", lambda e, r=r, c0=c0: e.tensor_copy(out=self.AB[:, l, r, 1, :], in_=VC[:, c0:c0 + 8]), reads=[b_VC], writes=[self.cbuf])
                    mk.op("dve", lambda e, r=r, c0=c0: e.scalar_tensor_tensor(out=self.AB[:, l, r, 2, :], in0=VC[:, c0 + 32:c0 + 40], scalar=1.0,
                                                                            in1=VC[:, 104:112], op0=ALU.add, op1=ALU.mult),
                          reads=[b_VC], writes=[self.cbuf])
                    mk.op("dve", lambda e, r=r, c0=c0: e.tensor_copy(out=self.AB[:, l, r, 3, :], in_=VC[:, c0 + 24:c0 + 32]), reads=[b_VC], writes=[self.cbuf])
                mk.dma("sp", dl[:], dr["diff_lambda"][l].rearrange("a d -> (a d)").partition_broadcast(128), writes=[b_dl])
                dlv = dl[:].rearrange("p (a b) d -> p a b d", b=2)
                mk.op("dve", lambda e: e.tensor_tensor(out=pr[:], in0=dlv[:, :, 0, :], in1=dlv[:, :, 1, :], op=ALU.mult), reads=[b_dl], writes=[b_dl])
                mk.op("dve", lambda e: e.reduce_sum(out=sm[:], in_=pr[:], axis=AX.X), reads=[b_dl], writes=[b_dl])
                mk.op("act", lambda e: e.activation(out=sm[:], in_=sm[:], func=AF.Exp), reads=[b_dl], writes=[b_dl])
                lam_init = 0.8 - 0.6 * math.exp(-0.3 * l)
                mk.op("dve", lambda e, l=l, li=lam_init: e.tensor_scalar(out=self.lam[:, l, 0:1], in0=sm[:, 0:1], scalar1=sm[:, 1:2], scalar2=li,
                                                                       op0=ALU.subtract, op1=ALU.add), reads=[b_dl], writes=[self.cbuf])
                mk.op("dve", lambda e, l=l: e.tensor_scalar(out=self.lam[:, l, 1:2], in0=self.lam[:, l, 0:1], scalar1=-1.0, scalar2=None, op0=ALU.mult),
                      reads=[self.cbuf], writes=[self.cbuf])

    def norm_setup(self, st):
        self.n_xn = self.sbpool(st, "n_xn", [128, D], BF16, 3)
        self.n_ss = self.sbpool(st, "n_ss", [128, 2], F32, 4)
        self.n_pT = self.pspool(st, "n_pT", [128, NKC, 128], BF16, 2)

    def norm_gen(self, xt, xb, tile, l, which):
        mk = self.mk
        cond = self.cond_of_tile(tile)
        xn, xnb = self.n_xn.next()
        ss, ssb = self.n_ss.next()
        mk.op("act", lambda e: e.activation(out=xn[:], in_=xt, func=AF.Square, accum_out=ss[:, 0:1]), reads=[xb], writes=[xnb, ssb])
        mk.op("act", lambda e: e.activation(out=ss[:, 1:2], in_=ss[:, 0:1], func=AF.Ln, scale=1.0 / D, bias=EPS), reads=[ssb], writes=[ssb])
        mk.op("act", lambda e: e.activation(out=ss[:, 1:2], in_=ss[:, 1:2], func=AF.Exp, scale=-0.5), reads=[ssb], writes=[ssb])
        mk.op("act", lambda e: e.activation(out=xn[:], in_=xt, func=AF.Copy, scale=ss[:, 1:2]), reads=[xb, ssb], writes=[xnb])
        yield
        pT, pTb = self.n_pT.next()

        def tr(e):
            for kc in range(NKC):
                ins = e.transpose(out=pT[:, kc, :], in_=xn[:, kc * 128:(kc + 1) * 128], identity=self.ident_b[:])
            return ins
        mk.op("pe", tr, reads=[xnb, self.cbuf], writes=[pTb])
        yield
        hb = self.hT_b[tile]
        a_i, b_i = (0, 1) if which == 1 else (2, 3)
        for kc in range(NKC):
            dst = self.hT[:, kc, tile * 128:(tile + 1) * 128]
            A = self.AB[:, l, cond, a_i, kc:kc + 1]
            B = self.AB[:, l, cond, b_i, kc:kc + 1]
            if kc % 2 == 0:
                mk.op("dve", lambda e, dst=dst, A=A, B=B, kc=kc: e.tensor_scalar(out=dst, in0=pT[:, kc, :], scalar1=A, scalar2=B, op0=ALU.mult, op1=ALU.add),
                      reads=[pTb, self.cbuf], writes=[hb])
            else:
                mk.op("act", lambda e, dst=dst, A=A, B=B, kc=kc: e.activation(out=dst, in_=pT[:, kc, :], func=AF.Identity, scale=A, bias=B),
                      reads=[pTb, self.cbuf], writes=[hb])

    def phase_A(self, l):
        mk, dr = self.mk, self.dram
        with ExitStack() as st:
            self.norm_setup(st)
            xp = self.sbpool(st, "a_x", [128, D], F32, 4)

            def tile_gen(t):
                xt, xb = xp.next()
                mk.dma("sp", xt[:], dr["X"][t * 128:(t + 1) * 128, :], writes=[xb])
                yield from self.norm_gen(xt[:], xb, t, l, 1)
            run_pipeline((tile_gen(t) for t in range(self.NT)), 4)

    def phase_B(self, l):
        mk, dr, nc = self.mk, self.dram, self.nc
        NT, NB, NTOK, TS = self.NT, self.NB, self.NTOK, self.TS
        hT, hTb = self.hT, self.hT_b
        NPT = (2 * SPL) // 128
        with ExitStack() as st:
            wf = self.sbpool(st, "b_wf", [128, NKC, 256], F32, 3)
            wb = self.sbpool(st, "b_wb", [128, NKC, 512], BF16, 3)
            pacc = self.pspool(st, "b_pa", [128, 512], F32, 4)
            pTp = self.pspool(st, "b_pT", [128, 4, 128], BF16, 2)
            stF = self.sbpool(st, "b_sF", [128, 512], BF16, 4)
            stT = self.sbpool(st, "b_sT", [128, 4, 512], BF16, 3)
            tmp = self.sbpool(st, "b_tmp", [128, 512], F32, 3)
            rawp = self.sbpool(st, "b_raw", [128, 512], F32, 3)
            sqp = self.sbpool(st, "b_sq", [128, 512], F32, 2)
            up = self.sbpool(st, "b_u", [128, 512], F32, 3)
            wp = self.sbpool(st, "b_w", [128, 512], F32, 2)
            tbf = self.sbpool(st, "b_tbf", [128, 512], BF16, 4)
            small = self.sbpool(st, "b_sm", [128, 16], F32, 6)
            aT = self.sb(st, "b_aT", [33, NTOK], BF16)
            aTb = Buf()
            BDf = self.sb(st, "b_BDf", [33, 512], F32)
            BD = self.sb(st, "b_BD", [33, 512], BF16)
            gqk = self.sb(st, "b_gqk", [128, 2, 64], F32)
            rope = self.sb(st, "b_rope", [128, TS // 128, 2, 64], F32)
            gsw = self.sb(st, "b_gsw", [128, 64], F32)
            tabb = Buf()
            cb = Buf()
            mk.dma("sp", gqk[:], dr["diff_qk_norm"][l].rearrange("a d -> (a d)").partition_broadcast(128), writes=[cb])
            bdb = Buf()
            mk.op("pool", lambda e: e.memset(BDf[:], 0.0), writes=[bdb])
            mk.dma("sp", BDf[0:16, 0:256], dr["w_gla_a2"][l, 0], writes=[bdb])
            mk.dma("sp", BDf[16:32, 256:512], dr["w_gla_a2"][l, 1], writes=[bdb])
            mk.dma("sp", BDf[32:33, :], dr["b_gla_a"][l:l + 1].rearrange("o a d -> o (a d)"), writes=[bdb])
            mk.barrier()
            mk.op("pool", lambda e: e.tensor_copy(out=BD[:], in_=BDf[:]), reads=[bdb], writes=[bdb])
            mk.op("pool", lambda e: e.memset(aT[32:33, :], 1.0), writes=[aTb])

            w_in = dr["w_in"][l]
            self.filler_setup(st)

            def load_unit(c0, W):
                wt, wbuf = wb.next()
                for h0 in range(0, W, 256):
                    ww = min(256, W - h0)
                    ft, fb = wf.next()
                    mk.dma("sp", ft[:, :, 0:ww], w_in[:, c0 + h0:c0 + h0 + ww].rearrange("(kc p) n -> p kc n", p=128), writes=[fb])
                    mk.op("pool", lambda e, ft=ft, h0=h0, ww=ww: e.tensor_copy(out=wt[:, :, h0:h0 + ww], in_=ft[:, :, 0:ww]),
                          reads=[fb], writes=[wbuf])
                return wt, wbuf

            def mm_F(wt, wbuf, cc, tb, M=128):
                pt, pb = pacc.next()

                def mm(e):
                    for kc in range(NKC):
                        ins = e.matmul(pt[0:M, :], lhsT=wt[:, kc, cc * 128:cc * 128 + M], rhs=hT[:, kc, tb * 512:(tb + 1) * 512],
                                       start=(kc == 0), stop=(kc == NKC - 1))
                    return ins
                mk.op("pe", mm, reads=[wbuf] + hTb[tb * 4:(tb + 1) * 4], writes=[pb])
                return pt, pb

            def mm_T(wt, wbuf, t, c_lo, W):
                pt, pb = pacc.next()

                def mm(e):
                    for kc in range(NKC):
                        ins = e.matmul(pt[:, 0:W], lhsT=hT[:, kc, t * 128:(t + 1) * 128], rhs=wt[:, kc, c_lo:c_lo + W],
                                       start=(kc == 0), stop=(kc == NKC - 1))
                    return ins
                mk.op("pe", mm, reads=[wbuf, hTb[t]], writes=[pb])
                return pt, pb

            flip = [0]

            def evac(dst, src, reads, writes, func=None, scale=1.0, eng=None):
                if func is None and scale == 1.0 and eng is None:
                    flip[0] ^= 1
                    eng = "dve" if flip[0] else "act"
                if func is None and scale == 1.0 and eng == "dve":
                    mk.op("dve", lambda e: e.tensor_copy(out=dst, in_=src), reads=reads, writes=writes)
                elif func is None and eng == "dve":
                    mk.op("dve", lambda e: e.tensor_scalar(out=dst, in0=src, scalar1=scale, scalar2=None, op0=ALU.mult), reads=reads, writes=writes)
                else:
                    f = AF.Copy if func is None else func
                    mk.op("act", lambda e: e.activation(out=dst, in_=src, func=f, scale=scale), reads=reads, writes=writes)

            def do_F(wt, wbuf, ncc, dst, func=None, scale=1.0, eng=None, cc0=0):
                for cc in range(ncc):
                    for tb in range(NB):
                        pt, pb = mm_F(wt, wbuf, cc0 + cc, tb)
                        s, sb_ = stF.next()
                        evac(s[:], pt[:], [pb], [sb_], func, scale, eng)
                        mk.dma("sp", dst[cc * 128:(cc + 1) * 128, tb * 512:(tb + 1) * 512], s[:], reads=[sb_])

            def do_T_plain(wt, wbuf, c_lo, W, dst, func=None, f32_out=None):
                for t in range(NT):
                    pt, pb = mm_T(wt, wbuf, t, c_lo, W)
                    if t % 4 == 0:
                        s4, s4b = stT.next()
                    fo = f32_out(t) if f32_out is not None else None
                    if fo is not None:
                        tt, ttb = tmp.next()
                        evac(tt[:, 0:W], pt[:, 0:W], [pb], [ttb], eng="act")
                        mk.dma("sp", fo, tt[:, 0:W], reads=[ttb])
                        mk.op("pool", lambda e, tt=tt, s4=s4, t=t: e.tensor_copy(out=s4[:, t % 4, 0:W], in_=tt[:, 0:W]), reads=[ttb], writes=[s4b])
                    else:
                        evac(s4[:, t % 4, 0:W], pt[:, 0:W], [pb], [s4b], func)
                    if t % 4 == 3:
                        tb = t // 4
                        mk.dma("sp", dst[tb * 512:(tb + 1) * 512, :].rearrange("(t p) c -> p t c", p=128), s4[:, :, 0:W], reads=[s4b])

            def do_qk(wt, wbuf, j, dstT):
                gain = gqk[:, j, :].unsqueeze(1).broadcast_to([128, 8, 64])
                nts = TS // 128
                gv = gqk[:, j, :].rearrange("p (a h f) -> p a h f", a=2, h=2)
                swv = gsw[:].rearrange("p (a h f) -> p a h f", a=2, h=2)
                mk.op("dve", lambda e: e.tensor_copy(out=swv[:, :, 0, :], in_=gv[:, :, 1, :]), reads=[cb], writes=[tabb])
                mk.op("dve", lambda e: e.tensor_copy(out=swv[:, :, 1, :], in_=gv[:, :, 0, :]), reads=[cb], writes=[tabb])
                mk.dma("sp", rope[:], dr["k_rope"].rearrange("(t p) a d -> p t a d", p=128), writes=[tabb])
                cg = rope[:, :, 0, :]
                sg = rope[:, :, 1, :]
                mk.op("dve", lambda e: e.tensor_tensor(out=cg, in0=cg, in1=gqk[:, j, :].unsqueeze(1).broadcast_to([128, nts, 64]), op=ALU.mult),
                      reads=[cb], writes=[tabb])
                mk.op("dve", lambda e: e.tensor_tensor(out=sg, in0=sg, in1=gsw[:].unsqueeze(1).broadcast_to([128, nts, 64]), op=ALU.mult),
                      reads=[cb, tabb], writes=[tabb])
                v3 = lambda x: x[:].rearrange("p (g d) -> p g d", g=8)
                v4 = lambda x: x[:].rearrange("p (g a x) -> p g a x", g=8, a=2)
                s4box = [None]

                def qk_tile(t):
                    pt, pb = mm_T(wt, wbuf, t, 0, 512)
                    self.filler(6)
                    yield
                    raw, rawb = rawp.next()
                    mk.op("act", lambda e: e.activation(out=raw[:], in_=pt[:], func=AF.Copy), reads=[pb], writes=[rawb])
                    sq, sqb = sqp.next()
                    mk.op("act", lambda e: e.activation(out=sq[:], in_=pt[:], func=AF.Square), reads=[pb], writes=[sqb])
                    sm, smb = small.next()
                    mk.op("dve", lambda e: e.reduce_sum(out=sm[:, 0:8], in_=v3(sq), axis=AX.X), reads=[sqb], writes=[smb])
                    yield
                    mk.op("act", lambda e: e.activation(out=sm[:, 8:16], in_=sm[:, 0:8], func=AF.Ln, scale=1.0 / 64, bias=EPS), reads=[smb], writes=[smb])
                    mk.op("act", lambda e: e.activation(out=sm[:, 8:16], in_=sm[:, 8:16], func=AF.Exp, scale=-0.5), reads=[smb], writes=[smb])
                    rstd = sm[:, 8:16].unsqueeze(2).broadcast_to([128, 8, 64])
                    qr, qrb = tbf.next()
                    if t < NPT:
                        qn, qnb = up.next()
                    else:
                        ti = t - NPT
                        cosv = cg[:, ti, :].unsqueeze(1).broadcast_to([128, 8, 64])
                        sinv = sg[:, ti, :].rearrange("p (a x) -> p a x", a=2).unsqueeze(1).broadcast_to([128, 8, 2, 32])
                        u, ub = up.next()
                        mk.op("dve", lambda e: e.tensor_tensor(out=v3(u), in0=v3(raw), in1=cosv, op=ALU.mult), reads=[rawb, tabb], writes=[ub])
                        w_, wb_ = wp.next()
                        mk.op("dve", lambda e: e.tensor_tensor(out=v4(w_)[:, :, :, 0:16], in0=v4(raw)[:, :, :, 16:32], in1=sinv[:, :, :, 0:16], op=ALU.mult),
                              reads=[rawb, tabb], writes=[wb_])
                        mk.op("dve", lambda e: e.tensor_tensor(out=v4(w_)[:, :, :, 16:32], in0=v4(raw)[:, :, :, 0:16], in1=sinv[:, :, :, 16:32], op=ALU.mult),
                              reads=[rawb, tabb], writes=[wb_])
                        mk.op("pool", lambda e: e.tensor_tensor(out=u[:], in0=u[:], in1=w_[:], op=ALU.add), reads=[ub, wb_], writes=[ub])
                    yield
                    if t < NPT:
                        mk.op("dve", lambda e: e.tensor_tensor(out=v3(qn), in0=v3(raw), in1=rstd, op=ALU.mult), reads=[rawb, smb], writes=[qnb])
                        mk.op("pool", lambda e: e.tensor_tensor(out=v3(qn), in0=v3(qn), in1=gain, op=ALU.mult), reads=[qnb, cb], writes=[qnb])
                        if j == 1:
                            sq_i, tt_i = t // 2, t % 2
                            mk.dma("sp", dr["nk"][sq_i, l, tt_i * 128:(tt_i + 1) * 128, :], qn[:], reads=[qnb])
                        mk.op("act", lambda e: e.activation(out=qr[:], in_=qn[:], func=AF.Copy), reads=[qnb], writes=[qrb])
                    else:
                        mk.op("dve", lambda e: e.tensor_tensor(out=v3(qr), in0=v3(u), in1=rstd, op=ALU.mult), reads=[ub, smb], writes=[qrb])
                    pT, pTb = pTp.next()

                    def tr(e):
                        for h in range(4):
                            ins = e.transpose(out=pT[:, h, :], in_=qr[:, h * 128:(h + 1) * 128], identity=self.ident_b[:])
                        return ins
                    mk.op("pe", tr, reads=[qrb], writes=[pTb])
                    yield
                    if t % 4 == 0:
                        s4box[0] = stT.next()
                    s4, s4b = s4box[0]
                    evac(s4[:, :, (t % 4) * 128:(t % 4 + 1) * 128], pT[:], [pTb], [s4b], eng="act")
                    if t % 4 == 3:
                        tb = t // 4
                        mk.dma("sp", dstT.rearrange("(h p) t -> p h t", p=128)[:, :, tb * 512:(tb + 1) * 512], s4[:], reads=[s4b])
                run_pipeline((qk_tile(t) for t in range(NT)), 4)

            def do_ag(wt, wbuf):
                for tb in range(NB):
                    pt, pb = mm_F(wt, wbuf, 0, tb, M=32)
                    evac(aT[0:32, tb * 512:(tb + 1) * 512], pt[0:32, :], [pb], [aTb])
                for t in range(NT):
                    pt, pb = pacc.next()
                    mk.op("pe", lambda e, pt=pt, t=t: e.matmul(pt[:], lhsT=aT[0:33, t * 128:(t + 1) * 128], rhs=BD[0:33, :], start=True, stop=True),
                          reads=[aTb, bdb], writes=[pb])
                    e1, e1b = tmp.next()
                    mk.op("act", lambda e, e1=e1, pt=pt: e.activation(out=e1[:], in_=pt[:], func=AF.Exp, scale=-1.0), reads=[pb], writes=[e1b])
                    mk.op("act", lambda e, e1=e1: e.activation(out=e1[:], in_=e1[:], func=AF.Ln, bias=1.0), reads=[e1b], writes=[e1b])
                    mk.dma("sp", dr["LG"][t * 128:(t + 1) * 128, :], e1[:], reads=[e1b])

            def nv_out(t):
                if t >= NPT:
                    return None
                return dr["nv"][t // 2, l, (t % 2) * 128:(t % 2 + 1) * 128, :]

            units = [("uf", C_UF, 512), ("qk", C_QG, 512), ("vg", C_VG, 512), ("rg", C_RG, 512), ("ag", C_AG, 32),
                     ("qd", C_QD, 512), ("kd", C_KD, 512), ("vd", C_VD, 512)] + [(f"g{i}", C_GT + 512 * i, 512) for i in range(6)]
            nxt = load_unit(units[0][1], units[0][2])
            for ui, (name, c0, W) in enumerate(units):
                wt, wbuf = nxt
                if ui + 1 < len(units):
                    nxt = load_unit(units[ui + 1][1], units[ui + 1][2])
                if name == "uf":
                    do_F(wt, wbuf, 4, dr["UFT"], eng="dve")
                elif name == "qk":
                    do_F(wt, wbuf, 2, dr["QGT"], scale=0.125, eng="dve")
                    do_F(wt, wbuf, 2, dr["KGT"], eng="dve", cc0=2)
                    do_T_plain(wt, wbuf, 256, 256, dr["KG"])
                elif name == "vg":
                    do_T_plain(wt, wbuf, 0, 512, dr["VG"])
                elif name == "rg":
                    do_T_plain(wt, wbuf, 0, 512, dr["RG"], func=AF.Silu)
                elif name == "ag":
                    do_ag(wt, wbuf)
                elif name == "qd":
                    do_qk(wt, wbuf, 0, dr["QDT"])
                elif name == "kd":
                    do_qk(wt, wbuf, 1, dr["KDT"])
                elif name == "vd":
                    do_T_plain(wt, wbuf, 0, 512, dr["VD"], f32_out=nv_out)
                else:
                    gi = int(name[1:])
                    do_F(wt, wbuf, 4, dr["GT"][gi * 512:(gi + 1) * 512, :], func=AF.Sigmoid)


    def phase_C(self, l):
        mk, dr = self.mk, self.dram
        with ExitStack() as st:
            cscf = self.sb(st, "c_cscf", [128, 256], F32)
            csc = self.sb(st, "c_csc", [128, 256], BF16)
            cb = Buf()
            mk.dma("sp", cscf[:], dr["k_csc"], writes=[cb])
            mk.op("dve", lambda e: e.tensor_copy(out=csc[:], in_=cscf[:]), reads=[cb], writes=[cb])
            ntmax = max(T for _, T in self.seqs) // 128
            PQ = self.sb(st, "c_PQ", [128, ntmax, 1024], BF16)
            PQb = [Buf() for _ in range(ntmax)]
            uTp = self.sbpool(st, "c_uT", [128, 4, 512], BF16, 2)
            ppq = self.pspool(st, "c_ppq", [128, 1024], F32, 1)
            pf = [self.ps(st, f"c_pf{g}", [128, 512], F32) for g in range(4)]
            pfb = [Buf() for _ in range(4)]
            ctp = self.sbpool(st, "c_ct", [128, 4, 512], BF16, 4)
            nstp = self.sbpool(st, "c_nst", [128, 4, 512], BF16, 4)
            fst = self.sbpool(st, "c_fst", [128, 4, 512], BF16, 2)
            UFTv = dr["UFT"].rearrange("(g p) t -> p g t", p=128)
            FTv = dr["FT"].rearrange("(g p) t -> p g t", p=128)
            for si, (t0, T) in enumerate(self.seqs):
                nt = T // 128
                ctD, nstD = (dr["k_ctP"], dr["k_nstP"]) if T == SPL else (dr["k_ctS"], dr["k_nstS"])
                PW = min(512, T)
                for pc in range(T // PW):
                    uT, uTb = uTp.next()
                    mk.dma("sp", uT[:, :, 0:PW], UFTv[:, :, t0 + pc * PW:t0 + (pc + 1) * PW], writes=[uTb])
                    for tl in range(PW // 128):
                        tile = pc * (PW // 128) + tl
                        pq, pqb = ppq.next()

                        def mm(e, uT=uT, tl=tl, pq=pq):
                            for g in range(4):
                                ins = e.matmul(pq[:, g * 256:(g + 1) * 256], lhsT=uT[:, g, tl * 128:(tl + 1) * 128], rhs=csc[:], start=True, stop=True)
                            return ins
                        mk.op("pe", mm, reads=[uTb, cb], writes=[pqb])
                        mk.op("act", lambda e, pq=pq, tile=tile: e.activation(out=PQ[:, tile, 0:512], in_=pq[:, 0:512], func=AF.Copy), reads=[pqb], writes=[PQb[tile]])
                        mk.op("dve", lambda e, pq=pq, tile=tile: e.tensor_copy(out=PQ[:, tile, 512:1024], in_=pq[:, 512:1024]), reads=[pqb], writes=[PQb[tile]])
                NP = min(512, T)
                TG = min(4, nt)
                for pb in range(T // NP):
                    for tg in range(nt // TG):
                        ct, ctb = ctp.next()
                        nst, nstb = nstp.next()
                        r0 = tg * TG * 128
                        mk.dma("sp", ct[:, 0:TG, 0:NP], ctD[r0:r0 + TG * 128, pb * NP:(pb + 1) * NP].rearrange("(tc p) n -> p tc n", p=128), writes=[ctb])
                        mk.dma("sp", nst[:, 0:TG, 0:NP], nstD[r0:r0 + TG * 128, pb * NP:(pb + 1) * NP].rearrange("(tc p) n -> p tc n", p=128), writes=[nstb])
                        for g in range(4):
                            def mm(e, g=g, tg=tg, ct=ct, nst=nst):
                                for tc in range(TG):
                                    tile = tg * TG + tc
                                    first = (tg == 0 and tc == 0)
                                    last = (tg == nt // TG - 1 and tc == TG - 1)
                                    e.matmul(pf[g][:, 0:NP], lhsT=PQ[:, tile, g * 256:g * 256 + 128], rhs=ct[:, tc, 0:NP], start=first, stop=False)
                                    ins = e.matmul(pf[g][:, 0:NP], lhsT=PQ[:, tile, g * 256 + 128:g * 256 + 256], rhs=nst[:, tc, 0:NP], start=False, stop=last)
                                return ins
                            mk.op("pe", mm, reads=[ctb, nstb] + PQb[tg * TG:(tg + 1) * TG], writes=[pfb[g]])
                    fs, fsb = fst.next()
                    for g in range(4):
                        if g % 2 == 0:
                            mk.op("act", lambda e, g=g, fs=fs: e.activation(out=fs[:, g, 0:NP], in_=pf[g][:, 0:NP], func=AF.Copy), reads=[pfb[g]], writes=[fsb])
                        else:
                            mk.op("dve", lambda e, g=g, fs=fs: e.tensor_copy(out=fs[:, g, 0:NP], in_=pf[g][:, 0:NP]), reads=[pfb[g]], writes=[fsb])
                    mk.dma("sp", FTv[:, :, t0 + pb * NP:t0 + (pb + 1) * NP], fs[:, :, 0:NP], reads=[fsb])

    def phase_D(self, l):
        mk, dr = self.mk, self.dram
        with ExitStack() as st:
            tri = self.sb(st, "d_tri", [64, 6, 64], F32)
            cb = Buf()
            mk.dma("sp", tri[:], dr["k_tri"], writes=[cb])
            S32 = [self.sb(st, f"d_S32_{d}", [64, 4, 128], F32) for d in range(2)]
            Sb = [self.sb(st, f"d_Sb_{d}", [64, 4, 128], BF16) for d in range(2)]
            S32b = [Buf(), Buf()]
            Sbb = [Buf(), Buf()]
            Tmax = max(T for _, T in self.seqs)
            qTa = self.sb(st, "d_qT", [64, 4, Tmax], BF16)
            kTa = self.sb(st, "d_kT", [64, 4, Tmax], BF16)
            qkb = Buf()
            Lp = self.sbpool(st, "d_L", [64, 256], F32, 3)
            kp = self.sbpool(st, "d_k", [64, 256], BF16, 4)
            vp = self.sbpool(st, "d_v", [64, 512], BF16, 5)
            eqp = self.sbpool(st, "d_eq", [64, 4, 64], F32, 5)
            ekp = self.sbpool(st, "d_ek", [64, 4, 64], F32, 3)
            qsp = self.sbpool(st, "d_qs", [64, 4, 64], BF16, 4)
            ksp = self.sbpool(st, "d_ks", [64, 4, 64], BF16, 3)
            edp = self.sbpool(st, "d_ed", [64, 256], F32, 3)
            kdp = self.sbpool(st, "d_kd", [64, 256], BF16, 4)
            atp = self.sbpool(st, "d_at", [64, 4, 64], BF16, 3)
            osp = self.sbpool(st, "d_os", [64, 512], F32, 3)
            pA = self.pspool(st, "d_pA", [128, 512], F32, 2)
            pB = self.pspool(st, "d_pB", [128, 512], F32, 2)
            pS = self.pspool(st, "d_pS", [128, 512], F32, 2)
            pO = self.pspool(st, "d_pO", [128, 512], F32, 2)
            QGTv = dr["QGT"].rearrange("(h k) t -> k h t", k=64)
            KGTv = dr["KGT"].rearrange("(h k) t -> k h t", k=64)

            def chunk(t0, c, d):
                tok0 = t0 + c * 64
                Lc, Lb = Lp.next()
                kc_, kb = kp.next()
                vc, vb = vp.next()
                mk.dma("sp", Lc[:], dr["LG"][tok0:tok0 + 64, d * 256:(d + 1) * 256], writes=[Lb])
                mk.dma("sp", kc_[:], dr["KG"][tok0:tok0 + 64, :], writes=[kb])
                mk.dma("sp", vc[:], dr["VG"][tok0:tok0 + 64, :], writes=[vb])
                a, ab = pA.next()
                pc = a[0:64, 0:256].rearrange("p (h i) -> p h i", h=4)
                psuf = a[0:64, 256:512]

                def mm0(e):
                    for h in range(4):
                        e.matmul(pc[:, h, :], lhsT=Lc[:, h * 64:(h + 1) * 64], rhs=tri[:, d, :], start=True, stop=True)
                    return e.matmul(psuf, lhsT=tri[:, 2 + d, :], rhs=Lc[:], start=True, stop=True)
                mk.op("pe", mm0, reads=[Lb, cb], writes=[ab])
                yield
                eq, eqb = eqp.next()
                ek, ekb = ekp.next()
                qs, qsb = qsp.next()
                ks, ksb = ksp.next()
                ed, edb = edp.next()
                kd, kdb = kdp.next()
                mk.op("act", lambda e: e.activation(out=eq[:], in_=pc, func=AF.Exp, scale=-1.0), reads=[ab], writes=[eqb])
                mk.op("act", lambda e: e.activation(out=ek[:], in_=pc, func=AF.Exp), reads=[ab], writes=[ekb])
                mk.op("act", lambda e: e.activation(out=ed[:], in_=psuf, func=AF.Exp, scale=-1.0), reads=[ab], writes=[edb])
                mk.op("dve", lambda e: e.tensor_tensor(out=qs[:], in0=qTa[:, :, c * 64:(c + 1) * 64], in1=eq[:], op=ALU.mult), reads=[eqb, qkb], writes=[qsb])
                mk.op("pool", lambda e: e.tensor_tensor(out=ks[:], in0=kTa[:, :, c * 64:(c + 1) * 64], in1=ek[:], op=ALU.mult), reads=[ekb, qkb], writes=[ksb])
                mk.op("dve", lambda e: e.tensor_tensor(out=kd[:], in0=kc_[:], in1=ed[:], op=ALU.mult), reads=[edb, kb], writes=[kdb])
                b_, bb = pB.next()
                pa = b_[0:64, 0:256].rearrange("p (h i) -> p h i", h=4)

                def mm1(e):
                    for h in range(4):
                        ins = e.matmul(pa[:, h, :], lhsT=ks[:, h, :], rhs=qs[:, h, :], start=True, stop=True)
                    return ins
                mk.op("pe", mm1, reads=[ksb, qsb], writes=[bb])
                yield
                at, atb = atp.next()
                mk.op("dve", lambda e: e.tensor_tensor(out=at[:], in0=pa, in1=tri[:, 4 + d, :].unsqueeze(1).broadcast_to([64, 4, 64]), op=ALU.mult),
                      reads=[bb, cb], writes=[atb])
                o_, ob = pO.next()
                s_, sb_ = pS.next()
                ps_ = s_[0:64, :].rearrange("p (h v) -> p h v", h=4)

                def mm2(e):
                    for h in range(4):
                        e.matmul(o_[0:64, h * 128:(h + 1) * 128], lhsT=at[:, h, :], rhs=vc[:, h * 128:(h + 1) * 128], start=True, stop=False)
                        e.matmul(o_[0:64, h * 128:(h + 1) * 128], lhsT=qs[:, h, :], rhs=Sb[d][:, h, :], start=False, stop=True)
                    for h in range(4):
                        ins = e.matmul(ps_[:, h, :], lhsT=kd[:, h * 64:(h + 1) * 64], rhs=vc[:, h * 128:(h + 1) * 128], start=True, stop=True)
                    return ins
                mk.op("pe", mm2, reads=[atb, vb, qsb, Sbb[d], kdb], writes=[ob, sb_])
                yield
                os_, osb = osp.next()
                mk.op("act", lambda e: e.activation(out=os_[:], in_=o_[0:64, :], func=AF.Copy), reads=[ob], writes=[osb])
                mk.dma("sp", dr["OF" if d == 0 else "OB"][tok0:tok0 + 64, :], os_[:], reads=[osb])
                col = 63 if d == 0 else 0
                for h in range(4):
                    mk.op("dve", lambda e, h=h: e.scalar_tensor_tensor(out=S32[d][:, h, :], in0=S32[d][:, h, :], scalar=eq[:, h, col:col + 1], in1=ps_[:, h, :],
                                                                     op0=ALU.mult, op1=ALU.add), reads=[eqb, sb_, S32b[d]], writes=[S32b[d]])
                mk.op("act", lambda e: e.activation(out=Sb[d][:], in_=S32[d][:], func=AF.Copy), reads=[S32b[d]], writes=[Sbb[d]])

            for si, (t0, T) in enumerate(self.seqs):
                nch = T // 64
                mk.dma("sp", qTa[:, :, 0:T], QGTv[:, :, t0:t0 + T], writes=[qkb])
                mk.dma("sp", kTa[:, :, 0:T], KGTv[:, :, t0:t0 + T], writes=[qkb])
                for d in range(2):
                    if si < 2:
                        mk.op("pool", lambda e, d=d: e.memset(S32[d][:], 0.0), writes=[S32b[d]])
                    else:
                        mk.dma("sp", S32[d][:], dr["state"][l, d].rearrange("h k v -> k h v"), writes=[S32b[d]])
                    mk.op("act", lambda e, d=d: e.activation(out=Sb[d][:], in_=S32[d][:], func=AF.Copy), reads=[S32b[d]], writes=[Sbb[d]])
                gens = []
                for c in range(nch):
                    gens.append(chunk(t0, c, 0))
                    gens.append(chunk(t0, nch - 1 - c, 1))
                run_pipeline(gens, 4)
                if si < 2:
                    for d in range(2):
                        mk.dma("sp", dr["ns"][si, l, d].rearrange("h k v -> k h v"), S32[d][:], reads=[S32b[d]])
        mk.barrier()
        with ExitStack() as st:
            cb = Buf()
            gn = self.sb(st, "d_gn", [128, 128], F32)
            mk.dma("sp", gn[:], dr["gla_norm"][l].partition_broadcast(128), writes=[cb])
            obp = self.sbpool(st, "d_ob", [128, 512], F32, 3)
            rgp = self.sbpool(st, "d_rg", [128, 512], BF16, 3)
            self.post_norm_T(st, "d", dr["OGT"], gn, cb, 1.0,
                             loader=lambda t, o, ob_: (mk.dma("sp", o[:], dr["OF"][t * 128:(t + 1) * 128, :], writes=[ob_])),
                             extra=(obp, rgp, dr))

    def post_norm_T(self, st, pfx, dstT, gain, gb, gscale, loader, extra):
        mk = self.mk
        obp, rgp, dr = extra
        ofp = self.sbpool(st, pfx + "_pof", [128, 512], F32, 4)
        g2p = self.sbpool(st, pfx + "_pg2", [128, 512], F32, 4)
        jkp = self.sbpool(st, pfx + "_pjk", [128, 128], BF16, 2)
        smp = self.sbpool(st, pfx + "_psm", [128, 8], F32, 4)
        onp = self.sbpool(st, pfx + "_pon", [128, 512], BF16, 3)
        pTp = self.pspool(st, pfx + "_ppT", [128, 4, 128], BF16, 2)
        stT = self.sbpool(st, pfx + "_pst", [128, 4, 512], BF16, 2)
        v3 = lambda x: x[:].rearrange("p (h v) -> p h v", h=4)
        s4box = [None]

        def tile_gen(t):
            of, ofb = ofp.next()
            ob, obb = obp.next()
            rg, rgb = rgp.next()
            loader(t, of, ofb)
            mk.dma("sp", ob[:], dr["OB"][t * 128:(t + 1) * 128, :], writes=[obb])
            mk.dma("sp", rg[:], dr["RG"][t * 128:(t + 1) * 128, :], writes=[rgb])
            mk.op("dve", lambda e: e.tensor_tensor(out=of[:], in0=of[:], in1=ob[:], op=ALU.add), reads=[ofb, obb], writes=[ofb])
            g2, g2b = g2p.next()
            mk.op("pool", lambda e: e.tensor_tensor(out=v3(g2), in0=v3(rg), in1=gain[:].unsqueeze(1).broadcast_to([128, 4, 128]), op=ALU.mult),
                  reads=[rgb, gb], writes=[g2b])
            yield
            sm, smb = smp.next()
            for h in range(4):
                jk, jkb = jkp.next()
                mk.op("act", lambda e, jk=jk, h=h: e.activation(out=jk[:], in_=of[:, h * 128:(h + 1) * 128], func=AF.Square, accum_out=sm[:, h:h + 1]),
                      reads=[ofb], writes=[jkb, smb])
            mk.op("act", lambda e: e.activation(out=sm[:, 4:8], in_=sm[:, 0:4], func=AF.Ln, scale=1.0 / 128, bias=EPS), reads=[smb], writes=[smb])
            mk.op("act", lambda e: e.activation(out=sm[:, 4:8], in_=sm[:, 4:8], func=AF.Exp, scale=-0.5), reads=[smb], writes=[smb])
            yield
            on, onb = onp.next()
            for h in range(4):
                mk.op("dve", lambda e, h=h: e.scalar_tensor_tensor(out=on[:, h * 128:(h + 1) * 128], in0=of[:, h * 128:(h + 1) * 128],
                                                                 scalar=sm[:, 4 + h:5 + h], in1=g2[:, h * 128:(h + 1) * 128], op0=ALU.mult, op1=ALU.mult),
                      reads=[ofb, g2b, smb], writes=[onb])
            yield
            pT, pTb = pTp.next()

            def tr(e):
                for h in range(4):
                    ins = e.transpose(out=pT[:, h, :], in_=on[:, h * 128:(h + 1) * 128], identity=self.ident_b[:])
                return ins
            mk.op("pe", tr, reads=[onb], writes=[pTb])
            yield
            if t % 4 == 0:
                s4box[0] = stT.next()
            s4, s4b = s4box[0]
            mk.op("act", lambda e: e.activation(out=s4[:, :, (t % 4) * 128:(t % 4 + 1) * 128], in_=pT[:], func=AF.Copy), reads=[pTb], writes=[s4b])
            if t % 4 == 3:
                tb = t // 4
                mk.dma("sp", dstT.rearrange("(h p) t -> p h t", p=128)[:, :, tb * 512:(tb + 1) * 512], s4[:], reads=[s4b])
        run_pipeline((tile_gen(t) for t in range(self.NT)), 5)

    def phase_E(self, l):
        mk, dr = self.mk, self.dram
        lam_init = 0.8 - 0.6 * math.exp(-0.3 * l)
        TKmax = self.TS + PAST
        nkcmax = TKmax // 128
        with ExitStack() as st:
            KT = self.sb(st, "e_KT", [128, 4, TKmax], BF16)
            V = self.sb(st, "e_V", [128, nkcmax, 4, 132], BF16)
            npiece = (TKmax + 511) // 512
            KTb = [Buf() for _ in range(npiece)]
            Vb = [Buf() for _ in range(nkcmax)]
            zeros = self.sb(st, "e_zero", [1, 512], BF16)
            gdn = self.sb(st, "e_gdn", [128, 128], F32)
            cb = Buf()
            mk.op("pool", lambda e: e.memset(zeros[:], 0.0), writes=[cb])
            mk.op("pool", lambda e: e.memset(V[:, :, :, 128:132], 1.0), writes=[cb])
            mk.dma("sp", gdn[:], dr["diff_norm"][l].partition_broadcast(128), writes=[cb])
            mk.op("dve", lambda e: e.tensor_scalar(out=gdn[:], in0=gdn[:], scalar1=1.0 - lam_init, scalar2=None, op0=ALU.mult), reads=[cb], writes=[cb])
            mk.barrier()
            QTz = [self.sbpool(st, f"e_QT{m}", [128, 4, 512], BF16, 2) for m in range(2)]
            for m in range(2):
                for tz in QTz[m].tiles:
                    lo = 64 * (1 - m)
                    mk.op("pool", lambda e, tz=tz, lo=lo: e.memset(tz[lo:lo + 64, :, :], 0.0), writes=[cb])
            pS2 = self.pspool(st, "e_pS", [128, 1024], F32, 2)
            acc = [self.ps(st, f"e_acc{i}", [128, 512], F32) for i in range(3)]
            accb = [Buf() for _ in range(3)]
            pTo = self.pspool(st, "e_pTo", [128, 4, 128], BF16, 1)
            ptp = self.sbpool(st, "e_pt", [128, 1024], BF16, 4)
            ckf = self.sbpool(st, "e_ckf", [128, 512], F32, 2)
            ckb = self.sbpool(st, "e_ckb", [128, 512], BF16, 2)
            odp = self.sbpool(st, "e_od", [128, 4, 128], BF16, 8)
            o1p = self.sbpool(st, "e_o1", [128, 128], F32, 5)
            o2p = self.sbpool(st, "e_o2", [128, 128], F32, 5)
            jkp = self.sbpool(st, "e_jk", [128, 128], F32, 4)
            smp = self.sbpool(st, "e_sm", [128, 8], F32, 6)
            st4p = self.sbpool(st, "e_st4", [128, 4, 512], BF16, 1)
            accsp = self.sbpool(st, "e_accs", [128, 3, 512], F32, 2)
            KDTv = dr["KDT"].rearrange("(h p) t -> p h t", p=128)
            QDTv = dr["QDT"].rearrange("(h p) t -> p h t", p=128)
            ODTv = dr["ODT"].rearrange("(h p) t -> p h t", p=128)
            for si, (t0, T) in enumerate(self.seqs):
                TK = T + (PAST if si == 2 else 0)
                nkc = TK // 128
                for pc in range((T + 511) // 512):
                    w = min(512, T - pc * 512)
                    mk.dma("sp", KT[:, :, pc * 512:pc * 512 + w], KDTv[:, :, t0 + pc * 512:t0 + pc * 512 + w], writes=[KTb[pc]])
                    for tl in range(w // 128):
                        r0 = t0 + pc * 512 + tl * 128
                        mk.dma("sp", V[:, pc * 4 + tl, :, 0:128], dr["VD"][r0:r0 + 128, :].rearrange("p (h v) -> p h v", h=4), writes=[Vb[pc * 4 + tl]])
                if si == 2:
                    pcx = T // 512
                    for tl in range(PAST // 128):
                        kf, kfb = ckf.next()
                        kb_, kbb = ckb.next()
                        mk.dma("sp", kf[:], dr["cache_k"][l, tl * 128:(tl + 1) * 128, :], writes=[kfb])
                        mk.op("pool", lambda e, kf=kf, kb_=kb_: e.tensor_copy(out=kb_[:], in_=kf[:]), reads=[kfb], writes=[kbb])
                        pT, pTb = pTo.next()

                        def tr(e, kb_=kb_, pT=pT):
                            for h in range(4):
                                ins = e.transpose(out=pT[:, h, :], in_=kb_[:, h * 128:(h + 1) * 128], identity=self.ident_b[:])
                            return ins
                        mk.op("pe", tr, reads=[kbb], writes=[pTb])
                        mk.op("dve", lambda e, pT=pT, tl=tl: e.tensor_copy(out=KT[:, :, T + tl * 128:T + (tl + 1) * 128], in_=pT[:]), reads=[pTb], writes=[KTb[pcx]])
                        vf, vfb = ckf.next()
                        mk.dma("sp", vf[:], dr["cache_v"][l, tl * 128:(tl + 1) * 128, :], writes=[vfb])
                        mk.op("pool", lambda e, vf=vf, tl=tl: e.tensor_copy(out=V[:, T // 128 + tl, :, 0:128], in_=vf[:].rearrange("p (h v) -> p h v", h=4)),
                              reads=[vfb], writes=[Vb[T // 128 + tl]])
                QW = min(512, T)
                nqs = QW // 128
                nqb = T // QW
                LOOK = 2

                def load_Q(qb):
                    QTm = []
                    for m in range(2):
                        qt_, qtb_ = QTz[m].next()
                        mk.dma("sp", qt_[m * 64:(m + 1) * 64, :, 0:QW], QDTv[m * 64:(m + 1) * 64, :, t0 + qb * QW:t0 + (qb + 1) * QW], writes=[qtb_])
                        QTm.append((qt_, qtb_))
                    return QTm

                def emit_S(QTm, h, kc):
                    ps_, psb = pS2.next()

                    def mmS(e):
                        for m in range(2):
                            ins = e.matmul(ps_[:, m * 512:m * 512 + QW], lhsT=KT[:, h, kc * 128:(kc + 1) * 128], rhs=QTm[m][0][:, h, 0:QW], start=True, stop=True)
                        return ins
                    mk.op("pe", mmS, reads=[KTb[kc // 4], QTm[0][1], QTm[1][1], cb], writes=[psb])
                    pt, ptb = ptp.next()
                    if QW == 512:
                        mk.op("act", lambda e: e.activation(out=pt[:], in_=ps_[:], func=AF.Exp, scale=0.125), reads=[psb], writes=[ptb])
                    else:
                        mk.op("act", lambda e: e.activation(out=pt[:].rearrange("p (m q) -> p m q", m=2)[:, :, 0:QW], in_=ps_[:].rearrange("p (m q) -> p m q", m=2)[:, :, 0:QW],
                                                            func=AF.Exp, scale=0.125), reads=[psb], writes=[ptb])
                    return pt, ptb

                def emit_PV(h, kc, pt, ptb):
                    def pv(e):
                        for m in range(2):
                            for qs in range(nqs):
                                a = m * 4 + qs
                                c0 = (a % 3) * 129
                                ins = e.matmul(acc[a // 3][:, c0:c0 + 129], lhsT=pt[:, m * 512 + qs * 128:m * 512 + (qs + 1) * 128], rhs=V[:, kc, h, 0:129],
                                               start=False, stop=(kc == nkc - 1), skip_group_check=True)
                        return ins
                    mk.op("pe", pv, reads=[ptb, Vb[kc]], writes=accb)

                def finalize(h, ods):
                    acs, acsb = accsp.next()
                    for b_ in range(3):
                        mk.op("dve", lambda e, b_=b_: e.tensor_copy(out=acs[:, b_, :], in_=acc[b_][:]), reads=[accb[b_]], writes=[acsb])

                    def fin(qs):
                        a1i, a2i = qs, 4 + qs
                        A1 = acs[:, a1i // 3, (a1i % 3) * 129:(a1i % 3) * 129 + 129]
                        A2 = acs[:, a2i // 3, (a2i % 3) * 129:(a2i % 3) * 129 + 129]
                        sm, smb = smp.next()
                        mk.op("dve", lambda e: e.reciprocal(out=sm[:, 0:1], in_=A1[:, 128:129]), reads=[acsb], writes=[smb])
                        mk.op("dve", lambda e: e.reciprocal(out=sm[:, 1:2], in_=A2[:, 128:129]), reads=[acsb], writes=[smb])
                        mk.op("dve", lambda e: e.tensor_tensor(out=sm[:, 2:3], in0=sm[:, 1:2], in1=self.lam[:, l, 1:2], op=ALU.mult), reads=[smb, self.cbuf], writes=[smb])
                        o1, o1b = o1p.next()
                        mk.op("dve", lambda e: e.tensor_scalar(out=o1[:], in0=A1[:, 0:128], scalar1=sm[:, 0:1], scalar2=None, op0=ALU.mult), reads=[acsb, smb], writes=[o1b])
                        o2, o2b = o2p.next()
                        mk.op("dve", lambda e: e.scalar_tensor_tensor(out=o2[:], in0=A2[:, 0:128], scalar=sm[:, 2:3], in1=o1[:], op0=ALU.mult, op1=ALU.add),
                              reads=[acsb, smb, o1b], writes=[o2b])
                        jk, jkb = jkp.next()
                        mk.op("dve", lambda e: e.tensor_tensor(out=jk[:], in0=o2[:], in1=o2[:], op=ALU.mult), reads=[o2b], writes=[jkb])
                        mk.op("dve", lambda e: e.reduce_sum(out=sm[:, 3:4], in_=jk[:], axis=AX.X), reads=[jkb], writes=[smb])
                        yield
                        mk.op("act", lambda e: e.activation(out=sm[:, 4:5], in_=sm[:, 3:4], func=AF.Ln, scale=1.0 / 128, bias=EPS), reads=[smb], writes=[smb])
                        mk.op("act", lambda e: e.activation(out=sm[:, 4:5], in_=sm[:, 4:5], func=AF.Exp, scale=-0.5), reads=[smb], writes=[smb])
                        yield
                        od, odb = ods[qs]
                        mk.op("dve", lambda e: e.scalar_tensor_tensor(out=od[:, h, :], in0=o2[:], scalar=sm[:, 4:5], in1=gdn[:], op0=ALU.mult, op1=ALU.mult),
                              reads=[o2b, smb, cb], writes=[odb])
                    run_pipeline((fin(qs) for qs in range(nqs)), 4)

                def qblock_end(qb, ods):
                    s4, s4b = st4p.next()
                    for qs in range(nqs):
                        od, odb = ods[qs]
                        pT, pTb = pTo.next()

                        def tr(e, od=od, pT=pT):
                            for h in range(4):
                                ins = e.transpose(out=pT[:, h, :], in_=od[:, h, :], identity=self.ident_b[:])
                            return ins
                        mk.op("pe", tr, reads=[odb], writes=[pTb])
                        mk.op("dve", lambda e, s4=s4, pT=pT, qs=qs: e.tensor_copy(out=s4[:, :, qs * 128:(qs + 1) * 128], in_=pT[:]), reads=[pTb], writes=[s4b])
                    mk.dma("sp", ODTv[:, :, t0 + qb * QW:t0 + (qb + 1) * QW], s4[:, :, 0:QW], reads=[s4b])

                allsteps = [(qb, h, kc) for qb in range(nqb) for h in range(4) for kc in range(nkc)]
                Q = {}
                odss = {}
                pend = []
                for i in range(len(allsteps) + LOOK):
                    if i < len(allsteps):
                        qb, h, kc = allsteps[i]
                        if h == 0 and kc == 0:
                            if qb == 0:
                                Q[0] = load_Q(0)
                            if qb + 1 < nqb:
                                Q[qb + 1] = load_Q(qb + 1)
                        pend.append(emit_S(Q[qb], h, kc))
                    if i >= LOOK:
                        qb, h, kc = allsteps[i - LOOK]
                        if kc == 0:
                            if h == 0:
                                odss[qb] = [odp.next() for _ in range(nqs)]
                            for b_ in range(3):
                                mk.op("pe", lambda e, b_=b_: e.matmul(acc[b_][:], lhsT=zeros[0:1, 0:128], rhs=zeros[0:1, 0:512], start=True, stop=False, skip_group_check=True),
                                      reads=[cb], writes=[accb[b_]])
                        emit_PV(h, kc, *pend[i - LOOK])
                        pend[i - LOOK] = None
                        if kc == nkc - 1:
                            finalize(h, odss[qb])
                            if h == 3:
                                qblock_end(qb, odss[qb])

    def phase_F1(self, l):
        mk, dr = self.mk, self.dram
        with ExitStack() as st:
            wbr = self.sb(st, "f_wbr", [128, 3, 4, D], BF16)
            wbbs = [Buf(), Buf(), Buf()]
            wst = self.sbpool(st, "f_wst", [128, D], F32, 3)
            for br, nm in enumerate(("w_fou", "w_gla_o", "w_diff_o")):
                for g in range(4):
                    ws, wsb = wst.next()
                    mk.dma("sp", ws[:], dr[nm][l, g * 128:(g + 1) * 128, :], writes=[wsb])
                    ce = ("pool", "dve", "act")[(br * 4 + g) % 3]
                    if ce == "act":
                        mk.op("act", lambda e, ws=ws, br=br, g=g: e.activation(out=wbr[:, br, g, :], in_=ws[:], func=AF.Copy), reads=[wsb], writes=[wbbs[(br * 4 + g) % 3]])
                    else:
                        mk.op(ce, lambda e, ws=ws, br=br, g=g: e.tensor_copy(out=wbr[:, br, g, :], in_=ws[:]), reads=[wsb], writes=[wbbs[(br * 4 + g) % 3]])
            f3p = self.sbpool(st, "f_f3", [128, 3, 4, 512], BF16, 2)
            gtp = self.sbpool(st, "f_gt", [128, 512], BF16, 12)
            pX = self.pspool(st, "f_pX", [128, 512], F32, 6)
            tp = self.sbpool(st, "f_t", [128, 512], BF16, 9)
            mp = self.sbpool(st, "f_m", [128, 512], BF16, 4)
            srcs = [dr[n].rearrange("(g p) t -> p g t", p=128) for n in ("FT", "OGT", "ODT")]
            def load_f3(tb):
                f3, f3b = f3p.next()
                for br in range(3):
                    mk.dma("sp", f3[:, br, :, :], srcs[br][:, :, tb * 512:(tb + 1) * 512], writes=[f3b])
                return f3, f3b
            nxt_f3 = load_f3(0)
            for tb in range(self.NB):
                f3, f3b = nxt_f3
                if tb + 1 < self.NB:
                    nxt_f3 = load_f3(tb + 1)
                gts = {}

                def load_gt(oc):
                    for br in range(3):
                        gt, gtb = gtp.next()
                        mk.dma("sp", gt[:], dr["GT"][(br * 8 + oc) * 128:(br * 8 + oc + 1) * 128, tb * 512:(tb + 1) * 512], writes=[gtb])
                        gts[(oc, br)] = (gt, gtb)
                load_gt(0)
                load_gt(1)
                def oc_gen(oc, tb=tb, f3=f3, f3b=f3b, gts=gts, load_gt=load_gt):
                    if oc + 2 < 8:
                        load_gt(oc + 2)
                    pxs = []
                    for br in range(3):
                        px, pxb = pX.next()

                        def mm(e, px=px, br=br):
                            for g in range(4):
                                ins = e.matmul(px[:], lhsT=wbr[:, br, g, oc * 128:(oc + 1) * 128], rhs=f3[:, br, g, :], start=(g == 0), stop=(g == 3))
                            return ins
                        mk.op("pe", mm, reads=wbbs + [f3b], writes=[pxb])
                        pxs.append((px, pxb))
                    yield
                    ts_ = []
                    for br in range(3):
                        gt, gtb = gts[(oc, br)]
                        px, pxb = pxs[br]
                        t_, tb_ = tp.next()
                        mk.op("dve", lambda e, t_=t_, px=px, gt=gt: e.tensor_tensor(out=t_[:], in0=px[:], in1=gt[:], op=ALU.mult), reads=[pxb, gtb], writes=[tb_])
                        ts_.append((t_, tb_))
                    m_, mb = mp.next()
                    mk.op("dve", lambda e: e.tensor_tensor(out=m_[:], in0=ts_[0][0][:], in1=ts_[1][0][:], op=ALU.add), reads=[ts_[0][1], ts_[1][1]], writes=[mb])
                    mk.op("dve", lambda e: e.tensor_tensor(out=m_[:], in0=m_[:], in1=ts_[2][0][:], op=ALU.add), reads=[ts_[2][1], mb], writes=[mb])
                    mk.dma("sp", dr["MT"][oc * 128:(oc + 1) * 128, tb * 512:(tb + 1) * 512], m_[:], reads=[mb])
                run_pipeline((oc_gen(oc) for oc in range(8)), 2)

    def resid_phase(self, l, pfx, act_name, nk, w_name, g_col, norm_l, norm_which, final):
        mk, dr = self.mk, self.dram
        with ExitStack() as st:
            self.norm_setup(st)
            W = self.sb(st, pfx + "_W", [128, nk, D], BF16)
            Wbs = [Buf(), Buf(), Buf()]
            xp = self.sbpool(st, pfx + "_x", [128, D], F32, 4)
            for j in range(nk):
                ws, wsb = xp.next()
                mk.dma("sp", ws[:], dr[w_name][l, j * 128:(j + 1) * 128, :], writes=[wsb])
                ce = ("pool", "dve", "act")[j % 3]
                if ce == "act":
                    mk.op("act", lambda e, ws=ws, j=j: e.activation(out=W[:, j, :], in_=ws[:], func=AF.Copy), reads=[wsb], writes=[Wbs[j % 3]])
                else:
                    mk.op(ce, lambda e, ws=ws, j=j: e.tensor_copy(out=W[:, j, :], in_=ws[:]), reads=[wsb], writes=[Wbs[j % 3]])
            gb = self.sb(st, pfx + "_g", [128, 2, D], F32)
            gbb = Buf()
            for r in range(2):
                mk.dma("sp", gb[:, r, :], dr["MOD"][l, r, g_col * D:(g_col + 1) * D].partition_broadcast(128), writes=[gbb])
            ap_ = self.sbpool(st, pfx + "_a", [128, nk, 512], BF16, 2)
            po = self.pspool(st, pfx + "_po", [128, 512], F32, 4)
            tp = self.sbpool(st, pfx + "_t", [128, 512], F32, 2)
            src = dr[act_name].rearrange("(j p) t -> p j t", p=128)
            self.filler_setup(st)
            nfill = 8 if nk <= 8 else 4
            blocks = {}

            def load_a(tb):
                a_, ab = ap_.next()
                mk.dma("sp", a_[:], src[:, :, tb * 512:(tb + 1) * 512], writes=[ab])
                blocks[tb] = (a_, ab)
            load_a(0)

            def tile_gen(t):
                tb, tl = t // 4, t % 4
                if tl == 1 and tb + 1 < self.NB:
                    load_a(tb + 1)
                a_, ab = blocks[tb]
                cond = self.cond_of_tile(t)
                xt, xb = xp.next()
                mk.dma("sp", xt[:], dr["X"][t * 128:(t + 1) * 128, :], writes=[xb])
                ps = []
                for half in range(2):
                    p_, pb = po.next()

                    def mm(e, p_=p_, half=half):
                        for j in range(nk):
                            ins = e.matmul(p_[:], lhsT=a_[:, j, tl * 128:(tl + 1) * 128], rhs=W[:, j, half * 512:(half + 1) * 512], start=(j == 0), stop=(j == nk - 1))
                        return ins
                    mk.op("pe", mm, reads=[ab] + Wbs, writes=[pb])
                    ps.append((p_, pb))
                self.filler(nfill)
                yield
                for half in range(2):
                    p_, pb = ps[half]
                    t_, tb_ = tp.next()
                    mk.op("dve", lambda e, t_=t_, p_=p_, half=half: e.tensor_tensor(out=t_[:], in0=p_[:], in1=gb[:, cond, half * 512:(half + 1) * 512], op=ALU.mult),
                          reads=[pb, gbb], writes=[tb_])
                    mk.op("dve", lambda e, t_=t_, half=half: e.tensor_tensor(out=xt[:, half * 512:(half + 1) * 512], in0=xt[:, half * 512:(half + 1) * 512], in1=t_[:], op=ALU.add),
                          reads=[tb_, xb], writes=[xb])
                if final:
                    if t < 4:
                        mk.dma("sp", dr["yp"][t * 128:(t + 1) * 128, :], xt[:], reads=[xb])
                    else:
                        mk.dma("sp", dr["ys"][(t - 4) * 128:(t - 3) * 128, :], xt[:], reads=[xb])
                else:
                    mk.dma("sp", dr["X"][t * 128:(t + 1) * 128, :], xt[:], reads=[xb])
                    yield from self.norm_gen(xt[:], xb, t, norm_l, norm_which)
            run_pipeline((tile_gen(t) for t in range(self.NT)), 5)

    def phase_F2(self, l):
        self.resid_phase(l, "f2", "MT", 8, "w_out", 2, l, 2, False)

    def phase_I(self, l):
        last = (l == self.L - 1)
        self.resid_phase(l, "i", "AT", NFF, "w_ff_down", 5, l + 1, 1, last)

    def phase_H(self, l):
        mk, dr = self.mk, self.dram
        hT, hTb, NB = self.hT, self.hT_b, self.NB
        with ExitStack() as st:
            wf = self.sbpool(st, "h_wf", [128, NKC, 256], F32, 3)
            wg = self.sbpool(st, "h_wg", [128, NKC, 512], BF16, 2)
            wu = self.sbpool(st, "h_wu", [128, NKC, 512], BF16, 2)
            pg = self.pspool(st, "h_pg", [128, 512], F32, 3)
            pu = self.pspool(st, "h_pu", [128, 512], F32, 3)
            sgp = self.sbpool(st, "h_sg", [128, 512], F32, 3)
            stp = self.sbpool(st, "h_st", [128, 512], BF16, 4)

            def load(pool, wname, c0, W):
                wt, wbuf = pool.next()
                for h0 in range(0, W, 256):
                    ft, fb = wf.next()
                    mk.dma("sp", ft[:], dr[wname][l][:, c0 + h0:c0 + h0 + 256].rearrange("(kc p) n -> p kc n", p=128), writes=[fb])
                    mk.op("pool", lambda e, ft=ft, h0=h0, wt=wt: e.tensor_copy(out=wt[:, :, h0:h0 + 256], in_=ft[:]), reads=[fb], writes=[wbuf])
                return wt, wbuf
            units = [(c0, min(512, DFF - c0)) for c0 in range(0, DFF, 512)]
            nxt = (load(wg, "w_ff_gate", *units[0]), load(wu, "w_ff_up", *units[0]))
            for ui, (c0, W) in enumerate(units):
                (wgt, wgb), (wut, wub) = nxt
                if ui + 1 < len(units):
                    nxt = (load(wg, "w_ff_gate", *units[ui + 1]), load(wu, "w_ff_up", *units[ui + 1]))
                for cc in range(W // 128):
                    for tb in range(NB):
                        g_, gb_ = pg.next()
                        u_, ub_ = pu.next()

                        def mm(e, g_=g_, u_=u_, cc=cc, tb=tb, wgt=wgt, wut=wut):
                            for kc in range(NKC):
                                e.matmul(g_[:], lhsT=wgt[:, kc, cc * 128:(cc + 1) * 128], rhs=hT[:, kc, tb * 512:(tb + 1) * 512], start=(kc == 0), stop=(kc == NKC - 1))
                            for kc in range(NKC):
                                ins = e.matmul(u_[:], lhsT=wut[:, kc, cc * 128:(cc + 1) * 128], rhs=hT[:, kc, tb * 512:(tb + 1) * 512], start=(kc == 0), stop=(kc == NKC - 1))
                            return ins
                        mk.op("pe", mm, reads=[wgb, wub] + hTb[tb * 4:(tb + 1) * 4], writes=[gb_, ub_])
                        sg, sgb = sgp.next()
                        mk.op("act", lambda e, sg=sg, g_=g_: e.activation(out=sg[:], in_=g_[:], func=AF.Silu), reads=[gb_], writes=[sgb])
                        s_, sb_ = stp.next()
                        mk.op("dve", lambda e, s_=s_, sg=sg, u_=u_: e.tensor_tensor(out=s_[:], in0=u_[:], in1=sg[:], op=ALU.mult), reads=[ub_, sgb], writes=[sb_])
                        mk.dma("sp", dr["AT"][c0 + cc * 128:c0 + (cc + 1) * 128, tb * 512:(tb + 1) * 512], s_[:], reads=[sb_])


def _bf(a):
    return np.ascontiguousarray(a.astype(ml_dtypes.bfloat16))


def host_consts(TS):
    k = {}
    k["k_ident"] = np.eye(128, dtype=np.float32)
    c = np.arange(128)
    ang = 2 * np.pi * np.outer(c, c) / 128.0
    k["k_csc"] = (np.concatenate([np.cos(ang), np.sin(ang)], axis=1) / np.sqrt(128.0)).astype(np.float32)
    for nm, T in (("P", SPL), ("S", TS)):
        t = np.arange(T, dtype=np.int64)
        ang = 2 * np.pi * ((np.outer(t, t) % T).astype(np.float64)) / T
        k["k_ct" + nm] = _bf(np.cos(ang) / np.sqrt(T))
        k["k_nst" + nm] = _bf(-np.sin(ang) / np.sqrt(T))
    j = np.arange(64)[:, None]
    i = np.arange(64)[None, :]
    tri = np.zeros((64, 6, 64), np.float32)
    tri[:, 0, :] = (j <= i) / 16.0
    tri[:, 1, :] = (j >= i) / 16.0
    tri[:, 2, :] = (j > i) / 16.0
    tri[:, 3, :] = (j < i) / 16.0
    tri[:, 4, :] = (j <= i)
    tri[:, 5, :] = (j >= i)
    k["k_tri"] = tri
    rows = TS // 64
    row = np.repeat(np.arange(rows), 64).astype(np.float32)
    col = np.tile(np.arange(64), rows).astype(np.float32)
    inv = (10000.0 ** (-np.arange(16, dtype=np.float32) * 2.0 / 32)).astype(np.float32)
    ar = row[:, None] * inv
    ac = col[:, None] * inv
    angm = np.concatenate([ar, ar, ac, ac], axis=-1)
    cos = np.cos(angm).astype(np.float32)
    sin = np.sin(angm).astype(np.float32)
    sgn = np.tile(np.concatenate([-np.ones(16), np.ones(16)]), 2).astype(np.float32)
    k["k_rope"] = np.ascontiguousarray(np.stack([cos, sin * sgn], axis=1)).astype(np.float32)
    return k


WNAMES = ["w_mod", "b_mod", "norm1", "norm2", "w_in", "w_gla_a2", "b_gla_a", "gla_norm", "diff_qk_norm", "diff_lambda",
          "diff_norm", "w_fou", "w_gla_o", "w_diff_o", "w_out", "w_ff_gate", "w_ff_up", "w_ff_down"]


def make_in_map(inp, core, L, TS, consts):
    f = lambda a: np.ascontiguousarray(np.asarray(a, dtype=np.float32))
    m = {}
    m["xp"] = f(inp["x_prompt"][2 * core:2 * core + 2]).reshape(2 * SPL, D)
    m["xs"] = f(inp["x_sample"][core]).reshape(TS, D)
    m["cvec"] = f(np.stack([np.asarray(inp["c_ctx"]), np.asarray(inp["c"][core])], axis=0))
    m["cache_k"] = f(inp["cache_diff_k"][core]).reshape(L, PAST, 512)
    m["cache_v"] = f(inp["cache_diff_v"][core]).reshape(L, PAST, 512)
    m["state"] = f(inp["state_gla"][core])
    for n in WNAMES:
        m[n] = f(inp[n])
    m.update(consts)
    return m


_CACHE = {}


def kernel(**inputs):
    L, TS, NCORE = 4, 4096, 8
    inp = {k: np.asarray(v) for k, v in inputs.items()}
    if "nc" not in _CACHE:
        _CACHE["nc"] = Builder(L=L, TS=TS).build()
        _CACHE["consts"] = host_consts(TS)
    nc = _CACHE["nc"]
    in_maps = [make_in_map(inp, c, L, TS, _CACHE["consts"]) for c in range(NCORE)]
    res = run_bass_kernel_spmd(nc, in_maps, core_ids=list(range(NCORE)))
    R = res.results
    f = lambda n: np.stack([np.asarray(R[c][n], dtype=np.float32) for c in range(NCORE)], axis=0)
    yp = f("yp").reshape(NCORE * 2, SPL, D)
    ys = f("ys").reshape(NCORE, TS, D)
    nk = f("nk").reshape(NCORE * 2, L, SPL, 4, 2, 64)
    nv = f("nv").reshape(NCORE * 2, L, SPL, 4, 128)
    ns = f("ns").reshape(NCORE * 2, L, 2, 4, 64, 128)
    return (yp, ys, nk, nv, ns)
```

```python
import math
from contextlib import ExitStack
import numpy as np
import ml_dtypes
import concourse.bass as bass
import concourse.mybir as mybir
from concourse.bass_utils import run_bass_kernel_spmd

F32 = mybir.dt.float32
BF16 = mybir.dt.bfloat16
AF = mybir.ActivationFunctionType
ALU = mybir.AluOpType
AX = mybir.AxisListType

D = 1024
NKC = 8
INC = 6688
DFF = 2816
NFF = 22
SPL = 256
PAST = 256
EPS = 1e-6
C_UF, C_QG, C_KG, C_VG, C_RG, C_AG, C_QD, C_KD, C_VD, C_GT = 0, 512, 768, 1024, 1536, 2048, 2080, 2592, 3104, 3616


class Ev:
    __slots__ = ("sem", "sid", "val", "owner", "kind")

    def __init__(self, sem, sid, val, owner, kind):
        self.sem, self.sid, self.val, self.owner, self.kind = sem, sid, val, owner, kind


class Buf:
    __slots__ = ("w", "r")

    def __init__(self):
        self.w = None
        self.r = {}


class MK:
    CE = ("pe", "act", "dve", "pool")
    KD = 6

    def __init__(self, nc):
        self.nc = nc
        self.E = dict(pe=nc.tensor, act=nc.scalar, dve=nc.vector, pool=nc.gpsimd, sp=nc.sync)
        self.nsem = 0
        self.csem = {}
        self.cnt = {}
        for e in self.CE:
            self.csem[e] = self._newsem("c_" + e)
            self.cnt[e] = 0
        self.waited = {e: {} for e in self.E}
        self.dq = {}
        for q in ("sp", "pool", "act"):
            self.dq[q] = dict(sems=[self._newsem("d_" + q) for _ in range(self.KD)], vals=[0] * self.KD, n=0)
        self.same_sync = True
        self.nins = 0

    def _newsem(self, name):
        self.nsem += 1
        return (self.nc.alloc_semaphore(name=f"{name}_{self.nsem}"), self.nsem)

    def _wait(self, e, ev):
        w = self.waited[e]
        if w.get(ev.sid, 0) >= ev.val:
            return
        self.E[e].wait_ge(ev.sem, ev.val)
        w[ev.sid] = ev.val

    def _deps(self, reads, writes):
        evs = []
        for b in reads:
            if b.w is not None:
                evs.append(b.w)
        for b in writes:
            if b.w is not None:
                evs.append(b.w)
            evs.extend(b.r.values())
        return evs

    def _record(self, ev, reads, writes):
        for b in reads:
            b.r[ev.sid] = ev
        for b in writes:
            b.w = ev
            b.r = {}

    def op(self, e, fn, reads=(), writes=()):
        for ev in self._deps(reads, writes):
            if ev.kind == "c" and ev.owner == e and (e == "pe" or not self.same_sync):
                continue
            self._wait(e, ev)
        ins = fn(self.E[e])
        self.cnt[e] += 1
        self.nins += 1
        sem, sid = self.csem[e]
        ins.then_inc(sem, 1)
        ev = Ev(sem, sid, self.cnt[e], e, "c")
        self._record(ev, reads, writes)
        return ev

    def dma(self, q, out, in_, reads=(), writes=(), **kw):
        for ev in self._deps(reads, writes):
            self._wait(q, ev)
        d = self.dq[q]
        slot = d["n"] % self.KD
        d["n"] += 1
        sem, sid = d["sems"][slot]
        if d["vals"][slot] > 0:
            self._wait(q, Ev(sem, sid, d["vals"][slot], q, "d"))
        ins = self.E[q].dma_start(out=out, in_=in_, **kw)
        d["vals"][slot] += 16
        ins.then_inc(sem, 16)
        self.nins += 1
        ev = Ev(sem, sid, d["vals"][slot], q, "d")
        self._record(ev, reads, writes)
        return ev

    def barrier(self):
        evs = []
        for e in self.CE:
            if self.cnt[e] > 0:
                sem, sid = self.csem[e]
                evs.append(Ev(sem, sid, self.cnt[e], e, "c"))
        for q, d in self.dq.items():
            for (sem, sid), v in zip(d["sems"], d["vals"]):
                if v > 0:
                    evs.append(Ev(sem, sid, v, q, "d"))
        for e in self.E:
            for ev in evs:
                self._wait(e, ev)
        for e in self.CE:
            if self.cnt[e] > 12000:
                self.csem[e] = self._newsem("c_" + e)
                self.cnt[e] = 0
        for q, d in self.dq.items():
            for i in range(self.KD):
                if d["vals"][i] > 12000:
                    d["sems"][i] = self._newsem("d_" + q)
                    d["vals"][i] = 0


class Pool_:
    def __init__(self, tiles):
        self.tiles = tiles
        self.bufs = [Buf() for _ in tiles]
        self.i = 0

    def next(self):
        k = self.i % len(self.tiles)
        self.i += 1
        return self.tiles[k], self.bufs[k]


def run_pipeline(gens, depth):
    active = []
    it = iter(gens)
    done = False
    while True:
        if not done and len(active) < depth:
            try:
                active.append(next(it))
            except StopIteration:
                done = True
        if not active:
            break
        nxt = []
        for g in active:
            try:
                next(g)
                nxt.append(g)
            except StopIteration:
                pass
        active = nxt


class Builder:
    def __init__(self, L=4, TS=4096, stop_after=None, debug=()):
        self.L, self.TS = L, TS
        self.NTOK = 2 * SPL + TS
        self.NT = self.NTOK // 128
        self.NB = self.NTOK // 512
        self.seqs = [(0, SPL), (SPL, SPL), (2 * SPL, TS)]
        self.stop_after = stop_after
        self.debug = set(debug)
        self.nc = bass.Bass("TRN2", target_bir_lowering=False)
        self.mk = MK(self.nc)
        self.dram = {}

    def din(self, name, shape, dt=F32):
        self.dram[name] = self.nc.dram_tensor(name, list(shape), dt, kind="ExternalInput").ap()
        return self.dram[name]

    def dout(self, name, shape, dt=F32):
        self.dram[name] = self.nc.dram_tensor(name, list(shape), dt, kind="ExternalOutput").ap()
        return self.dram[name]

    def dscr(self, name, shape, dt):
        kind = "ExternalOutput" if name in self.debug else "Internal"
        self.dram[name] = self.nc.dram_tensor(name, list(shape), dt, kind=kind).ap()
        return self.dram[name]

    def dbg(self, name, ap, reads):
        if "dbg_" + name in self.debug:
            o = self.nc.dram_tensor("dbg_" + name, list(ap.shape), ap.dtype, kind="ExternalOutput").ap()
            self.mk.dma("sp", o, ap, reads=reads)

    def sb(self, st, name, shape, dt):
        self.uid = getattr(self, "uid", 0) + 1
        return st.enter_context(self.nc.sbuf_tensor(f"{name}_u{self.uid}", list(shape), dt))

    def ps(self, st, name, shape, dt=F32):
        self.uid = getattr(self, "uid", 0) + 1
        return st.enter_context(self.nc.psum_tensor(f"{name}_u{self.uid}", list(shape), dt))

    def sbpool(self, st, name, shape, dt, n):
        return Pool_([self.sb(st, f"{name}{i}", shape, dt) for i in range(n)])

    def pspool(self, st, name, shape, dt, n):
        return Pool_([self.ps(st, f"{name}{i}", shape, dt) for i in range(n)])

    def filler_setup(self, st):
        self.fz = self.sb(st, "fill_z", [128, 512], BF16)
        self.fp = self.ps(st, "fill_p", [128, 512], F32)
        self.fb = Buf()
        self.mk.op("pool", lambda e: e.memset(self.fz[:], 0.0), writes=[self.fb])

    def filler(self, n):
        def mm(e):
            for _ in range(n):
                ins = e.matmul(self.fp[:], lhsT=self.fz[:, 0:128], rhs=self.fz[:], start=True, stop=True)
            return ins
        self.mk.op("pe", mm, reads=[self.fb], writes=[])

    def cond_of_tile(self, t):
        return 0 if t < (2 * SPL) // 128 else 1

    def declare(self):
        L, TS, NTOK = self.L, self.TS, self.NTOK
        di = self.din
        di("xp", [2 * SPL, D]); di("xs", [TS, D]); di("cvec", [2, D])
        di("cache_k", [L, PAST, 512]); di("cache_v", [L, PAST, 512]); di("state", [L, 2, 4, 64, 128])
        di("w_mod", [L, D, 6 * D]); di("b_mod", [L, 6 * D]); di("norm1", [L, D]); di("norm2", [L, D])
        di("w_in", [L, D, INC]); di("w_gla_a2", [L, 2, 16, 256]); di("b_gla_a", [L, 2, 256])
        di("gla_norm", [L, 128]); di("diff_qk_norm", [L, 2, 64]); di("diff_lambda", [L, 4, 64]); di("diff_norm", [L, 128])
        di("w_fou", [L, 512, D]); di("w_gla_o", [L, 512, D]); di("w_diff_o", [L, 512, D]); di("w_out", [L, D, D])
        di("w_ff_gate", [L, D, DFF]); di("w_ff_up", [L, D, DFF]); di("w_ff_down", [L, DFF, D])
        di("k_ident", [128, 128]); di("k_csc", [128, 256])
        di("k_ctP", [SPL, SPL], BF16); di("k_nstP", [SPL, SPL], BF16)
        di("k_ctS", [TS, TS], BF16); di("k_nstS", [TS, TS], BF16)
        di("k_tri", [64, 6, 64])
        di("k_rope", [TS, 2, 64])
        do = self.dout
        do("yp", [2 * SPL, D]); do("ys", [TS, D])
        do("nk", [2, L, SPL, 512]); do("nv", [2, L, SPL, 512]); do("ns", [2, L, 2, 4, 64, 128])
        ds = self.dscr
        ds("X", [NTOK, D], F32); ds("MOD", [L, 2, 6 * D], F32)
        ds("UFT", [512, NTOK], BF16); ds("QGT", [256, NTOK], BF16); ds("KGT", [256, NTOK], BF16)
        ds("KG", [NTOK, 256], BF16); ds("VG", [NTOK, 512], BF16); ds("RG", [NTOK, 512], BF16)
        ds("LG", [NTOK, 512], F32); ds("QDT", [512, NTOK], BF16); ds("KDT", [512, NTOK], BF16)
        ds("VD", [NTOK, 512], BF16); ds("GT", [3072, NTOK], BF16)
        ds("FT", [512, NTOK], BF16); ds("OGT", [512, NTOK], BF16); ds("ODT", [512, NTOK], BF16)
        ds("OF", [NTOK, 512], F32); ds("OB", [NTOK, 512], F32)
        ds("MT", [D, NTOK], BF16); ds("AT", [DFF, NTOK], BF16)

    def build(self):
        self.declare()
        nc, mk, dr = self.nc, self.mk, self.dram
        L = self.L
        with ExitStack() as st:
            self.hT = self.sb(st, "hT", [128, NKC, self.NTOK], BF16)
            self.hT_b = [Buf() for _ in range(self.NT)]
            self.ident_f = self.sb(st, "ident_f", [128, 128], F32)
            self.ident_b = self.sb(st, "ident_b", [128, 128], BF16)
            self.AB = self.sb(st, "AB", [128, L, 2, 4, 8], F32)
            self.lam = self.sb(st, "lam", [128, L, 2], F32)
            self.cbuf = Buf()
            mk.dma("sp", self.ident_f[:], dr["k_ident"], writes=[self.cbuf])
            mk.op("dve", lambda e: e.tensor_copy(out=self.ident_b[:], in_=self.ident_f[:]), reads=[self.cbuf], writes=[self.cbuf])
            for r0 in range(0, self.NTOK, 512):
                src = dr["xp"][r0:r0 + 512, :] if r0 < 2 * SPL else dr["xs"][r0 - 2 * SPL:r0 - 2 * SPL + 512, :]
                mk.dma("sp", dr["X"][r0:r0 + 512, :], src)
            self.prologue()
            mk.barrier()
            if self.stop_after == "prologue":
                return self.finish()
            self.phase_A(0)
            for l in range(L):
                for ph in (self.phase_B, self.phase_C, self.phase_D, self.phase_E, self.phase_F1, self.phase_F2,
                           self.phase_H, self.phase_I):
                    mk.barrier()
                    ph(l)
                    if self.stop_after == (ph.__name__[6:], l):
                        return self.finish()
            return self.finish()

    def finish(self):
        self.mk.barrier()
        return self.nc

    def prologue(self):
        nc, mk, dr, L = self.nc, self.mk, self.dram, self.L
        with ExitStack() as st:
            c16 = self.sb(st, "c16", [16, 128], F32)
            sT = self.sb(st, "sT", [128, 16], F32)
            bm = self.sb(st, "bm", [2, 6 * D], F32)
            mrow = self.sb(st, "mrow", [2, 6 * D], F32)
            wm = self.sbpool(st, "wm", [128, NKC, 512], F32, 3)
            pm = self.pspool(st, "pm", [128, 512], F32, 2)
            pt = self.ps(st, "pt", [128, 128], F32)
            VR = self.sb(st, "VR", [112, 128], F32)
            VC = self.sb(st, "VC", [128, 112], F32)
            dl = self.sb(st, "dl", [128, 4, 64], F32)
            pr = self.sb(st, "pr", [128, 2, 64], F32)
            sm = self.sb(st, "sm", [128, 2], F32)
            b_c16, b_sT, b_bm, b_mrow, b_pt, b_VR, b_VC, b_dl = (Buf() for _ in range(8))
            b_VRs = (Buf(), Buf(), Buf())
            mk.dma("sp", c16[:], dr["cvec"].rearrange("r (kc p) -> (r kc) p", p=128), writes=[b_c16])
            mk.op("pe", lambda e: e.transpose(out=pt[:, 0:16], in_=c16[:], identity=self.ident_f[0:16, 0:16]),
                  reads=[b_c16, self.cbuf], writes=[b_pt])
            mk.op("act", lambda e: e.activation(out=sT[:], in_=pt[:, 0:16], func=AF.Silu), reads=[b_pt], writes=[b_sT])
            sT2 = self.sb(st, "sT2", [128, NKC, 2], F32)
            mk.op("dve", lambda e: e.tensor_copy(out=sT2[:], in_=sT[:].rearrange("p (r kc) -> p kc r", r=2)), reads=[b_sT], writes=[b_sT])
            sTv = sT2[:]
            self.dbg("sT", sT[:], [b_sT])
            for l in range(L):
                mk.dma("sp", bm[:], dr["b_mod"][l].partition_broadcast(2), writes=[b_bm])
                for cc in range(12):
                    wt, wb_ = wm.next()
                    mk.dma("sp", wt[:], dr["w_mod"][l][:, cc * 512:(cc + 1) * 512].rearrange("(kc p) n -> p kc n", p=128),
                           writes=[wb_])
                    pmt, pmb = pm.next()

                    def mm(e, wt=wt, pmt=pmt):
                        for kc in range(NKC):
                            ins = e.matmul(pmt[0:2, :], lhsT=sTv[:, kc, :], rhs=wt[:, kc, :], start=(kc == 0), stop=(kc == NKC - 1))
                        return ins
                    mk.op("pe", mm, reads=[wb_, b_sT], writes=[pmb])
                    mk.op("dve", lambda e, pmt=pmt, cc=cc: e.tensor_tensor(out=mrow[:, cc * 512:(cc + 1) * 512], in0=pmt[0:2, :],
                                                                          in1=bm[:, cc * 512:(cc + 1) * 512], op=ALU.add),
                          reads=[pmb, b_bm], writes=[b_mrow])
                b_MOD = Buf()
                mk.dma("sp", dr["MOD"][l], mrow[:], reads=[b_mrow], writes=[b_MOD])
                b_V0, b_V1, b_V2 = b_VRs
                mk.dma("sp", VR[0:96, :], dr["MOD"][l].rearrange("r (j p) -> (r j) p", p=128), reads=[b_MOD], writes=[b_V0])
                mk.dma("sp", VR[96:104, :], dr["norm1"][l].rearrange("(j p) -> j p", p=128), writes=[b_V1])
                mk.dma("sp", VR[104:112, :], dr["norm2"][l].rearrange("(j p) -> j p", p=128), writes=[b_V2])
                mk.op("pe", lambda e: e.transpose(out=pt[:, 0:112], in_=VR[:], identity=self.ident_f[0:112, 0:112]),
                      reads=[b_V0, b_V1, b_V2], writes=[b_pt])
                mk.op("dve", lambda e: e.tensor_copy(out=VC[:], in_=pt[:, 0:112]), reads=[b_pt], writes=[b_VC])
                for r in range(2):
                    c0 = r * 48
                    mk.op("dve", lambda e, r=r, c0=c0: e.scalar_tensor_tensor(out=self.AB[:, l, r, 0, :], in0=VC[:, c0 + 8:c0 + 16], scalar=1.0,
                                                                            in1=VC[:, 96:104], op0=ALU.add, op1=ALU.mult),
                          reads=[b_VC], writes=[self.cbuf])
                    mk.op("dve", lambda e, r=r, c0=c0: e.tensor_copy(out=self.AB[:, l, r, 1, :], in_=VC[:, c0:c0 + 8]), reads=[b_VC], writes=[self.cbuf])
                    mk.op("dve", lambda e, r=r, c0=c0: e.scalar_tensor_tensor(out=self.AB[:, l, r, 2, :], in0=VC[:, c0 + 32:c0 + 40], scalar=1.0,
                                                                            in1=VC[:, 104:112], op0=ALU.add, op1=ALU.mult),
                          reads=[b_VC], writes=[self.cbuf])
                    mk.op("dve", lambda e, r=r, c0=c0: e.tensor_copy(out=self.AB[:, l, r, 3, :], in_=VC[:, c0 + 24:c0 + 32]), reads=[b_VC], writes=[self.cbuf])
                mk.dma("sp", dl[:], dr["diff_lambda"][l].rearrange("a d -> (a d)").partition_broadcast(128), writes=[b_dl])
                dlv = dl[:].rearrange("p (a b) d -> p a b d", b=2)
                mk.op("dve", lambda e: e.tensor_tensor(out=pr[:], in0=dlv[:, :, 0, :], in1=dlv[:, :, 1, :], op=ALU.mult), reads=[b_dl], writes=[b_dl])
                mk.op("dve", lambda e: e.reduce_sum(out=sm[:], in_=pr[:], axis=AX.X), reads=[b_dl], writes=[b_dl])
                mk.op("act", lambda e: e.activation(out=sm[:], in_=sm[:], func=AF.Exp), reads=[b_dl], writes=[b_dl])
                lam_init = 0.8 - 0.6 * math.exp(-0.3 * l)
                mk.op("dve", lambda e, l=l, li=lam_init: e.tensor_scalar(out=self.lam[:, l, 0:1], in0=sm[:, 0:1], scalar1=sm[:, 1:2], scalar2=li,
                                                                       op0=ALU.subtract, op1=ALU.add), reads=[b_dl], writes=[self.cbuf])
                mk.op("dve", lambda e, l=l: e.tensor_scalar(out=self.lam[:, l, 1:2], in0=self.lam[:, l, 0:1], scalar1=-1.0, scalar2=None, op0=ALU.mult),
                      reads=[self.cbuf], writes=[self.cbuf])

    def norm_setup(self, st):
        self.n_xn = self.sbpool(st, "n_xn", [128, D], BF16, 3)
        self.n_ss = self.sbpool(st, "n_ss", [128, 2], F32, 4)
        self.n_pT = self.pspool(st, "n_pT", [128, NKC, 128], BF16, 2)

    def norm_gen(self, xt, xb, tile, l, which):
        mk = self.mk
        cond = self.cond_of_tile(tile)
        xn, xnb = self.n_xn.next()
        ss, ssb = self.n_ss.next()
        mk.op("act", lambda e: e.activation(out=xn[:], in_=xt, func=AF.Square, accum_out=ss[:, 0:1]), reads=[xb], writes=[xnb, ssb])
        mk.op("act", lambda e: e.activation(out=ss[:, 1:2], in_=ss[:, 0:1], func=AF.Ln, scale=1.0 / D, bias=EPS), reads=[ssb], writes=[ssb])
        mk.op("act", lambda e: e.activation(out=ss[:, 1:2], in_=ss[:, 1:2], func=AF.Exp, scale=-0.5), reads=[ssb], writes=[ssb])
        mk.op("act", lambda e: e.activation(out=xn[:], in_=xt, func=AF.Copy, scale=ss[:, 1:2]), reads=[xb, ssb], writes=[xnb])
        yield
        pT, pTb = self.n_pT.next()

        def tr(e):
            for kc in range(NKC):
                ins = e.transpose(out=pT[:, kc, :], in_=xn[:, kc * 128:(kc + 1) * 128], identity=self.ident_b[:])
            return ins
        mk.op("pe", tr, reads=[xnb, self.cbuf], writes=[pTb])
        yield
        hb = self.hT_b[tile]
        a_i, b_i = (0, 1) if which == 1 else (2, 3)
        for kc in range(NKC):
            dst = self.hT[:, kc, tile * 128:(tile + 1) * 128]
            A = self.AB[:, l, cond, a_i, kc:kc + 1]
            B = self.AB[:, l, cond, b_i, kc:kc + 1]
            if kc % 2 == 0:
                mk.op("dve", lambda e, dst=dst, A=A, B=B, kc=kc: e.tensor_scalar(out=dst, in0=pT[:, kc, :], scalar1=A, scalar2=B, op0=ALU.mult, op1=ALU.add),
                      reads=[pTb, self.cbuf], writes=[hb])
            else:
                mk.op("act", lambda e, dst=dst, A=A, B=B, kc=kc: e.activation(out=dst, in_=pT[:, kc, :], func=AF.Identity, scale=A, bias=B),
                      reads=[pTb, self.cbuf], writes=[hb])

    def phase_A(self, l):
        mk, dr = self.mk, self.dram
        with ExitStack() as st:
            self.norm_setup(st)
            xp = self.sbpool(st, "a_x", [128, D], F32, 4)

            def tile_gen(t):
                xt, xb = xp.next()
                mk.dma("sp", xt[:], dr["X"][t * 128:(t + 1) * 128, :], writes=[xb])
                yield from self.norm_gen(xt[:], xb, t, l, 1)
            run_pipeline((tile_gen(t) for t in range(self.NT)), 4)

    def phase_B(self, l):
        mk, dr, nc = self.mk, self.dram, self.nc
        NT, NB, NTOK, TS = self.NT, self.NB, self.NTOK, self.TS
        hT, hTb = self.hT, self.hT_b
        NPT = (2 * SPL) // 128
        with ExitStack() as st:
            wf = self.sbpool(st, "b_wf", [128, NKC, 256], F32, 2)
            wb = self.sbpool(st, "b_wb", [128, NKC, 512], BF16, 4)
            pacc = self.pspool(st, "b_pa", [128, 512], F32, 4)
            pTp = self.pspool(st, "b_pT", [128, 4, 128], BF16, 2)
            stF = self.sbpool(st, "b_sF", [128, 512], BF16, 4)
            stT = self.sbpool(st, "b_sT", [128, 4, 512], BF16, 4)
            tmp = self.sbpool(st, "b_tmp", [128, 512], F32, 3)
            rawp = self.sbpool(st, "b_raw", [128, 512], F32, 3)
            sqp = self.sbpool(st, "b_sq", [128, 512], F32, 2)
            up = self.sbpool(st, "b_u", [128, 512], F32, 3)
            wp = self.sbpool(st, "b_w", [128, 512], F32, 2)
            tbf = self.sbpool(st, "b_tbf", [128, 512], BF16, 4)
            small = self.sbpool(st, "b_sm", [128, 16], F32, 6)
            aT = self.sb(st, "b_aT", [33, NTOK], BF16)
            aTb = Buf()
            BDf = self.sb(st, "b_BDf", [33, 512], F32)
            BD = self.sb(st, "b_BD", [33, 512], BF16)
            gqk = self.sb(st, "b_gqk", [128, 2, 64], F32)
            rope = self.sb(st, "b_rope", [128, TS // 128, 2, 64], F32)
            gsw = self.sb(st, "b_gsw", [128, 64], F32)
            tabb = Buf()
            cb = Buf()
            mk.dma("sp", gqk[:], dr["diff_qk_norm"][l].rearrange("a d -> (a d)").partition_broadcast(128), writes=[cb])
            bdb = Buf()
            mk.op("pool", lambda e: e.memset(BDf[:], 0.0), writes=[bdb])
            mk.dma("sp", BDf[0:16, 0:256], dr["w_gla_a2"][l, 0], writes=[bdb])
            mk.dma("sp", BDf[16:32, 256:512], dr["w_gla_a2"][l, 1], writes=[bdb])
            mk.dma("sp", BDf[32:33, :], dr["b_gla_a"][l:l + 1].rearrange("o a d -> o (a d)"), writes=[bdb])
            mk.barrier()
            mk.op("pool", lambda e: e.tensor_copy(out=BD[:], in_=BDf[:]), reads=[bdb], writes=[bdb])
            mk.op("pool", lambda e: e.memset(aT[32:33, :], 1.0), writes=[aTb])

            w_in = dr["w_in"][l]
            self.filler_setup(st)

            def load_unit(c0, W):
                wt, wbuf = wb.next()
                for h0 in range(0, W, 256):
                    ww = min(256, W - h0)
                    ft, fb = wf.next()
                    mk.dma("sp", ft[:, :, 0:ww], w_in[:, c0 + h0:c0 + h0 + ww].rearrange("(kc p) n -> p kc n", p=128), writes=[fb])
                    mk.op("pool", lambda e, ft=ft, h0=h0, ww=ww: e.tensor_copy(out=wt[:, :, h0:h0 + ww], in_=ft[:, :, 0:ww]),
                          reads=[fb], writes=[wbuf])
                return wt, wbuf

            def mm_F(wt, wbuf, cc, tb, M=128):
                pt, pb = pacc.next()

                def mm(e):
                    for kc in range(NKC):
                        ins = e.matmul(pt[0:M, :], lhsT=wt[:, kc, cc * 128:cc * 128 + M], rhs=hT[:, kc, tb * 512:(tb + 1) * 512],
                                       start=(kc == 0), stop=(kc == NKC - 1))
                    return ins
                mk.op("pe", mm, reads=[wbuf] + hTb[tb * 4:(tb + 1) * 4], writes=[pb])
                return pt, pb

            def mm_T(wt, wbuf, t, c_lo, W):
                pt, pb = pacc.next()

                def mm(e):
                    for kc in range(NKC):
                        ins = e.matmul(pt[:, 0:W], lhsT=hT[:, kc, t * 128:(t + 1) * 128], rhs=wt[:, kc, c_lo:c_lo + W],
                                       start=(kc == 0), stop=(kc == NKC - 1))
                    return ins
                mk.op("pe", mm, reads=[wbuf, hTb[t]], writes=[pb])
                return pt, pb

            flip = [0]

            def evac(dst, src, reads, writes, func=None, scale=1.0, eng=None):
                if func is None and scale == 1.0 and eng is None:
                    flip[0] ^= 1
                    eng = "dve" if flip[0] else "act"
                if func is None and scale == 1.0 and eng == "dve":
                    mk.op("dve", lambda e: e.tensor_copy(out=dst, in_=src), reads=reads, writes=writes)
                elif func is None and eng == "dve":
                    mk.op("dve", lambda e: e.tensor_scalar(out=dst, in0=src, scalar1=scale, scalar2=None, op0=ALU.mult), reads=reads, writes=writes)
                else:
                    f = AF.Copy if func is None else func
                    mk.op("act", lambda e: e.activation(out=dst, in_=src, func=f, scale=scale), reads=reads, writes=writes)

            def do_F(wt, wbuf, ncc, dst, func=None, scale=1.0, eng=None, cc0=0):
                for cc in range(ncc):
                    for tb in range(NB):
                        pt, pb = mm_F(wt, wbuf, cc0 + cc, tb)
                        s, sb_ = stF.next()
                        evac(s[:], pt[:], [pb], [sb_], func, scale, eng)
                        mk.dma("sp", dst[cc * 128:(cc + 1) * 128, tb * 512:(tb + 1) * 512], s[:], reads=[sb_])

            def T_plain_gens(wt, wbuf, c_lo, W, dst, func=None, f32_out=None, eng=None):
                box = [None]

                def tile(t):
                    pt, pb = mm_T(wt, wbuf, t, c_lo, W)
                    yield
                    if t % 4 == 0:
                        box[0] = stT.next()
                    s4, s4b = box[0]
                    fo = f32_out(t) if f32_out is not None else None
                    if fo is not None:
                        tt, ttb = tmp.next()
                        evac(tt[:, 0:W], pt[:, 0:W], [pb], [ttb], eng="act")
                        mk.dma("sp", fo, tt[:, 0:W], reads=[ttb])
                        mk.op("pool", lambda e: e.tensor_copy(out=s4[:, t % 4, 0:W], in_=tt[:, 0:W]), reads=[ttb], writes=[s4b])
                    else:
                        evac(s4[:, t % 4, 0:W], pt[:, 0:W], [pb], [s4b], func, eng=eng)
                    if t % 4 == 3:
                        tb = t // 4
                        mk.dma("sp", dst[tb * 512:(tb + 1) * 512, :].rearrange("(t p) c -> p t c", p=128), s4[:, :, 0:W], reads=[s4b])
                return [tile(t) for t in range(NT)]

            def do_T_plain(wt, wbuf, c_lo, W, dst, func=None, f32_out=None):
                run_pipeline(T_plain_gens(wt, wbuf, c_lo, W, dst, func, f32_out), 2)

            def do_qk(wt, wbuf, j, dstT, co=None):
                gain = gqk[:, j, :].unsqueeze(1).broadcast_to([128, 8, 64])
                nts = TS // 128
                gv = gqk[:, j, :].rearrange("p (a h f) -> p a h f", a=2, h=2)
                swv = gsw[:].rearrange("p (a h f) -> p a h f", a=2, h=2)
                mk.op("dve", lambda e: e.tensor_copy(out=swv[:, :, 0, :], in_=gv[:, :, 1, :]), reads=[cb], writes=[tabb])
                mk.op("dve", lambda e: e.tensor_copy(out=swv[:, :, 1, :], in_=gv[:, :, 0, :]), reads=[cb], writes=[tabb])
                mk.dma("sp", rope[:], dr["k_rope"].rearrange("(t p) a d -> p t a d", p=128), writes=[tabb])
                cg = rope[:, :, 0, :]
                sg = rope[:, :, 1, :]
                mk.op("dve", lambda e: e.tensor_tensor(out=cg, in0=cg, in1=gqk[:, j, :].unsqueeze(1).broadcast_to([128, nts, 64]), op=ALU.mult),
                      reads=[cb], writes=[tabb])
                mk.op("dve", lambda e: e.tensor_tensor(out=sg, in0=sg, in1=gsw[:].unsqueeze(1).broadcast_to([128, nts, 64]), op=ALU.mult),
                      reads=[cb, tabb], writes=[tabb])
                v3 = lambda x: x[:].rearrange("p (g d) -> p g d", g=8)
                v4 = lambda x: x[:].rearrange("p (g a x) -> p g a x", g=8, a=2)
                s4box = [None]

                def qk_tile(t):
                    pt, pb = mm_T(wt, wbuf, t, 0, 512)
                    if co is None:
                        self.filler(6)
                    yield
                    raw, rawb = rawp.next()
                    mk.op("act", lambda e: e.activation(out=raw[:], in_=pt[:], func=AF.Copy), reads=[pb], writes=[rawb])
                    sq, sqb = sqp.next()
                    mk.op("act", lambda e: e.activation(out=sq[:], in_=pt[:], func=AF.Square), reads=[pb], writes=[sqb])
                    sm, smb = small.next()
                    mk.op("dve", lambda e: e.reduce_sum(out=sm[:, 0:8], in_=v3(sq), axis=AX.X), reads=[sqb], writes=[smb])
                    yield
                    mk.op("act", lambda e: e.activation(out=sm[:, 8:16], in_=sm[:, 0:8], func=AF.Ln, scale=1.0 / 64, bias=EPS), reads=[smb], writes=[smb])
                    mk.op("act", lambda e: e.activation(out=sm[:, 8:16], in_=sm[:, 8:16], func=AF.Exp, scale=-0.5), reads=[smb], writes=[smb])
                    rstd = sm[:, 8:16].unsqueeze(2).broadcast_to([128, 8, 64])
                    qr, qrb = tbf.next()
                    if t < NPT:
                        qn, qnb = up.next()
                    else:
                        ti = t - NPT
                        cosv = cg[:, ti, :].unsqueeze(1).broadcast_to([128, 8, 64])
                        sinv = sg[:, ti, :].rearrange("p (a x) -> p a x", a=2).unsqueeze(1).broadcast_to([128, 8, 2, 32])
                        u, ub = up.next()
                        mk.op("dve", lambda e: e.tensor_tensor(out=v3(u), in0=v3(raw), in1=cosv, op=ALU.mult), reads=[rawb, tabb], writes=[ub])
                        w_, wb_ = wp.next()
                        mk.op("dve", lambda e: e.tensor_tensor(out=v4(w_)[:, :, :, 0:16], in0=v4(raw)[:, :, :, 16:32], in1=sinv[:, :, :, 0:16], op=ALU.mult),
                              reads=[rawb, tabb], writes=[wb_])
                        mk.op("dve", lambda e: e.tensor_tensor(out=v4(w_)[:, :, :, 16:32], in0=v4(raw)[:, :, :, 0:16], in1=sinv[:, :, :, 16:32], op=ALU.mult),
                              reads=[rawb, tabb], writes=[wb_])
                        mk.op("pool", lambda e: e.tensor_tensor(out=u[:], in0=u[:], in1=w_[:], op=ALU.add), reads=[ub, wb_], writes=[ub])
                    yield
                    if t < NPT:
                        mk.op("dve", lambda e: e.tensor_tensor(out=v3(qn), in0=v3(raw), in1=rstd, op=ALU.mult), reads=[rawb, smb], writes=[qnb])
                        mk.op("pool", lambda e: e.tensor_tensor(out=v3(qn), in0=v3(qn), in1=gain, op=ALU.mult), reads=[qnb, cb], writes=[qnb])
                        if j == 1:
                            sq_i, tt_i = t // 2, t % 2
                            mk.dma("sp", dr["nk"][sq_i, l, tt_i * 128:(tt_i + 1) * 128, :], qn[:], reads=[qnb])
                        mk.op("act", lambda e: e.activation(out=qr[:], in_=qn[:], func=AF.Copy), reads=[qnb], writes=[qrb])
                    else:
                        mk.op("dve", lambda e: e.tensor_tensor(out=v3(qr), in0=v3(u), in1=rstd, op=ALU.mult), reads=[ub, smb], writes=[qrb])
                    pT, pTb = pTp.next()

                    def tr(e):
                        for h in range(4):
                            ins = e.transpose(out=pT[:, h, :], in_=qr[:, h * 128:(h + 1) * 128], identity=self.ident_b[:])
                        return ins
                    mk.op("pe", tr, reads=[qrb], writes=[pTb])
                    yield
                    if t % 4 == 0:
                        s4box[0] = stT.next()
                    s4, s4b = s4box[0]
                    evac(s4[:, :, (t % 4) * 128:(t % 4 + 1) * 128], pT[:], [pTb], [s4b], eng="act")
                    if t % 4 == 3:
                        tb = t // 4
                        mk.dma("sp", dstT.rearrange("(h p) t -> p h t", p=128)[:, :, tb * 512:(tb + 1) * 512], s4[:], reads=[s4b])
                gens = [qk_tile(t) for t in range(NT)]
                if co is not None:
                    mixed = []
                    for g1, g2 in zip(gens, co):
                        mixed += [g1, g2]
                    run_pipeline(mixed, 8)
                else:
                    run_pipeline(gens, 4)

            def do_ag(wt, wbuf):
                for tb in range(NB):
                    pt, pb = mm_F(wt, wbuf, 0, tb, M=32)
                    evac(aT[0:32, tb * 512:(tb + 1) * 512], pt[0:32, :], [pb], [aTb])
                for t in range(NT):
                    pt, pb = pacc.next()
                    mk.op("pe", lambda e, pt=pt, t=t: e.matmul(pt[:], lhsT=aT[0:33, t * 128:(t + 1) * 128], rhs=BD[0:33, :], start=True, stop=True),
                          reads=[aTb, bdb], writes=[pb])
                    e1, e1b = tmp.next()
                    mk.op("act", lambda e, e1=e1, pt=pt: e.activation(out=e1[:], in_=pt[:], func=AF.Exp, scale=-1.0), reads=[pb], writes=[e1b])
                    mk.op("act", lambda e, e1=e1: e.activation(out=e1[:], in_=e1[:], func=AF.Ln, bias=1.0), reads=[e1b], writes=[e1b])
                    mk.dma("sp", dr["LG"][t * 128:(t + 1) * 128, :], e1[:], reads=[e1b])

            def nv_out(t):
                if t >= NPT:
                    return None
                return dr["nv"][t // 2, l, (t % 2) * 128:(t % 2 + 1) * 128, :]

            U = {"uf": (C_UF, 512), "qk": (C_QG, 512), "vg": (C_VG, 512), "rg": (C_RG, 512), "ag": (C_AG, 32),
                 "qd": (C_QD, 512), "kd": (C_KD, 512), "vd": (C_VD, 512)}
            for i in range(6):
                U[f"g{i}"] = (C_GT + 512 * i, 512)
            steps = [["uf"], ["qk"], ["rg"], ["ag"], ["qd", "vg"], ["kd", "vd"]] + [[f"g{i}"] for i in range(6)]
            load_step = lambda names: [load_unit(*U[n]) for n in names]
            nxt = load_step(steps[0])
            for si_, names in enumerate(steps):
                cur = nxt
                if si_ + 1 < len(steps):
                    nxt = load_step(steps[si_ + 1])
                wt, wbuf = cur[0]
                name = names[0]
                if name == "uf":
                    do_F(wt, wbuf, 4, dr["UFT"], eng="dve")
                elif name == "qk":
                    do_F(wt, wbuf, 2, dr["QGT"], scale=0.125, eng="dve")
                    do_F(wt, wbuf, 2, dr["KGT"], eng="dve", cc0=2)
                    do_T_plain(wt, wbuf, 256, 256, dr["KG"])
                elif name == "rg":
                    do_T_plain(wt, wbuf, 0, 512, dr["RG"], func=AF.Silu)
                elif name == "ag":
                    do_ag(wt, wbuf)
                elif name == "qd":
                    do_qk(wt, wbuf, 0, dr["QDT"], co=T_plain_gens(cur[1][0], cur[1][1], 0, 512, dr["VG"], eng="act"))
                elif name == "kd":
                    do_qk(wt, wbuf, 1, dr["KDT"], co=T_plain_gens(cur[1][0], cur[1][1], 0, 512, dr["VD"], f32_out=nv_out))
                else:
                    gi = int(name[1:])
                    do_F(wt, wbuf, 4, dr["GT"][gi * 512:(gi + 1) * 512, :], func=AF.Sigmoid)


    def phase_C(self, l):
        mk, dr = self.mk, self.dram
        with ExitStack() as st:
            cscf = self.sb(st, "c_cscf", [128, 256], F32)
            csc = self.sb(st, "c_csc", [128, 256], BF16)
            cb = Buf()
            mk.dma("sp", cscf[:], dr["k_csc"], writes=[cb])
            mk.op("dve", lambda e: e.tensor_copy(out=csc[:], in_=cscf[:]), reads=[cb], writes=[cb])
            ntmax = max(T for _, T in self.seqs) // 128
            PQ = self.sb(st, "c_PQ", [128, ntmax, 1024], BF16)
            PQb = [Buf() for _ in range(ntmax)]
            uTp = self.sbpool(st, "c_uT", [128, 4, 512], BF16, 2)
            ppq = self.pspool(st, "c_ppq", [128, 1024], F32, 1)
            pf = [self.ps(st, f"c_pf{g}", [128, 512], F32) for g in range(4)]
            pfb = [Buf() for _ in range(4)]
            ctp = self.sbpool(st, "c_ct", [128, 4, 512], BF16, 4)
            nstp = self.sbpool(st, "c_nst", [128, 4, 512], BF16, 4)
            fst = self.sbpool(st, "c_fst", [128, 4, 512], BF16, 2)
            UFTv = dr["UFT"].rearrange("(g p) t -> p g t", p=128)
            FTv = dr["FT"].rearrange("(g p) t -> p g t", p=128)
            for si, (t0, T) in enumerate(self.seqs):
                nt = T // 128
                ctD, nstD = (dr["k_ctP"], dr["k_nstP"]) if T == SPL else (dr["k_ctS"], dr["k_nstS"])
                PW = min(512, T)
                for pc in range(T // PW):
                    uT, uTb = uTp.next()
                    mk.dma("sp", uT[:, :, 0:PW], UFTv[:, :, t0 + pc * PW:t0 + (pc + 1) * PW], writes=[uTb])
                    for tl in range(PW // 128):
                        tile = pc * (PW // 128) + tl
                        pq, pqb = ppq.next()

                        def mm(e, uT=uT, tl=tl, pq=pq):
                            for g in range(4):
                                ins = e.matmul(pq[:, g * 256:(g + 1) * 256], lhsT=uT[:, g, tl * 128:(tl + 1) * 128], rhs=csc[:], start=True, stop=True)
                            return ins
                        mk.op("pe", mm, reads=[uTb, cb], writes=[pqb])
                        mk.op("act", lambda e, pq=pq, tile=tile: e.activation(out=PQ[:, tile, 0:512], in_=pq[:, 0:512], func=AF.Copy), reads=[pqb], writes=[PQb[tile]])
                        mk.op("dve", lambda e, pq=pq, tile=tile: e.tensor_copy(out=PQ[:, tile, 512:1024], in_=pq[:, 512:1024]), reads=[pqb], writes=[PQb[tile]])
                NP = min(512, T)
                TG = min(4, nt)
                for pb in range(T // NP):
                    for tg in range(nt // TG):
                        ct, ctb = ctp.next()
                        nst, nstb = nstp.next()
                        r0 = tg * TG * 128
                        mk.dma("sp", ct[:, 0:TG, 0:NP], ctD[r0:r0 + TG * 128, pb * NP:(pb + 1) * NP].rearrange("(tc p) n -> p tc n", p=128), writes=[ctb])
                        mk.dma("sp", nst[:, 0:TG, 0:NP], nstD[r0:r0 + TG * 128, pb * NP:(pb + 1) * NP].rearrange("(tc p) n -> p tc n", p=128), writes=[nstb])
                        for g in range(4):
                            def mm(e, g=g, tg=tg, ct=ct, nst=nst):
                                for tc in range(TG):
                                    tile = tg * TG + tc
                                    first = (tg == 0 and tc == 0)
                                    last = (tg == nt // TG - 1 and tc == TG - 1)
                                    e.matmul(pf[g][:, 0:NP], lhsT=PQ[:, tile, g * 256:g * 256 + 128], rhs=ct[:, tc, 0:NP], start=first, stop=False)
                                    ins = e.matmul(pf[g][:, 0:NP], lhsT=PQ[:, tile, g * 256 + 128:g * 256 + 256], rhs=nst[:, tc, 0:NP], start=False, stop=last)
                                return ins
                            mk.op("pe", mm, reads=[ctb, nstb] + PQb[tg * TG:(tg + 1) * TG], writes=[pfb[g]])
                    fs, fsb = fst.next()
                    for g in range(4):
                        if g % 2 == 0:
                            mk.op("act", lambda e, g=g, fs=fs: e.activation(out=fs[:, g, 0:NP], in_=pf[g][:, 0:NP], func=AF.Copy), reads=[pfb[g]], writes=[fsb])
                        else:
                            mk.op("dve", lambda e, g=g, fs=fs: e.tensor_copy(out=fs[:, g, 0:NP], in_=pf[g][:, 0:NP]), reads=[pfb[g]], writes=[fsb])
                    mk.dma("sp", FTv[:, :, t0 + pb * NP:t0 + (pb + 1) * NP], fs[:, :, 0:NP], reads=[fsb])

    def phase_D(self, l):
        mk, dr = self.mk, self.dram
        with ExitStack() as st:
            tri = self.sb(st, "d_tri", [64, 6, 64], F32)
            cb = Buf()
            mk.dma("sp", tri[:], dr["k_tri"], writes=[cb])
            S32 = [self.sb(st, f"d_S32_{d}", [64, 4, 128], F32) for d in range(2)]
            Sb = [self.sb(st, f"d_Sb_{d}", [64, 4, 128], BF16) for d in range(2)]
            S32b = [Buf(), Buf()]
            Sbb = [Buf(), Buf()]
            Tmax = max(T for _, T in self.seqs)
            qTa = self.sb(st, "d_qT", [64, 4, Tmax], BF16)
            kTa = self.sb(st, "d_kT", [64, 4, Tmax], BF16)
            qkb = Buf()
            Lp = self.sbpool(st, "d_L", [64, 256], F32, 3)
            kp = self.sbpool(st, "d_k", [64, 256], BF16, 4)
            vp = self.sbpool(st, "d_v", [64, 512], BF16, 5)
            eqp = self.sbpool(st, "d_eq", [64, 4, 64], F32, 5)
            ekp = self.sbpool(st, "d_ek", [64, 4, 64], F32, 3)
            qsp = self.sbpool(st, "d_qs", [64, 4, 64], BF16, 4)
            ksp = self.sbpool(st, "d_ks", [64, 4, 64], BF16, 3)
            edp = self.sbpool(st, "d_ed", [64, 256], F32, 3)
            kdp = self.sbpool(st, "d_kd", [64, 256], BF16, 4)
            atp = self.sbpool(st, "d_at", [64, 4, 64], BF16, 3)
            osp = self.sbpool(st, "d_os", [64, 512], F32, 3)
            pA = self.pspool(st, "d_pA", [128, 512], F32, 2)
            pB = self.pspool(st, "d_pB", [128, 512], F32, 2)
            pS = self.pspool(st, "d_pS", [128, 512], F32, 2)
            pO = self.pspool(st, "d_pO", [128, 512], F32, 2)
            QGTv = dr["QGT"].rearrange("(h k) t -> k h t", k=64)
            KGTv = dr["KGT"].rearrange("(h k) t -> k h t", k=64)

            def chunk(t0, c, d):
                tok0 = t0 + c * 64
                Lc, Lb = Lp.next()
                kc_, kb = kp.next()
                vc, vb = vp.next()
                mk.dma("sp", Lc[:], dr["LG"][tok0:tok0 + 64, d * 256:(d + 1) * 256], writes=[Lb])
                mk.dma("sp", kc_[:], dr["KG"][tok0:tok0 + 64, :], writes=[kb])
                mk.dma("sp", vc[:], dr["VG"][tok0:tok0 + 64, :], writes=[vb])
                a, ab = pA.next()
                pc = a[0:64, 0:256].rearrange("p (h i) -> p h i", h=4)
                psuf = a[0:64, 256:512]

                def mm0(e):
                    for h in range(4):
                        e.matmul(pc[:, h, :], lhsT=Lc[:, h * 64:(h + 1) * 64], rhs=tri[:, d, :], start=True, stop=True)
                    return e.matmul(psuf, lhsT=tri[:, 2 + d, :], rhs=Lc[:], start=True, stop=True)
                mk.op("pe", mm0, reads=[Lb, cb], writes=[ab])
                yield
                eq, eqb = eqp.next()
                ek, ekb = ekp.next()
                qs, qsb = qsp.next()
                ks, ksb = ksp.next()
                ed, edb = edp.next()
                kd, kdb = kdp.next()
                mk.op("act", lambda e: e.activation(out=eq[:], in_=pc, func=AF.Exp, scale=-1.0), reads=[ab], writes=[eqb])
                mk.op("act", lambda e: e.activation(out=ek[:], in_=pc, func=AF.Exp), reads=[ab], writes=[ekb])
                mk.op("act", lambda e: e.activation(out=ed[:], in_=psuf, func=AF.Exp, scale=-1.0), reads=[ab], writes=[edb])
                mk.op("dve", lambda e: e.tensor_tensor(out=qs[:], in0=qTa[:, :, c * 64:(c + 1) * 64], in1=eq[:], op=ALU.mult), reads=[eqb, qkb], writes=[qsb])
                mk.op("pool", lambda e: e.tensor_tensor(out=ks[:], in0=kTa[:, :, c * 64:(c + 1) * 64], in1=ek[:], op=ALU.mult), reads=[ekb, qkb], writes=[ksb])
                mk.op("dve", lambda e: e.tensor_tensor(out=kd[:], in0=kc_[:], in1=ed[:], op=ALU.mult), reads=[edb, kb], writes=[kdb])
                b_, bb = pB.next()
                pa = b_[0:64, 0:256].rearrange("p (h i) -> p h i", h=4)

                def mm1(e):
                    for h in range(4):
                        ins = e.matmul(pa[:, h, :], lhsT=ks[:, h, :], rhs=qs[:, h, :], start=True, stop=True)
                    return ins
                mk.op("pe", mm1, reads=[ksb, qsb], writes=[bb])
                yield
                at, atb = atp.next()
                mk.op("dve", lambda e: e.tensor_tensor(out=at[:], in0=pa, in1=tri[:, 4 + d, :].unsqueeze(1).broadcast_to([64, 4, 64]), op=ALU.mult),
                      reads=[bb, cb], writes=[atb])
                o_, ob = pO.next()
                s_, sb_ = pS.next()
                ps_ = s_[0:64, :].rearrange("p (h v) -> p h v", h=4)

                def mm2(e):
                    for h in range(4):
                        e.matmul(o_[0:64, h * 128:(h + 1) * 128], lhsT=at[:, h, :], rhs=vc[:, h * 128:(h + 1) * 128], start=True, stop=False)
                        e.matmul(o_[0:64, h * 128:(h + 1) * 128], lhsT=qs[:, h, :], rhs=Sb[d][:, h, :], start=False, stop=True)
                    for h in range(4):
                        ins = e.matmul(ps_[:, h, :], lhsT=kd[:, h * 64:(h + 1) * 64], rhs=vc[:, h * 128:(h + 1) * 128], start=True, stop=True)
                    return ins
                mk.op("pe", mm2, reads=[atb, vb, qsb, Sbb[d], kdb], writes=[ob, sb_])
                yield
                os_, osb = osp.next()
                mk.op("act", lambda e: e.activation(out=os_[:], in_=o_[0:64, :], func=AF.Copy), reads=[ob], writes=[osb])
                mk.dma("sp", dr["OF" if d == 0 else "OB"][tok0:tok0 + 64, :], os_[:], reads=[osb])
                col = 63 if d == 0 else 0
                for h in range(4):
                    mk.op("dve", lambda e, h=h: e.scalar_tensor_tensor(out=S32[d][:, h, :], in0=S32[d][:, h, :], scalar=eq[:, h, col:col + 1], in1=ps_[:, h, :],
                                                                     op0=ALU.mult, op1=ALU.add), reads=[eqb, sb_, S32b[d]], writes=[S32b[d]])
                mk.op("act", lambda e: e.activation(out=Sb[d][:], in_=S32[d][:], func=AF.Copy), reads=[S32b[d]], writes=[Sbb[d]])

            for si, (t0, T) in enumerate(self.seqs):
                nch = T // 64
                mk.dma("sp", qTa[:, :, 0:T], QGTv[:, :, t0:t0 + T], writes=[qkb])
                mk.dma("sp", kTa[:, :, 0:T], KGTv[:, :, t0:t0 + T], writes=[qkb])
                for d in range(2):
                    if si < 2:
                        mk.op("pool", lambda e, d=d: e.memset(S32[d][:], 0.0), writes=[S32b[d]])
                    else:
                        mk.dma("sp", S32[d][:], dr["state"][l, d].rearrange("h k v -> k h v"), writes=[S32b[d]])
                    mk.op("act", lambda e, d=d: e.activation(out=Sb[d][:], in_=S32[d][:], func=AF.Copy), reads=[S32b[d]], writes=[Sbb[d]])
                gens = []
                for c in range(nch):
                    gens.append(chunk(t0, c, 0))
                    gens.append(chunk(t0, nch - 1 - c, 1))
                run_pipeline(gens, 4)
                if si < 2:
                    for d in range(2):
                        mk.dma("sp", dr["ns"][si, l, d].rearrange("h k v -> k h v"), S32[d][:], reads=[S32b[d]])
        mk.barrier()
        with ExitStack() as st:
            cb = Buf()
            gn = self.sb(st, "d_gn", [128, 128], F32)
            mk.dma("sp", gn[:], dr["gla_norm"][l].partition_broadcast(128), writes=[cb])
            obp = self.sbpool(st, "d_ob", [128, 512], F32, 3)
            rgp = self.sbpool(st, "d_rg", [128, 512], BF16, 3)
            self.post_norm_T(st, "d", dr["OGT"], gn, cb, 1.0,
                             loader=lambda t, o, ob_: (mk.dma("sp", o[:], dr["OF"][t * 128:(t + 1) * 128, :], writes=[ob_])),
                             extra=(obp, rgp, dr))

    def post_norm_T(self, st, pfx, dstT, gain, gb, gscale, loader, extra):
        mk = self.mk
        obp, rgp, dr = extra
        ofp = self.sbpool(st, pfx + "_pof", [128, 512], F32, 4)
        g2p = self.sbpool(st, pfx + "_pg2", [128, 512], F32, 4)
        jkp = self.sbpool(st, pfx + "_pjk", [128, 128], BF16, 2)
        smp = self.sbpool(st, pfx + "_psm", [128, 8], F32, 4)
        onp = self.sbpool(st, pfx + "_pon", [128, 512], BF16, 3)
        pTp = self.pspool(st, pfx + "_ppT", [128, 4, 128], BF16, 2)
        stT = self.sbpool(st, pfx + "_pst", [128, 4, 512], BF16, 2)
        v3 = lambda x: x[:].rearrange("p (h v) -> p h v", h=4)
        s4box = [None]

        def tile_gen(t):
            of, ofb = ofp.next()
            ob, obb = obp.next()
            rg, rgb = rgp.next()
            loader(t, of, ofb)
            mk.dma("sp", ob[:], dr["OB"][t * 128:(t + 1) * 128, :], writes=[obb])
            mk.dma("sp", rg[:], dr["RG"][t * 128:(t + 1) * 128, :], writes=[rgb])
            mk.op("dve", lambda e: e.tensor_tensor(out=of[:], in0=of[:], in1=ob[:], op=ALU.add), reads=[ofb, obb], writes=[ofb])
            g2, g2b = g2p.next()
            mk.op("pool", lambda e: e.tensor_tensor(out=v3(g2), in0=v3(rg), in1=gain[:].unsqueeze(1).broadcast_to([128, 4, 128]), op=ALU.mult),
                  reads=[rgb, gb], writes=[g2b])
            yield
            sm, smb = smp.next()
            for h in range(4):
                jk, jkb = jkp.next()
                mk.op("act", lambda e, jk=jk, h=h: e.activation(out=jk[:], in_=of[:, h * 128:(h + 1) * 128], func=AF.Square, accum_out=sm[:, h:h + 1]),
                      reads=[ofb], writes=[jkb, smb])
            mk.op("act", lambda e: e.activation(out=sm[:, 4:8], in_=sm[:, 0:4], func=AF.Ln, scale=1.0 / 128, bias=EPS), reads=[smb], writes=[smb])
            mk.op("act", lambda e: e.activation(out=sm[:, 4:8], in_=sm[:, 4:8], func=AF.Exp, scale=-0.5), reads=[smb], writes=[smb])
            yield
            on, onb = onp.next()
            for h in range(4):
                mk.op("dve", lambda e, h=h: e.scalar_tensor_tensor(out=on[:, h * 128:(h + 1) * 128], in0=of[:, h * 128:(h + 1) * 128],
                                                                 scalar=sm[:, 4 + h:5 + h], in1=g2[:, h * 128:(h + 1) * 128], op0=ALU.mult, op1=ALU.mult),
                      reads=[ofb, g2b, smb], writes=[onb])
            yield
            pT, pTb = pTp.next()

            def tr(e):
                for h in range(4):
                    ins = e.transpose(out=pT[:, h, :], in_=on[:, h * 128:(h + 1) * 128], identity=self.ident_b[:])
                return ins
            mk.op("pe", tr, reads=[onb], writes=[pTb])
            yield
            if t % 4 == 0:
                s4box[0] = stT.next()
            s4, s4b = s4box[0]
            mk.op("act", lambda e: e.activation(out=s4[:, :, (t % 4) * 128:(t % 4 + 1) * 128], in_=pT[:], func=AF.Copy), reads=[pTb], writes=[s4b])
            if t % 4 == 3:
                tb = t // 4
                mk.dma("sp", dstT.rearrange("(h p) t -> p h t", p=128)[:, :, tb * 512:(tb + 1) * 512], s4[:], reads=[s4b])
        run_pipeline((tile_gen(t) for t in range(self.NT)), 5)

    def phase_E(self, l):
        mk, dr = self.mk, self.dram
        lam_init = 0.8 - 0.6 * math.exp(-0.3 * l)
        TKmax = self.TS + PAST
        nkcmax = TKmax // 128
        with ExitStack() as st:
            KT = self.sb(st, "e_KT", [128, 4, TKmax], BF16)
            V = self.sb(st, "e_V", [128, nkcmax, 4, 132], BF16)
            npiece = (TKmax + 511) // 512
            KTb = [Buf() for _ in range(npiece)]
            Vb = [Buf() for _ in range(nkcmax)]
            zeros = self.sb(st, "e_zero", [1, 512], BF16)
            gdn = self.sb(st, "e_gdn", [128, 128], F32)
            cb = Buf()
            mk.op("pool", lambda e: e.memset(zeros[:], 0.0), writes=[cb])
            mk.op("pool", lambda e: e.memset(V[:, :, :, 128:132], 1.0), writes=[cb])
            mk.dma("sp", gdn[:], dr["diff_norm"][l].partition_broadcast(128), writes=[cb])
            mk.op("dve", lambda e: e.tensor_scalar(out=gdn[:], in0=gdn[:], scalar1=1.0 - lam_init, scalar2=None, op0=ALU.mult), reads=[cb], writes=[cb])
            mk.barrier()
            QTz = [self.sbpool(st, f"e_QT{m}", [128, 4, 512], BF16, 2) for m in range(2)]
            for m in range(2):
                for tz in QTz[m].tiles:
                    lo = 64 * (1 - m)
                    mk.op("pool", lambda e, tz=tz, lo=lo: e.memset(tz[lo:lo + 64, :, :], 0.0), writes=[cb])
            pS2 = self.pspool(st, "e_pS", [128, 1024], F32, 2)
            acc = [self.ps(st, f"e_acc{i}", [128, 512], F32) for i in range(3)]
            accb = [Buf() for _ in range(3)]
            pTo = self.pspool(st, "e_pTo", [128, 4, 128], BF16, 1)
            ptp = self.sbpool(st, "e_pt", [128, 1024], BF16, 4)
            ckf = self.sbpool(st, "e_ckf", [128, 512], F32, 2)
            ckb = self.sbpool(st, "e_ckb", [128, 512], BF16, 2)
            odp = self.sbpool(st, "e_od", [128, 4, 128], BF16, 8)
            o1p = self.sbpool(st, "e_o1", [128, 128], F32, 5)
            o2p = self.sbpool(st, "e_o2", [128, 128], F32, 5)
            jkp = self.sbpool(st, "e_jk", [128, 128], F32, 4)
            smp = self.sbpool(st, "e_sm", [128, 8], F32, 6)
            st4p = self.sbpool(st, "e_st4", [128, 4, 512], BF16, 1)
            accsp = self.sbpool(st, "e_accs", [128, 3, 512], F32, 2)
            KDTv = dr["KDT"].rearrange("(h p) t -> p h t", p=128)
            QDTv = dr["QDT"].rearrange("(h p) t -> p h t", p=128)
            ODTv = dr["ODT"].rearrange("(h p) t -> p h t", p=128)
            for si, (t0, T) in enumerate(self.seqs):
                TK = T + (PAST if si == 2 else 0)
                nkc = TK // 128
                for pc in range((T + 511) // 512):
                    w = min(512, T - pc * 512)
                    mk.dma("sp", KT[:, :, pc * 512:pc * 512 + w], KDTv[:, :, t0 + pc * 512:t0 + pc * 512 + w], writes=[KTb[pc]])
                    for tl in range(w // 128):
                        r0 = t0 + pc * 512 + tl * 128
                        mk.dma("sp", V[:, pc * 4 + tl, :, 0:128], dr["VD"][r0:r0 + 128, :].rearrange("p (h v) -> p h v", h=4), writes=[Vb[pc * 4 + tl]])
                if si == 2:
                    pcx = T // 512
                    for tl in range(PAST // 128):
                        kf, kfb = ckf.next()
                        kb_, kbb = ckb.next()
                        mk.dma("sp", kf[:], dr["cache_k"][l, tl * 128:(tl + 1) * 128, :], writes=[kfb])
                        mk.op("pool", lambda e, kf=kf, kb_=kb_: e.tensor_copy(out=kb_[:], in_=kf[:]), reads=[kfb], writes=[kbb])
                        pT, pTb = pTo.next()

                        def tr(e, kb_=kb_, pT=pT):
                            for h in range(4):
                                ins = e.transpose(out=pT[:, h, :], in_=kb_[:, h * 128:(h + 1) * 128], identity=self.ident_b[:])
                            return ins
                        mk.op("pe", tr, reads=[kbb], writes=[pTb])
                        mk.op("dve", lambda e, pT=pT, tl=tl: e.tensor_copy(out=KT[:, :, T + tl * 128:T + (tl + 1) * 128], in_=pT[:]), reads=[pTb], writes=[KTb[pcx]])
                        vf, vfb = ckf.next()
                        mk.dma("sp", vf[:], dr["cache_v"][l, tl * 128:(tl + 1) * 128, :], writes=[vfb])
                        mk.op("pool", lambda e, vf=vf, tl=tl: e.tensor_copy(out=V[:, T // 128 + tl, :, 0:128], in_=vf[:].rearrange("p (h v) -> p h v", h=4)),
                              reads=[vfb], writes=[Vb[T // 128 + tl]])
                QW = min(512, T)
                nqs = QW // 128
                nqb = T // QW
                LOOK = 2

                def load_Q(qb):
                    QTm = []
                    for m in range(2):
                        qt_, qtb_ = QTz[m].next()
                        mk.dma("sp", qt_[m * 64:(m + 1) * 64, :, 0:QW], QDTv[m * 64:(m + 1) * 64, :, t0 + qb * QW:t0 + (qb + 1) * QW], writes=[qtb_])
                        QTm.append((qt_, qtb_))
                    return QTm

                def emit_S(QTm, h, kc):
                    ps_, psb = pS2.next()

                    def mmS(e):
                        for m in range(2):
                            ins = e.matmul(ps_[:, m * 512:m * 512 + QW], lhsT=KT[:, h, kc * 128:(kc + 1) * 128], rhs=QTm[m][0][:, h, 0:QW], start=True, stop=True)
                        return ins
                    mk.op("pe", mmS, reads=[KTb[kc // 4], QTm[0][1], QTm[1][1], cb], writes=[psb])
                    pt, ptb = ptp.next()
                    if QW == 512:
                        mk.op("act", lambda e: e.activation(out=pt[:], in_=ps_[:], func=AF.Exp, scale=0.125), reads=[psb], writes=[ptb])
                    else:
                        mk.op("act", lambda e: e.activation(out=pt[:].rearrange("p (m q) -> p m q", m=2)[:, :, 0:QW], in_=ps_[:].rearrange("p (m q) -> p m q", m=2)[:, :, 0:QW],
                                                            func=AF.Exp, scale=0.125), reads=[psb], writes=[ptb])
                    return pt, ptb

                def emit_PV(h, kc, pt, ptb):
                    def pv(e):
                        for m in range(2):
                            for qs in range(nqs):
                                a = m * 4 + qs
                                c0 = (a % 3) * 129
                                ins = e.matmul(acc[a // 3][:, c0:c0 + 129], lhsT=pt[:, m * 512 + qs * 128:m * 512 + (qs + 1) * 128], rhs=V[:, kc, h, 0:129],
                                               start=False, stop=(kc == nkc - 1), skip_group_check=True)
                        return ins
                    mk.op("pe", pv, reads=[ptb, Vb[kc]], writes=accb)

                def finalize(h, ods):
                    acs, acsb = accsp.next()
                    for b_ in range(3):
                        mk.op("dve", lambda e, b_=b_: e.tensor_copy(out=acs[:, b_, :], in_=acc[b_][:]), reads=[accb[b_]], writes=[acsb])

                    def fin(qs):
                        a1i, a2i = qs, 4 + qs
                        A1 = acs[:, a1i // 3, (a1i % 3) * 129:(a1i % 3) * 129 + 129]
                        A2 = acs[:, a2i // 3, (a2i % 3) * 129:(a2i % 3) * 129 + 129]
                        sm, smb = smp.next()
                        mk.op("dve", lambda e: e.reciprocal(out=sm[:, 0:1], in_=A1[:, 128:129]), reads=[acsb], writes=[smb])
                        mk.op("dve", lambda e: e.reciprocal(out=sm[:, 1:2], in_=A2[:, 128:129]), reads=[acsb], writes=[smb])
                        mk.op("dve", lambda e: e.tensor_tensor(out=sm[:, 2:3], in0=sm[:, 1:2], in1=self.lam[:, l, 1:2], op=ALU.mult), reads=[smb, self.cbuf], writes=[smb])
                        o1, o1b = o1p.next()
                        mk.op("dve", lambda e: e.tensor_scalar(out=o1[:], in0=A1[:, 0:128], scalar1=sm[:, 0:1], scalar2=None, op0=ALU.mult), reads=[acsb, smb], writes=[o1b])
                        o2, o2b = o2p.next()
                        mk.op("dve", lambda e: e.scalar_tensor_tensor(out=o2[:], in0=A2[:, 0:128], scalar=sm[:, 2:3], in1=o1[:], op0=ALU.mult, op1=ALU.add),
                              reads=[acsb, smb, o1b], writes=[o2b])
                        jk, jkb = jkp.next()
                        mk.op("dve", lambda e: e.tensor_tensor(out=jk[:], in0=o2[:], in1=o2[:], op=ALU.mult), reads=[o2b], writes=[jkb])
                        mk.op("dve", lambda e: e.reduce_sum(out=sm[:, 3:4], in_=jk[:], axis=AX.X), reads=[jkb], writes=[smb])
                        yield
                        mk.op("act", lambda e: e.activation(out=sm[:, 4:5], in_=sm[:, 3:4], func=AF.Ln, scale=1.0 / 128, bias=EPS), reads=[smb], writes=[smb])
                        mk.op("act", lambda e: e.activation(out=sm[:, 4:5], in_=sm[:, 4:5], func=AF.Exp, scale=-0.5), reads=[smb], writes=[smb])
                        yield
                        od, odb = ods[qs]
                        mk.op("dve", lambda e: e.scalar_tensor_tensor(out=od[:, h, :], in0=o2[:], scalar=sm[:, 4:5], in1=gdn[:], op0=ALU.mult, op1=ALU.mult),
                              reads=[o2b, smb, cb], writes=[odb])
                    run_pipeline((fin(qs) for qs in range(nqs)), 4)

                def qblock_end(qb, ods):
                    s4, s4b = st4p.next()
                    for qs in range(nqs):
                        od, odb = ods[qs]
                        pT, pTb = pTo.next()

                        def tr(e, od=od, pT=pT):
                            for h in range(4):
                                ins = e.transpose(out=pT[:, h, :], in_=od[:, h, :], identity=self.ident_b[:])
                            return ins
                        mk.op("pe", tr, reads=[odb], writes=[pTb])
                        mk.op("dve", lambda e, s4=s4, pT=pT, qs=qs: e.tensor_copy(out=s4[:, :, qs * 128:(qs + 1) * 128], in_=pT[:]), reads=[pTb], writes=[s4b])
                    mk.dma("sp", ODTv[:, :, t0 + qb * QW:t0 + (qb + 1) * QW], s4[:, :, 0:QW], reads=[s4b])

                allsteps = [(qb, h, kc) for qb in range(nqb) for h in range(4) for kc in range(nkc)]
                Q = {}
                odss = {}
                pend = []
                for i in range(len(allsteps) + LOOK):
                    if i < len(allsteps):
                        qb, h, kc = allsteps[i]
                        if h == 0 and kc == 0:
                            if qb == 0:
                                Q[0] = load_Q(0)
                            if qb + 1 < nqb:
                                Q[qb + 1] = load_Q(qb + 1)
                        pend.append(emit_S(Q[qb], h, kc))
                    if i >= LOOK:
                        qb, h, kc = allsteps[i - LOOK]
                        if kc == 0:
                            if h == 0:
                                odss[qb] = [odp.next() for _ in range(nqs)]
                            for b_ in range(3):
                                mk.op("pe", lambda e, b_=b_: e.matmul(acc[b_][:], lhsT=zeros[0:1, 0:128], rhs=zeros[0:1, 0:512], start=True, stop=False, skip_group_check=True),
                                      reads=[cb], writes=[accb[b_]])
                        emit_PV(h, kc, *pend[i - LOOK])
                        pend[i - LOOK] = None
                        if kc == nkc - 1:
                            finalize(h, odss[qb])
                            if h == 3:
                                qblock_end(qb, odss[qb])

    def phase_F1(self, l):
        mk, dr = self.mk, self.dram
        with ExitStack() as st:
            wbr = self.sb(st, "f_wbr", [128, 3, 4, D], BF16)
            wbbs = [Buf(), Buf(), Buf()]
            wst = self.sbpool(st, "f_wst", [128, D], F32, 3)
            for br, nm in enumerate(("w_fou", "w_gla_o", "w_diff_o")):
                for g in range(4):
                    ws, wsb = wst.next()
                    mk.dma("sp", ws[:], dr[nm][l, g * 128:(g + 1) * 128, :], writes=[wsb])
                    ce = ("pool", "dve", "act")[(br * 4 + g) % 3]
                    if ce == "act":
                        mk.op("act", lambda e, ws=ws, br=br, g=g: e.activation(out=wbr[:, br, g, :], in_=ws[:], func=AF.Copy), reads=[wsb], writes=[wbbs[(br * 4 + g) % 3]])
                    else:
                        mk.op(ce, lambda e, ws=ws, br=br, g=g: e.tensor_copy(out=wbr[:, br, g, :], in_=ws[:]), reads=[wsb], writes=[wbbs[(br * 4 + g) % 3]])
            f3p = self.sbpool(st, "f_f3", [128, 3, 4, 512], BF16, 2)
            gtp = self.sbpool(st, "f_gt", [128, 512], BF16, 12)
            pX = self.pspool(st, "f_pX", [128, 512], F32, 6)
            tp = self.sbpool(st, "f_t", [128, 512], BF16, 9)
            mp = self.sbpool(st, "f_m", [128, 512], BF16, 4)
            srcs = [dr[n].rearrange("(g p) t -> p g t", p=128) for n in ("FT", "OGT", "ODT")]
            def load_f3(tb):
                f3, f3b = f3p.next()
                for br in range(3):
                    mk.dma("sp", f3[:, br, :, :], srcs[br][:, :, tb * 512:(tb + 1) * 512], writes=[f3b])
                return f3, f3b
            nxt_f3 = load_f3(0)
            for tb in range(self.NB):
                f3, f3b = nxt_f3
                if tb + 1 < self.NB:
                    nxt_f3 = load_f3(tb + 1)
                gts = {}

                def load_gt(oc):
                    for br in range(3):
                        gt, gtb = gtp.next()
                        mk.dma("sp", gt[:], dr["GT"][(br * 8 + oc) * 128:(br * 8 + oc + 1) * 128, tb * 512:(tb + 1) * 512], writes=[gtb])
                        gts[(oc, br)] = (gt, gtb)
                load_gt(0)
                load_gt(1)
                def oc_gen(oc, tb=tb, f3=f3, f3b=f3b, gts=gts, load_gt=load_gt):
                    if oc + 2 < 8:
                        load_gt(oc + 2)
                    pxs = []
                    for br in range(3):
                        px, pxb = pX.next()

                        def mm(e, px=px, br=br):
                            for g in range(4):
                                ins = e.matmul(px[:], lhsT=wbr[:, br, g, oc * 128:(oc + 1) * 128], rhs=f3[:, br, g, :], start=(g == 0), stop=(g == 3))
                            return ins
                        mk.op("pe", mm, reads=wbbs + [f3b], writes=[pxb])
                        pxs.append((px, pxb))
                    yield
                    ts_ = []
                    for br in range(3):
                        gt, gtb = gts[(oc, br)]
                        px, pxb = pxs[br]
                        t_, tb_ = tp.next()
                        mk.op("dve", lambda e, t_=t_, px=px, gt=gt: e.tensor_tensor(out=t_[:], in0=px[:], in1=gt[:], op=ALU.mult), reads=[pxb, gtb], writes=[tb_])
                        ts_.append((t_, tb_))
                    m_, mb = mp.next()
                    mk.op("dve", lambda e: e.tensor_tensor(out=m_[:], in0=ts_[0][0][:], in1=ts_[1][0][:], op=ALU.add), reads=[ts_[0][1], ts_[1][1]], writes=[mb])
                    mk.op("dve", lambda e: e.tensor_tensor(out=m_[:], in0=m_[:], in1=ts_[2][0][:], op=ALU.add), reads=[ts_[2][1], mb], writes=[mb])
                    mk.dma("sp", dr["MT"][oc * 128:(oc + 1) * 128, tb * 512:(tb + 1) * 512], m_[:], reads=[mb])
                run_pipeline((oc_gen(oc) for oc in range(8)), 2)

    def resid_phase(self, l, pfx, act_name, nk, w_name, g_col, norm_l, norm_which, final):
        mk, dr = self.mk, self.dram
        with ExitStack() as st:
            self.norm_setup(st)
            W = self.sb(st, pfx + "_W", [128, nk, D], BF16)
            Wbs = [Buf(), Buf(), Buf()]
            xp = self.sbpool(st, pfx + "_x", [128, D], F32, 4)
            for j in range(nk):
                ws, wsb = xp.next()
                mk.dma("sp", ws[:], dr[w_name][l, j * 128:(j + 1) * 128, :], writes=[wsb])
                ce = ("pool", "dve", "act")[j % 3]
                if ce == "act":
                    mk.op("act", lambda e, ws=ws, j=j: e.activation(out=W[:, j, :], in_=ws[:], func=AF.Copy), reads=[wsb], writes=[Wbs[j % 3]])
                else:
                    mk.op(ce, lambda e, ws=ws, j=j: e.tensor_copy(out=W[:, j, :], in_=ws[:]), reads=[wsb], writes=[Wbs[j % 3]])
            gb = self.sb(st, pfx + "_g", [128, 2, D], F32)
            gbb = Buf()
            for r in range(2):
                mk.dma("sp", gb[:, r, :], dr["MOD"][l, r, g_col * D:(g_col + 1) * D].partition_broadcast(128), writes=[gbb])
            ap_ = self.sbpool(st, pfx + "_a", [128, nk, 512], BF16, 2)
            po = self.pspool(st, pfx + "_po", [128, 512], F32, 4)
            tp = self.sbpool(st, pfx + "_t", [128, 512], F32, 2)
            src = dr[act_name].rearrange("(j p) t -> p j t", p=128)
            self.filler_setup(st)
            nfill = 8 if nk <= 8 else 4
            blocks = {}

            def load_a(tb):
                a_, ab = ap_.next()
                mk.dma("sp", a_[:], src[:, :, tb * 512:(tb + 1) * 512], writes=[ab])
                blocks[tb] = (a_, ab)
            load_a(0)

            def tile_gen(t):
                tb, tl = t // 4, t % 4
                if tl == 1 and tb + 1 < self.NB:
                    load_a(tb + 1)
                a_, ab = blocks[tb]
                cond = self.cond_of_tile(t)
                xt, xb = xp.next()
                mk.dma("sp", xt[:], dr["X"][t * 128:(t + 1) * 128, :], writes=[xb])
                ps = []
                for half in range(2):
                    p_, pb = po.next()

                    def mm(e, p_=p_, half=half):
                        for j in range(nk):
                            ins = e.matmul(p_[:], lhsT=a_[:, j, tl * 128:(tl + 1) * 128], rhs=W[:, j, half * 512:(half + 1) * 512], start=(j == 0), stop=(j == nk - 1))
                        return ins
                    mk.op("pe", mm, reads=[ab] + Wbs, writes=[pb])
                    ps.append((p_, pb))
                self.filler(nfill)
                yield
                for half in range(2):
                    p_, pb = ps[half]
                    t_, tb_ = tp.next()
                    mk.op("dve", lambda e, t_=t_, p_=p_, half=half: e.tensor_tensor(out=t_[:], in0=p_[:], in1=gb[:, cond, half * 512:(half + 1) * 512], op=ALU.mult),
                          reads=[pb, gbb], writes=[tb_])
                    mk.op("dve", lambda e, t_=t_, half=half: e.tensor_tensor(out=xt[:, half * 512:(half + 1) * 512], in0=xt[:, half * 512:(half + 1) * 512], in1=t_[:], op=ALU.add),
                          reads=[tb_, xb], writes=[xb])
                if final:
                    if t < 4:
                        mk.dma("sp", dr["yp"][t * 128:(t + 1) * 128, :], xt[:], reads=[xb])
                    else:
                        mk.dma("sp", dr["ys"][(t - 4) * 128:(t - 3) * 128, :], xt[:], reads=[xb])
                else:
                    mk.dma("sp", dr["X"][t * 128:(t + 1) * 128, :], xt[:], reads=[xb])
                    yield from self.norm_gen(xt[:], xb, t, norm_l, norm_which)
            run_pipeline((tile_gen(t) for t in range(self.NT)), 5)

    def phase_F2(self, l):
        self.resid_phase(l, "f2", "MT", 8, "w_out", 2, l, 2, False)

    def phase_I(self, l):
        last = (l == self.L - 1)
        self.resid_phase(l, "i", "AT", NFF, "w_ff_down", 5, l + 1, 1, last)

    def phase_H(self, l):
        mk, dr = self.mk, self.dram
        hT, hTb, NB = self.hT, self.hT_b, self.NB
        with ExitStack() as st:
            wf = self.sbpool(st, "h_wf", [128, NKC, 256], F32, 3)
            wg = self.sbpool(st, "h_wg", [128, NKC, 512], BF16, 2)
            wu = self.sbpool(st, "h_wu", [128, NKC, 512], BF16, 2)
            pg = self.pspool(st, "h_pg", [128, 512], F32, 3)
            pu = self.pspool(st, "h_pu", [128, 512], F32, 3)
            sgp = self.sbpool(st, "h_sg", [128, 512], F32, 3)
            stp = self.sbpool(st, "h_st", [128, 512], BF16, 4)

            def load(pool, wname, c0, W):
                wt, wbuf = pool.next()
                for h0 in range(0, W, 256):
                    ft, fb = wf.next()
                    mk.dma("sp", ft[:], dr[wname][l][:, c0 + h0:c0 + h0 + 256].rearrange("(kc p) n -> p kc n", p=128), writes=[fb])
                    mk.op("pool", lambda e, ft=ft, h0=h0, wt=wt: e.tensor_copy(out=wt[:, :, h0:h0 + 256], in_=ft[:]), reads=[fb], writes=[wbuf])
                return wt, wbuf
            units = [(c0, min(512, DFF - c0)) for c0 in range(0, DFF, 512)]
            nxt = (load(wg, "w_ff_gate", *units[0]), load(wu, "w_ff_up", *units[0]))
            for ui, (c0, W) in enumerate(units):
                (wgt, wgb), (wut, wub) = nxt
                if ui + 1 < len(units):
                    nxt = (load(wg, "w_ff_gate", *units[ui + 1]), load(wu, "w_ff_up", *units[ui + 1]))
                for cc in range(W // 128):
                    for tb in range(NB):
                        g_, gb_ = pg.next()
                        u_, ub_ = pu.next()

                        def mm(e, g_=g_, u_=u_, cc=cc, tb=tb, wgt=wgt, wut=wut):
                            for kc in range(NKC):
                                e.matmul(g_[:], lhsT=wgt[:, kc, cc * 128:(cc + 1) * 128], rhs=hT[:, kc, tb * 512:(tb + 1) * 512], start=(kc == 0), stop=(kc == NKC - 1))
                            for kc in range(NKC):
                                ins = e.matmul(u_[:], lhsT=wut[:, kc, cc * 128:(cc + 1) * 128], rhs=hT[:, kc, tb * 512:(tb + 1) * 512], start=(kc == 0), stop=(kc == NKC - 1))
                            return ins
                        mk.op("pe", mm, reads=[wgb, wub] + hTb[tb * 4:(tb + 1) * 4], writes=[gb_, ub_])
                        sg, sgb = sgp.next()
                        mk.op("act", lambda e, sg=sg, g_=g_: e.activation(out=sg[:], in_=g_[:], func=AF.Silu), reads=[gb_], writes=[sgb])
                        s_, sb_ = stp.next()
                        mk.op("dve", lambda e, s_=s_, sg=sg, u_=u_: e.tensor_tensor(out=s_[:], in0=u_[:], in1=sg[:], op=ALU.mult), reads=[ub_, sgb], writes=[sb_])
                        mk.dma("sp", dr["AT"][c0 + cc * 128:c0 + (cc + 1) * 128, tb * 512:(tb + 1) * 512], s_[:], reads=[sb_])


def _bf(a):
    return np.ascontiguousarray(a.astype(ml_dtypes.bfloat16))


def host_consts(TS):
    k = {}
    k["k_ident"] = np.eye(128, dtype=np.float32)
    c = np.arange(128)
    ang = 2 * np.pi * np.outer(c, c) / 128.0
    k["k_csc"] = (np.concatenate([np.cos(ang), np.sin(ang)], axis=1) / np.sqrt(128.0)).astype(np.float32)
    for nm, T in (("P", SPL), ("S", TS)):
        t = np.arange(T, dtype=np.int64)
        ang = 2 * np.pi * ((np.outer(t, t) % T).astype(np.float64)) / T
        k["k_ct" + nm] = _bf(np.cos(ang) / np.sqrt(T))
        k["k_nst" + nm] = _bf(-np.sin(ang) / np.sqrt(T))
    j = np.arange(64)[:, None]
    i = np.arange(64)[None, :]
    tri = np.zeros((64, 6, 64), np.float32)
    tri[:, 0, :] = (j <= i) / 16.0
    tri[:, 1, :] = (j >= i) / 16.0
    tri[:, 2, :] = (j > i) / 16.0
    tri[:, 3, :] = (j < i) / 16.0
    tri[:, 4, :] = (j <= i)
    tri[:, 5, :] = (j >= i)
    k["k_tri"] = tri
    rows = TS // 64
    row = np.repeat(np.arange(rows), 64).astype(np.float32)
    col = np.tile(np.arange(64), rows).astype(np.float32)
    inv = (10000.0 ** (-np.arange(16, dtype=np.float32) * 2.0 / 32)).astype(np.float32)
    ar = row[:, None] * inv
    ac = col[:, None] * inv
    angm = np.concatenate([ar, ar, ac, ac], axis=-1)
    cos = np.cos(angm).astype(np.float32)
    sin = np.sin(angm).astype(np.float32)
    sgn = np.tile(np.concatenate([-np.ones(16), np.ones(16)]), 2).astype(np.float32)
    k["k_rope"] = np.ascontiguousarray(np.stack([cos, sin * sgn], axis=1)).astype(np.float32)
    return k


WNAMES = ["w_mod", "b_mod", "norm1", "norm2", "w_in", "w_gla_a2", "b_gla_a", "gla_norm", "diff_qk_norm", "diff_lambda",
          "diff_norm", "w_fou", "w_gla_o", "w_diff_o", "w_out", "w_ff_gate", "w_ff_up", "w_ff_down"]


def make_in_map(inp, core, L, TS, consts):
    f = lambda a: np.ascontiguousarray(np.asarray(a, dtype=np.float32))
    m = {}
    m["xp"] = f(inp["x_prompt"][2 * core:2 * core + 2]).reshape(2 * SPL, D)
    m["xs"] = f(inp["x_sample"][core]).reshape(TS, D)
    m["cvec"] = f(np.stack([np.asarray(inp["c_ctx"]), np.asarray(inp["c"][core])], axis=0))
    m["cache_k"] = f(inp["cache_diff_k"][core]).reshape(L, PAST, 512)
    m["cache_v"] = f(inp["cache_diff_v"][core]).reshape(L, PAST, 512)
    m["state"] = f(inp["state_gla"][core])
    for n in WNAMES:
        m[n] = f(inp[n])
    m.update(consts)
    return m


_CACHE = {}


def kernel(**inputs):
    L, TS, NCORE = 4, 4096, 8
    inp = {k: np.asarray(v) for k, v in inputs.items()}
    if "nc" not in _CACHE:
        _CACHE["nc"] = Builder(L=L, TS=TS).build()
        _CACHE["consts"] = host_consts(TS)
    nc = _CACHE["nc"]
    in_maps = [make_in_map(inp, c, L, TS, _CACHE["consts"]) for c in range(NCORE)]
    res = run_bass_kernel_spmd(nc, in_maps, core_ids=list(range(NCORE)))
    R = res.results
    f = lambda n: np.stack([np.asarray(R[c][n], dtype=np.float32) for c in range(NCORE)], axis=0)
    yp = f("yp").reshape(NCORE * 2, SPL, D)
    ys = f("ys").reshape(NCORE, TS, D)
    nk = f("nk").reshape(NCORE * 2, L, SPL, 4, 2, 64)
    nv = f("nv").reshape(NCORE * 2, L, SPL, 4, 128)
    ns = f("ns").reshape(NCORE * 2, L, 2, 4, 64, 128)
    return (yp, ys, nk, nv, ns)
```

```python
import math
from contextlib import ExitStack
import numpy as np
import ml_dtypes
import concourse.bass as bass
import concourse.mybir as mybir
from concourse.bass_utils import run_bass_kernel_spmd

F32 = mybir.dt.float32
BF16 = mybir.dt.bfloat16
AF = mybir.ActivationFunctionType
ALU = mybir.AluOpType
AX = mybir.AxisListType

D = 1024
NKC = 8
INC = 6688
DFF = 2816
NFF = 22
SPL = 256
PAST = 256
EPS = 1e-6
C_UF, C_QG, C_KG, C_VG, C_RG, C_AG, C_QD, C_KD, C_VD, C_GT = 0, 512, 768, 1024, 1536, 2048, 2080, 2592, 3104, 3616


class Ev:
    __slots__ = ("sem", "sid", "val", "owner", "kind")

    def __init__(self, sem, sid, val, owner, kind):
        self.sem, self.sid, self.val, self.owner, self.kind = sem, sid, val, owner, kind


class Buf:
    __slots__ = ("w", "r")

    def __init__(self):
        self.w = None
        self.r = {}


class MK:
    CE = ("pe", "act", "dve", "pool")
    KD = 6

    def __init__(self, nc):
        self.nc = nc
        self.E = dict(pe=nc.tensor, act=nc.scalar, dve=nc.vector, pool=nc.gpsimd, sp=nc.sync)
        self.nsem = 0
        self.csem = {}
        self.cnt = {}
        for e in self.CE:
            self.csem[e] = self._newsem("c_" + e)
            self.cnt[e] = 0
        self.waited = {e: {} for e in self.E}
        self.dq = {}
        for q in ("sp", "pool", "act"):
            self.dq[q] = dict(sems=[self._newsem("d_" + q) for _ in range(self.KD)], vals=[0] * self.KD, n=0)
        self.same_sync = True
        self.nins = 0

    def _newsem(self, name):
        self.nsem += 1
        return (self.nc.alloc_semaphore(name=f"{name}_{self.nsem}"), self.nsem)

    def _wait(self, e, ev):
        w = self.waited[e]
        if w.get(ev.sid, 0) >= ev.val:
            return
        self.E[e].wait_ge(ev.sem, ev.val)
        w[ev.sid] = ev.val

    def _deps(self, reads, writes):
        evs = []
        for b in reads:
            if b.w is not None:
                evs.append(b.w)
        for b in writes:
            if b.w is not None:
                evs.append(b.w)
            evs.extend(b.r.values())
        return evs

    def _record(self, ev, reads, writes):
        for b in reads:
            b.r[ev.sid] = ev
        for b in writes:
            b.w = ev
            b.r = {}

    def op(self, e, fn, reads=(), writes=()):
        for ev in self._deps(reads, writes):
            if ev.kind == "c" and ev.owner == e and (e == "pe" or not self.same_sync):
                continue
            self._wait(e, ev)
        ins = fn(self.E[e])
        self.cnt[e] += 1
        self.nins += 1
        sem, sid = self.csem[e]
        ins.then_inc(sem, 1)
        ev = Ev(sem, sid, self.cnt[e], e, "c")
        self._record(ev, reads, writes)
        return ev

    def dma(self, q, out, in_, reads=(), writes=(), **kw):
        for ev in self._deps(reads, writes):
            self._wait(q, ev)
        d = self.dq[q]
        slot = d["n"] % self.KD
        d["n"] += 1
        sem, sid = d["sems"][slot]
        if d["vals"][slot] > 0:
            self._wait(q, Ev(sem, sid, d["vals"][slot], q, "d"))
        ins = self.E[q].dma_start(out=out, in_=in_, **kw)
        d["vals"][slot] += 16
        ins.then_inc(sem, 16)
        self.nins += 1
        ev = Ev(sem, sid, d["vals"][slot], q, "d")
        self._record(ev, reads, writes)
        return ev

    def barrier(self):
        evs = []
        for e in self.CE:
            if self.cnt[e] > 0:
                sem, sid = self.csem[e]
                evs.append(Ev(sem, sid, self.cnt[e], e, "c"))
        for q, d in self.dq.items():
            for (sem, sid), v in zip(d["sems"], d["vals"]):
                if v > 0:
                    evs.append(Ev(sem, sid, v, q, "d"))
        for e in self.E:
            for ev in evs:
                self._wait(e, ev)
        for e in self.CE:
            if self.cnt[e] > 12000:
                self.csem[e] = self._newsem("c_" + e)
                self.cnt[e] = 0
        for q, d in self.dq.items():
            for i in range(self.KD):
                if d["vals"][i] > 12000:
                    d["sems"][i] = self._newsem("d_" + q)
                    d["vals"][i] = 0


class Pool_:
    def __init__(self, tiles):
        self.tiles = tiles
        self.bufs = [Buf() for _ in tiles]
        self.i = 0

    def next(self):
        k = self.i % len(self.tiles)
        self.i += 1
        return self.tiles[k], self.bufs[k]


def run_pipeline(gens, depth):
    active = []
    it = iter(gens)
    done = False
    while True:
        if not done and len(active) < depth:
            try:
                active.append(next(it))
            except StopIteration:
                done = True
        if not active:
            break
        nxt = []
        for g in active:
            try:
                next(g)
                nxt.append(g)
            except StopIteration:
                pass
        active = nxt


class Builder:
    def __init__(self, L=4, TS=4096, stop_after=None, debug=()):
        self.L, self.TS = L, TS
        self.NTOK = 2 * SPL + TS
        self.NT = self.NTOK // 128
        self.NB = self.NTOK // 512
        self.seqs = [(0, SPL), (SPL, SPL), (2 * SPL, TS)]
        self.stop_after = stop_after
        self.debug = set(debug)
        self.nc = bass.Bass("TRN2", target_bir_lowering=False)
        self.mk = MK(self.nc)
        self.dram = {}

    def din(self, name, shape, dt=F32):
        self.dram[name] = self.nc.dram_tensor(name, list(shape), dt, kind="ExternalInput").ap()
        return self.dram[name]

    def dout(self, name, shape, dt=F32):
        self.dram[name] = self.nc.dram_tensor(name, list(shape), dt, kind="ExternalOutput").ap()
        return self.dram[name]

    def dscr(self, name, shape, dt):
        kind = "ExternalOutput" if name in self.debug else "Internal"
        self.dram[name] = self.nc.dram_tensor(name, list(shape), dt, kind=kind).ap()
        return self.dram[name]

    def dbg(self, name, ap, reads):
        if "dbg_" + name in self.debug:
            o = self.nc.dram_tensor("dbg_" + name, list(ap.shape), ap.dtype, kind="ExternalOutput").ap()
            self.mk.dma("sp", o, ap, reads=reads)

    def sb(self, st, name, shape, dt):
        self.uid = getattr(self, "uid", 0) + 1
        return st.enter_context(self.nc.sbuf_tensor(f"{name}_u{self.uid}", list(shape), dt))

    def ps(self, st, name, shape, dt=F32):
        self.uid = getattr(self, "uid", 0) + 1
        return st.enter_context(self.nc.psum_tensor(f"{name}_u{self.uid}", list(shape), dt))

    def sbpool(self, st, name, shape, dt, n):
        return Pool_([self.sb(st, f"{name}{i}", shape, dt) for i in range(n)])

    def pspool(self, st, name, shape, dt, n):
        return Pool_([self.ps(st, f"{name}{i}", shape, dt) for i in range(n)])

    def filler_setup(self, st):
        self.fz = self.sb(st, "fill_z", [128, 512], BF16)
        self.fp = self.ps(st, "fill_p", [128, 512], F32)
        self.fb = Buf()
        self.mk.op("pool", lambda e: e.memset(self.fz[:], 0.0), writes=[self.fb])

    def filler(self, n):
        def mm(e):
            for _ in range(n):
                ins = e.matmul(self.fp[:], lhsT=self.fz[:, 0:128], rhs=self.fz[:], start=True, stop=True)
            return ins
        self.mk.op("pe", mm, reads=[self.fb], writes=[])

    def cond_of_tile(self, t):
        return 0 if t < (2 * SPL) // 128 else 1

    def declare(self):
        L, TS, NTOK = self.L, self.TS, self.NTOK
        di = self.din
        di("xp", [2 * SPL, D]); di("xs", [TS, D]); di("cvec", [2, D])
        di("cache_k", [L, PAST, 512]); di("cache_v", [L, PAST, 512]); di("state", [L, 2, 4, 64, 128])
        di("w_mod", [L, D, 6 * D]); di("b_mod", [L, 6 * D]); di("norm1", [L, D]); di("norm2", [L, D])
        di("w_in", [L, D, INC]); di("w_gla_a2", [L, 2, 16, 256]); di("b_gla_a", [L, 2, 256])
        di("gla_norm", [L, 128]); di("diff_qk_norm", [L, 2, 64]); di("diff_lambda", [L, 4, 64]); di("diff_norm", [L, 128])
        di("w_fou", [L, 512, D]); di("w_gla_o", [L, 512, D]); di("w_diff_o", [L, 512, D]); di("w_out", [L, D, D])
        di("w_ff_gate", [L, D, DFF]); di("w_ff_up", [L, D, DFF]); di("w_ff_down", [L, DFF, D])
        di("k_ident", [128, 128]); di("k_csc", [128, 256])
        di("k_ctP", [SPL, SPL], BF16); di("k_nstP", [SPL, SPL], BF16)
        di("k_ctS", [TS, TS], BF16); di("k_nstS", [TS, TS], BF16)
        di("k_tri", [64, 6, 64])
        di("k_rope", [TS, 2, 64])
        do = self.dout
        do("yp", [2 * SPL, D]); do("ys", [TS, D])
        do("nk", [2, L, SPL, 512]); do("nv", [2, L, SPL, 512]); do("ns", [2, L, 2, 4, 64, 128])
        ds = self.dscr
        ds("X", [NTOK, D], F32); ds("MOD", [L, 2, 6 * D], F32)
        ds("UFT", [512, NTOK], BF16); ds("QGT", [256, NTOK], BF16); ds("KGT", [256, NTOK], BF16)
        ds("KG", [NTOK, 256], BF16); ds("VG", [NTOK, 512], BF16); ds("RG", [NTOK, 512], BF16)
        ds("LG", [NTOK, 512], F32); ds("QDT", [512, NTOK], BF16); ds("KDT", [512, NTOK], BF16)
        ds("VD", [NTOK, 512], BF16); ds("GT", [3072, NTOK], BF16)
        ds("FT", [512, NTOK], BF16); ds("OGT", [512, NTOK], BF16); ds("ODT", [512, NTOK], BF16)
        ds("OF", [NTOK, 512], F32); ds("OB", [NTOK, 512], F32)
        ds("MT", [D, NTOK], BF16); ds("AT", [DFF, NTOK], BF16)

    def build(self):
        self.declare()
        nc, mk, dr = self.nc, self.mk, self.dram
        L = self.L
        with ExitStack() as st:
            self.hT = self.sb(st, "hT", [128, NKC, self.NTOK], BF16)
            self.hT_b = [Buf() for _ in range(self.NT)]
            self.ident_f = self.sb(st, "ident_f", [128, 128], F32)
            self.ident_b = self.sb(st, "ident_b", [128, 128], BF16)
            self.AB = self.sb(st, "AB", [128, L, 2, 4, 8], F32)
            self.lam = self.sb(st, "lam", [128, L, 2], F32)
            self.cbuf = Buf()
            mk.dma("sp", self.ident_f[:], dr["k_ident"], writes=[self.cbuf])
            mk.op("dve", lambda e: e.tensor_copy(out=self.ident_b[:], in_=self.ident_f[:]), reads=[self.cbuf], writes=[self.cbuf])
            for r0 in range(0, self.NTOK, 512):
                src = dr["xp"][r0:r0 + 512, :] if r0 < 2 * SPL else dr["xs"][r0 - 2 * SPL:r0 - 2 * SPL + 512, :]
                mk.dma("sp", dr["X"][r0:r0 + 512, :], src)
            self.prologue()
            mk.barrier()
            if self.stop_after == "prologue":
                return self.finish()
            self.phase_A(0)
            for l in range(L):
                for ph in (self.phase_B, self.phase_C, self.phase_D, self.phase_E, self.phase_F1, self.phase_F2,
                           self.phase_H, self.phase_I):
                    mk.barrier()
                    ph(l)
                    if self.stop_after == (ph.__name__[6:], l):
                        return self.finish()
            return self.finish()

    def finish(self):
        self.mk.barrier()
        return self.nc

    def prologue(self):
        nc, mk, dr, L = self.nc, self.mk, self.dram, self.L
        with ExitStack() as st:
            c16 = self.sb(st, "c16", [16, 128], F32)
            sT = self.sb(st, "sT", [128, 16], F32)
            bm = self.sb(st, "bm", [2, 6 * D], F32)
            mrow = self.sb(st, "mrow", [2, 6 * D], F32)
            wm = self.sbpool(st, "wm", [128, NKC, 512], F32, 3)
            pm = self.pspool(st, "pm", [128, 512], F32, 2)
            pt = self.ps(st, "pt", [128, 128], F32)
            VR = self.sb(st, "VR", [112, 128], F32)
            VC = self.sb(st, "VC", [128, 112], F32)
            dl = self.sb(st, "dl", [128, 4, 64], F32)
            pr = self.sb(st, "pr", [128, 2, 64], F32)
            sm = self.sb(st, "sm", [128, 2], F32)
            b_c16, b_sT, b_bm, b_mrow, b_pt, b_VR, b_VC, b_dl = (Buf() for _ in range(8))
            b_VRs = (Buf(), Buf(), Buf())
            mk.dma("sp", c16[:], dr["cvec"].rearrange("r (kc p) -> (r kc) p", p=128), writes=[b_c16])
            mk.op("pe", lambda e: e.transpose(out=pt[:, 0:16], in_=c16[:], identity=self.ident_f[0:16, 0:16]),
                  reads=[b_c16, self.cbuf], writes=[b_pt])
            mk.op("act", lambda e: e.activation(out=sT[:], in_=pt[:, 0:16], func=AF.Silu), reads=[b_pt], writes=[b_sT])
            sT2 = self.sb(st, "sT2", [128, NKC, 2], F32)
            mk.op("dve", lambda e: e.tensor_copy(out=sT2[:], in_=sT[:].rearrange("p (r kc) -> p kc r", r=2)), reads=[b_sT], writes=[b_sT])
            sTv = sT2[:]
            self.dbg("sT", sT[:], [b_sT])
            for l in range(L):
                mk.dma("sp", bm[:], dr["b_mod"][l].partition_broadcast(2), writes=[b_bm])
                for cc in range(12):
                    wt, wb_ = wm.next()
                    mk.dma("sp", wt[:], dr["w_mod"][l][:, cc * 512:(cc + 1) * 512].rearrange("(kc p) n -> p kc n", p=128),
                           writes=[wb_])
                    pmt, pmb = pm.next()

                    def mm(e, wt=wt, pmt=pmt):
                        for kc in range(NKC):
                            ins = e.matmul(pmt[0:2, :], lhsT=sTv[:, kc, :], rhs=wt[:, kc, :], start=(kc == 0), stop=(kc == NKC - 1))
                        return ins
                    mk.op("pe", mm, reads=[wb_, b_sT], writes=[pmb])
                    mk.op("dve", lambda e, pmt=pmt, cc=cc: e.tensor_tensor(out=mrow[:, cc * 512:(cc + 1) * 512], in0=pmt[0:2, :],
                                                                          in1=bm[:, cc * 512:(cc + 1) * 512], op=ALU.add),
                          reads=[pmb, b_bm], writes=[b_mrow])
                b_MOD = Buf()
                mk.dma("sp", dr["MOD"][l], mrow[:], reads=[b_mrow], writes=[b_MOD])
                b_V0, b_V1, b_V2 = b_VRs
                mk.dma("sp", VR[0:96, :], dr["MOD"][l].rearrange("r (j p) -> (r j) p", p=128), reads=[b_MOD], writes=[b_V0])
                mk.dma("sp", VR[96:104, :], dr["norm1"][l].rearrange("(j p) -> j p", p=128), writes=[b_V1])
                mk.dma("sp", VR[104:112, :], dr["norm2"][l].rearrange("(j p) -> j p", p=128), writes=[b_V2])
                mk.op("pe", lambda e: e.transpose(out=pt[:, 0:112], in_=VR[:], identity=self.ident_f[0:112, 0:112]),
                      reads=[b_V0, b_V1, b_V2], writes=[b_pt])
                mk.op("dve", lambda e: e.tensor_copy(out=VC[:], in_=pt[:, 0:112]), reads=[b_pt], writes=[b_VC])
                for r in range(2):
                    c0 = r * 48
                    mk.op("dve", lambda e, r=r, c0=c0: e.scalar_tensor_tensor(out=self.AB[:, l, r, 0, :], in0=VC[:, c0 + 8:c0 + 16], scalar=1.0,
                                                                            in1=VC[:, 96:104], op0=ALU.add, op1=ALU.mult),
                          reads=[b_VC], writes=[self.cbuf])
                    mk.op("dve", lambda e, r=r, c0=c0: e.tensor_copy(out=self.AB[:, l, r, 1, :], in_=VC[:, c0:c0 + 8]), reads=[b_VC], writes=[self.cbuf])
                    mk.op("dve", lambda e, r=r, c0=c0: e.scalar_tensor_tensor(out=self.AB[:, l, r, 2, :], in0=VC[:, c0 + 32:c0 + 40], scalar=1.0,
                                                                            in1=VC[:, 104:112], op0=ALU.add, op1=ALU.mult),
                          reads=[b_VC], writes=[self.cbuf])
                    mk.op("dve", lambda e, r=r, c0=c0: e.tensor_copy(out=self.AB[:, l, r, 3, :], in_=VC[:, c0 + 24:c0 + 32]), reads=[b_VC], writes=[self.cbuf])
                mk.dma("sp", dl[:], dr["diff_lambda"][l].rearrange("a d -> (a d)").partition_broadcast(128), writes=[b_dl])
                dlv = dl[:].rearrange("p (a b) d -> p a b d", b=2)
                mk.op("dve", lambda e: e.tensor_tensor(out=pr[:], in0=dlv[:, :, 0, :], in1=dlv[:, :, 1, :], op=ALU.mult), reads=[b_dl], writes=[b_dl])
                mk.op("dve", lambda e: e.reduce_sum(out=sm[:], in_=pr[:], axis=AX.X), reads=[b_dl], writes=[b_dl])
                mk.op("act", lambda e: e.activation(out=sm[:], in_=sm[:], func=AF.Exp), reads=[b_dl], writes=[b_dl])
                lam_init = 0.8 - 0.6 * math.exp(-0.3 * l)
                mk.op("dve", lambda e, l=l, li=lam_init: e.tensor_scalar(out=self.lam[:, l, 0:1], in0=sm[:, 0:1], scalar1=sm[:, 1:2], scalar2=li,
                                                                       op0=ALU.subtract, op1=ALU.add), reads=[b_dl], writes=[self.cbuf])
                mk.op("dve", lambda e, l=l: e.tensor_scalar(out=self.lam[:, l, 1:2], in0=self.lam[:, l, 0:1], scalar1=-1.0, scalar2=None, op0=ALU.mult),
                      reads=[self.cbuf], writes=[self.cbuf])

    def norm_setup(self, st):
        self.n_xn = self.sbpool(st, "n_xn", [128, D], BF16, 3)
        self.n_ss = self.sbpool(st, "n_ss", [128, 2], F32, 4)
        self.n_pT = self.pspool(st, "n_pT", [128, NKC, 128], BF16, 2)

    def norm_gen(self, xt, xb, tile, l, which):
        mk = self.mk
        cond = self.cond_of_tile(tile)
        xn, xnb = self.n_xn.next()
        ss, ssb = self.n_ss.next()
        mk.op("act", lambda e: e.activation(out=xn[:], in_=xt, func=AF.Square, accum_out=ss[:, 0:1]), reads=[xb], writes=[xnb, ssb])
        mk.op("act", lambda e: e.activation(out=ss[:, 1:2], in_=ss[:, 0:1], func=AF.Ln, scale=1.0 / D, bias=EPS), reads=[ssb], writes=[ssb])
        mk.op("act", lambda e: e.activation(out=ss[:, 1:2], in_=ss[:, 1:2], func=AF.Exp, scale=-0.5), reads=[ssb], writes=[ssb])
        mk.op("act", lambda e: e.activation(out=xn[:], in_=xt, func=AF.Copy, scale=ss[:, 1:2]), reads=[xb, ssb], writes=[xnb])
        yield
        pT, pTb = self.n_pT.next()

        def tr(e):
            for kc in range(NKC):
                ins = e.transpose(out=pT[:, kc, :], in_=xn[:, kc * 128:(kc + 1) * 128], identity=self.ident_b[:])
            return ins
        mk.op("pe", tr, reads=[xnb, self.cbuf], writes=[pTb])
        yield
        hb = self.hT_b[tile]
        a_i, b_i = (0, 1) if which == 1 else (2, 3)
        for kc in range(NKC):
            dst = self.hT[:, kc, tile * 128:(tile + 1) * 128]
            A = self.AB[:, l, cond, a_i, kc:kc + 1]
            B = self.AB[:, l, cond, b_i, kc:kc + 1]
            if kc % 2 == 0:
                mk.op("dve", lambda e, dst=dst, A=A, B=B, kc=kc: e.tensor_scalar(out=dst, in0=pT[:, kc, :], scalar1=A, scalar2=B, op0=ALU.mult, op1=ALU.add),
                      reads=[pTb, self.cbuf], writes=[hb])
            else:
                mk.op("act", lambda e, dst=dst, A=A, B=B, kc=kc: e.activation(out=dst, in_=pT[:, kc, :], func=AF.Identity, scale=A, bias=B),
                      reads=[pTb, self.cbuf], writes=[hb])

    def phase_A(self, l):
        mk, dr = self.mk, self.dram
        with ExitStack() as st:
            self.norm_setup(st)
            xp = self.sbpool(st, "a_x", [128, D], F32, 4)

            def tile_gen(t):
                xt, xb = xp.next()
                mk.dma("sp", xt[:], dr["X"][t * 128:(t + 1) * 128, :], writes=[xb])
                yield from self.norm_gen(xt[:], xb, t, l, 1)
            run_pipeline((tile_gen(t) for t in range(self.NT)), 4)

    def phase_B(self, l):
        mk, dr, nc = self.mk, self.dram, self.nc
        NT, NB, NTOK, TS = self.NT, self.NB, self.NTOK, self.TS
        hT, hTb = self.hT, self.hT_b
        NPT = (2 * SPL) // 128
        with ExitStack() as st:
            wf = self.sbpool(st, "b_wf", [128, NKC, 256], F32, 2)
            wb = self.sbpool(st, "b_wb", [128, NKC, 512], BF16, 4)
            pacc = self.pspool(st, "b_pa", [128, 512], F32, 4)
            pTp = self.pspool(st, "b_pT", [128, 4, 128], BF16, 2)
            stF = self.sbpool(st, "b_sF", [128, 512], BF16, 4)
            stT = self.sbpool(st, "b_sT", [128, 4, 512], BF16, 4)
            tmp = self.sbpool(st, "b_tmp", [128, 512], F32, 3)
            rawp = self.sbpool(st, "b_raw", [128, 512], F32, 3)
            sqp = self.sbpool(st, "b_sq", [128, 512], F32, 2)
            up = self.sbpool(st, "b_u", [128, 512], F32, 3)
            wp = self.sbpool(st, "b_w", [128, 512], F32, 2)
            tbf = self.sbpool(st, "b_tbf", [128, 512], BF16, 4)
            small = self.sbpool(st, "b_sm", [128, 16], F32, 6)
            aT = self.sb(st, "b_aT", [33, NTOK], BF16)
            aTb = Buf()
            BDf = self.sb(st, "b_BDf", [33, 512], F32)
            BD = self.sb(st, "b_BD", [33, 512], BF16)
            gqk = self.sb(st, "b_gqk", [128, 2, 64], F32)
            rope = self.sb(st, "b_rope", [128, TS // 128, 2, 64], F32)
            gsw = self.sb(st, "b_gsw", [128, 64], F32)
            tabb = Buf()
            cb = Buf()
            mk.dma("sp", gqk[:], dr["diff_qk_norm"][l].rearrange("a d -> (a d)").partition_broadcast(128), writes=[cb])
            bdb = Buf()
            mk.op("pool", lambda e: e.memset(BDf[:], 0.0), writes=[bdb])
            mk.dma("sp", BDf[0:16, 0:256], dr["w_gla_a2"][l, 0], writes=[bdb])
            mk.dma("sp", BDf[16:32, 256:512], dr["w_gla_a2"][l, 1], writes=[bdb])
            mk.dma("sp", BDf[32:33, :], dr["b_gla_a"][l:l + 1].rearrange("o a d -> o (a d)"), writes=[bdb])
            mk.barrier()
            mk.op("pool", lambda e: e.tensor_copy(out=BD[:], in_=BDf[:]), reads=[bdb], writes=[bdb])
            mk.op("pool", lambda e: e.memset(aT[32:33, :], 1.0), writes=[aTb])

            w_in = dr["w_in"][l]
            self.filler_setup(st)

            def load_unit(c0, W):
                wt, wbuf = wb.next()
                for h0 in range(0, W, 256):
                    ww = min(256, W - h0)
                    ft, fb = wf.next()
                    mk.dma("sp", ft[:, :, 0:ww], w_in[:, c0 + h0:c0 + h0 + ww].rearrange("(kc p) n -> p kc n", p=128), writes=[fb])
                    mk.op("pool", lambda e, ft=ft, h0=h0, ww=ww: e.tensor_copy(out=wt[:, :, h0:h0 + ww], in_=ft[:, :, 0:ww]),
                          reads=[fb], writes=[wbuf])
                return wt, wbuf

            def mm_F(wt, wbuf, cc, tb, M=128):
                pt, pb = pacc.next()

                def mm(e):
                    for kc in range(NKC):
                        ins = e.matmul(pt[0:M, :], lhsT=wt[:, kc, cc * 128:cc * 128 + M], rhs=hT[:, kc, tb * 512:(tb + 1) * 512],
                                       start=(kc == 0), stop=(kc == NKC - 1))
                    return ins
                mk.op("pe", mm, reads=[wbuf] + hTb[tb * 4:(tb + 1) * 4], writes=[pb])
                return pt, pb

            def mm_T(wt, wbuf, t, c_lo, W):
                pt, pb = pacc.next()

                def mm(e):
                    for kc in range(NKC):
                        ins = e.matmul(pt[:, 0:W], lhsT=hT[:, kc, t * 128:(t + 1) * 128], rhs=wt[:, kc, c_lo:c_lo + W],
                                       start=(kc == 0), stop=(kc == NKC - 1))
                    return ins
                mk.op("pe", mm, reads=[wbuf, hTb[t]], writes=[pb])
                return pt, pb

            flip = [0]

            def evac(dst, src, reads, writes, func=None, scale=1.0, eng=None):
                if func is None and scale == 1.0 and eng is None:
                    flip[0] ^= 1
                    eng = "dve" if flip[0] else "act"
                if func is None and scale == 1.0 and eng == "dve":
                    mk.op("dve", lambda e: e.tensor_copy(out=dst, in_=src), reads=reads, writes=writes)
                elif func is None and eng == "dve":
                    mk.op("dve", lambda e: e.tensor_scalar(out=dst, in0=src, scalar1=scale, scalar2=None, op0=ALU.mult), reads=reads, writes=writes)
                else:
                    f = AF.Copy if func is None else func
                    mk.op("act", lambda e: e.activation(out=dst, in_=src, func=f, scale=scale), reads=reads, writes=writes)

            def do_F(wt, wbuf, ncc, dst, func=None, scale=1.0, eng=None, cc0=0):
                for cc in range(ncc):
                    for tb in range(NB):
                        pt, pb = mm_F(wt, wbuf, cc0 + cc, tb)
                        s, sb_ = stF.next()
                        evac(s[:], pt[:], [pb], [sb_], func, scale, eng)
                        mk.dma("sp", dst[cc * 128:(cc + 1) * 128, tb * 512:(tb + 1) * 512], s[:], reads=[sb_])

            def T_plain_gens(wt, wbuf, c_lo, W, dst, func=None, f32_out=None, eng=None):
                box = [None]

                def tile(t):
                    pt, pb = mm_T(wt, wbuf, t, c_lo, W)
                    yield
                    if t % 4 == 0:
                        box[0] = stT.next()
                    s4, s4b = box[0]
                    fo = f32_out(t) if f32_out is not None else None
                    if fo is not None:
                        tt, ttb = tmp.next()
                        evac(tt[:, 0:W], pt[:, 0:W], [pb], [ttb], eng="act")
                        mk.dma("sp", fo, tt[:, 0:W], reads=[ttb])
                        mk.op("pool", lambda e: e.tensor_copy(out=s4[:, t % 4, 0:W], in_=tt[:, 0:W]), reads=[ttb], writes=[s4b])
                    else:
                        evac(s4[:, t % 4, 0:W], pt[:, 0:W], [pb], [s4b], func, eng=eng)
                    if t % 4 == 3:
                        tb = t // 4
                        mk.dma("sp", dst[tb * 512:(tb + 1) * 512, :].rearrange("(t p) c -> p t c", p=128), s4[:, :, 0:W], reads=[s4b])
                return [tile(t) for t in range(NT)]

            def do_T_plain(wt, wbuf, c_lo, W, dst, func=None, f32_out=None):
                run_pipeline(T_plain_gens(wt, wbuf, c_lo, W, dst, func, f32_out), 2)

            def do_qk(wt, wbuf, j, dstT, co=None):
                gain = gqk[:, j, :].unsqueeze(1).broadcast_to([128, 8, 64])
                nts = TS // 128
                gv = gqk[:, j, :].rearrange("p (a h f) -> p a h f", a=2, h=2)
                swv = gsw[:].rearrange("p (a h f) -> p a h f", a=2, h=2)
                mk.op("dve", lambda e: e.tensor_copy(out=swv[:, :, 0, :], in_=gv[:, :, 1, :]), reads=[cb], writes=[tabb])
                mk.op("dve", lambda e: e.tensor_copy(out=swv[:, :, 1, :], in_=gv[:, :, 0, :]), reads=[cb], writes=[tabb])
                mk.dma("sp", rope[:], dr["k_rope"].rearrange("(t p) a d -> p t a d", p=128), writes=[tabb])
                cg = rope[:, :, 0, :]
                sg = rope[:, :, 1, :]
                mk.op("dve", lambda e: e.tensor_tensor(out=cg, in0=cg, in1=gqk[:, j, :].unsqueeze(1).broadcast_to([128, nts, 64]), op=ALU.mult),
                      reads=[cb], writes=[tabb])
                mk.op("dve", lambda e: e.tensor_tensor(out=sg, in0=sg, in1=gsw[:].unsqueeze(1).broadcast_to([128, nts, 64]), op=ALU.mult),
                      reads=[cb, tabb], writes=[tabb])
                v3 = lambda x: x[:].rearrange("p (g d) -> p g d", g=8)
                v4 = lambda x: x[:].rearrange("p (g a x) -> p g a x", g=8, a=2)
                s4box = [None]

                def qk_tile(t):
                    pt, pb = mm_T(wt, wbuf, t, 0, 512)
                    if co is None:
                        self.filler(6)
                    yield
                    raw, rawb = rawp.next()
                    mk.op("act", lambda e: e.activation(out=raw[:], in_=pt[:], func=AF.Copy), reads=[pb], writes=[rawb])
                    sq, sqb = sqp.next()
                    mk.op("act", lambda e: e.activation(out=sq[:], in_=pt[:], func=AF.Square), reads=[pb], writes=[sqb])
                    sm, smb = small.next()
                    mk.op("dve", lambda e: e.reduce_sum(out=sm[:, 0:8], in_=v3(sq), axis=AX.X), reads=[sqb], writes=[smb])
                    yield
                    mk.op("act", lambda e: e.activation(out=sm[:, 8:16], in_=sm[:, 0:8], func=AF.Ln, scale=1.0 / 64, bias=EPS), reads=[smb], writes=[smb])
                    mk.op("act", lambda e: e.activation(out=sm[:, 8:16], in_=sm[:, 8:16], func=AF.Exp, scale=-0.5), reads=[smb], writes=[smb])
                    rstd = sm[:, 8:16].unsqueeze(2).broadcast_to([128, 8, 64])
                    qr, qrb = tbf.next()
                    if t < NPT:
                        qn, qnb = up.next()
                    else:
                        ti = t - NPT
                        cosv = cg[:, ti, :].unsqueeze(1).broadcast_to([128, 8, 64])
                        sinv = sg[:, ti, :].rearrange("p (a x) -> p a x", a=2).unsqueeze(1).broadcast_to([128, 8, 2, 32])
                        u, ub = up.next()
                        mk.op("dve", lambda e: e.tensor_tensor(out=v3(u), in0=v3(raw), in1=cosv, op=ALU.mult), reads=[rawb, tabb], writes=[ub])
                        w_, wb_ = wp.next()
                        mk.op("dve", lambda e: e.tensor_tensor(out=v4(w_)[:, :, :, 0:16], in0=v4(raw)[:, :, :, 16:32], in1=sinv[:, :, :, 0:16], op=ALU.mult),
                              reads=[rawb, tabb], writes=[wb_])
                        mk.op("dve", lambda e: e.tensor_tensor(out=v4(w_)[:, :, :, 16:32], in0=v4(raw)[:, :, :, 0:16], in1=sinv[:, :, :, 16:32], op=ALU.mult),
                              reads=[rawb, tabb], writes=[wb_])
                        mk.op("pool", lambda e: e.tensor_tensor(out=u[:], in0=u[:], in1=w_[:], op=ALU.add), reads=[ub, wb_], writes=[ub])
                    yield
                    if t < NPT:
                        mk.op("dve", lambda e: e.tensor_tensor(out=v3(qn), in0=v3(raw), in1=rstd, op=ALU.mult), reads=[rawb, smb], writes=[qnb])
                        mk.op("pool", lambda e: e.tensor_tensor(out=v3(qn), in0=v3(qn), in1=gain, op=ALU.mult), reads=[qnb, cb], writes=[qnb])
                        if j == 1:
                            sq_i, tt_i = t // 2, t % 2
                            mk.dma("sp", dr["nk"][sq_i, l, tt_i * 128:(tt_i + 1) * 128, :], qn[:], reads=[qnb])
                        mk.op("act", lambda e: e.activation(out=qr[:], in_=qn[:], func=AF.Copy), reads=[qnb], writes=[qrb])
                    else:
                        mk.op("dve", lambda e: e.tensor_tensor(out=v3(qr), in0=v3(u), in1=rstd, op=ALU.mult), reads=[ub, smb], writes=[qrb])
                    pT, pTb = pTp.next()

                    def tr(e):
                        for h in range(4):
                            ins = e.transpose(out=pT[:, h, :], in_=qr[:, h * 128:(h + 1) * 128], identity=self.ident_b[:])
                        return ins
                    mk.op("pe", tr, reads=[qrb], writes=[pTb])
                    yield
                    if t % 4 == 0:
                        s4box[0] = stT.next()
                    s4, s4b = s4box[0]
                    evac(s4[:, :, (t % 4) * 128:(t % 4 + 1) * 128], pT[:], [pTb], [s4b], eng="act")
                    if t % 4 == 3:
                        tb = t // 4
                        mk.dma("sp", dstT.rearrange("(h p) t -> p h t", p=128)[:, :, tb * 512:(tb + 1) * 512], s4[:], reads=[s4b])
                gens = [qk_tile(t) for t in range(NT)]
                if co is not None:
                    mixed = []
                    for g1, g2 in zip(gens, co):
                        mixed += [g1, g2]
                    run_pipeline(mixed, 8)
                else:
                    run_pipeline(gens, 4)

            def do_ag(wt, wbuf):
                for tb in range(NB):
                    pt, pb = mm_F(wt, wbuf, 0, tb, M=32)
                    evac(aT[0:32, tb * 512:(tb + 1) * 512], pt[0:32, :], [pb], [aTb])
                for t in range(NT):
                    pt, pb = pacc.next()
                    mk.op("pe", lambda e, pt=pt, t=t: e.matmul(pt[:], lhsT=aT[0:33, t * 128:(t + 1) * 128], rhs=BD[0:33, :], start=True, stop=True),
                          reads=[aTb, bdb], writes=[pb])
                    e1, e1b = tmp.next()
                    mk.op("act", lambda e, e1=e1, pt=pt: e.activation(out=e1[:], in_=pt[:], func=AF.Exp, scale=-1.0), reads=[pb], writes=[e1b])
                    mk.op("act", lambda e, e1=e1: e.activation(out=e1[:], in_=e1[:], func=AF.Ln, bias=1.0), reads=[e1b], writes=[e1b])
                    mk.dma("sp", dr["LG"][t * 128:(t + 1) * 128, :], e1[:], reads=[e1b])

            def nv_out(t):
                if t >= NPT:
                    return None
                return dr["nv"][t // 2, l, (t % 2) * 128:(t % 2 + 1) * 128, :]

            U = {"uf": (C_UF, 512), "qk": (C_QG, 512), "vg": (C_VG, 512), "rg": (C_RG, 512), "ag": (C_AG, 32),
                 "qd": (C_QD, 512), "kd": (C_KD, 512), "vd": (C_VD, 512)}
            for i in range(6):
                U[f"g{i}"] = (C_GT + 512 * i, 512)
            steps = [["uf"], ["qk"], ["rg"], ["ag"], ["qd", "vg"], ["kd", "vd"]] + [[f"g{i}"] for i in range(6)]
            load_step = lambda names: [load_unit(*U[n]) for n in names]
            nxt = load_step(steps[0])
            for si_, names in enumerate(steps):
                cur = nxt
                if si_ + 1 < len(steps):
                    nxt = load_step(steps[si_ + 1])
                wt, wbuf = cur[0]
                name = names[0]
                if name == "uf":
                    do_F(wt, wbuf, 4, dr["UFT"], eng="dve")
                elif name == "qk":
                    do_F(wt, wbuf, 2, dr["QGT"], scale=0.125, eng="dve")
                    do_F(wt, wbuf, 2, dr["KGT"], eng="dve", cc0=2)
                    do_T_plain(wt, wbuf, 256, 256, dr["KG"])
                elif name == "rg":
                    do_T_plain(wt, wbuf, 0, 512, dr["RG"], func=AF.Silu)
                elif name == "ag":
                    do_ag(wt, wbuf)
                elif name == "qd":
                    do_qk(wt, wbuf, 0, dr["QDT"], co=T_plain_gens(cur[1][0], cur[1][1], 0, 512, dr["VG"], eng="act"))
                elif name == "kd":
                    do_qk(wt, wbuf, 1, dr["KDT"], co=T_plain_gens(cur[1][0], cur[1][1], 0, 512, dr["VD"], f32_out=nv_out))
                else:
                    gi = int(name[1:])
                    do_F(wt, wbuf, 4, dr["GT"][gi * 512:(gi + 1) * 512, :], func=AF.Sigmoid)


    def phase_C(self, l):
        mk, dr = self.mk, self.dram
        with ExitStack() as st:
            cscf = self.sb(st, "c_cscf", [128, 256], F32)
            csc = self.sb(st, "c_csc", [128, 256], BF16)
            cb = Buf()
            mk.dma("sp", cscf[:], dr["k_csc"], writes=[cb])
            mk.op("dve", lambda e: e.tensor_copy(out=csc[:], in_=cscf[:]), reads=[cb], writes=[cb])
            ntmax = max(T for _, T in self.seqs) // 128
            PQ = self.sb(st, "c_PQ", [128, ntmax, 1024], BF16)
            PQb = [Buf() for _ in range(ntmax)]
            uTp = self.sbpool(st, "c_uT", [128, 4, 512], BF16, 2)
            ppq = self.pspool(st, "c_ppq", [128, 1024], F32, 1)
            pf = [self.ps(st, f"c_pf{g}", [128, 512], F32) for g in range(4)]
            pfb = [Buf() for _ in range(4)]
            ctp = self.sbpool(st, "c_ct", [128, 4, 512], BF16, 4)
            nstp = self.sbpool(st, "c_nst", [128, 4, 512], BF16, 4)
            fst = self.sbpool(st, "c_fst", [128, 4, 512], BF16, 2)
            UFTv = dr["UFT"].rearrange("(g p) t -> p g t", p=128)
            FTv = dr["FT"].rearrange("(g p) t -> p g t", p=128)
            for si, (t0, T) in enumerate(self.seqs):
                nt = T // 128
                ctD, nstD = (dr["k_ctP"], dr["k_nstP"]) if T == SPL else (dr["k_ctS"], dr["k_nstS"])
                PW = min(512, T)
                for pc in range(T // PW):
                    uT, uTb = uTp.next()
                    mk.dma("sp", uT[:, :, 0:PW], UFTv[:, :, t0 + pc * PW:t0 + (pc + 1) * PW], writes=[uTb])
                    for tl in range(PW // 128):
                        tile = pc * (PW // 128) + tl
                        pq, pqb = ppq.next()

                        def mm(e, uT=uT, tl=tl, pq=pq):
                            for g in range(4):
                                ins = e.matmul(pq[:, g * 256:(g + 1) * 256], lhsT=uT[:, g, tl * 128:(tl + 1) * 128], rhs=csc[:], start=True, stop=True)
                            return ins
                        mk.op("pe", mm, reads=[uTb, cb], writes=[pqb])
                        mk.op("act", lambda e, pq=pq, tile=tile: e.activation(out=PQ[:, tile, 0:512], in_=pq[:, 0:512], func=AF.Copy), reads=[pqb], writes=[PQb[tile]])
                        mk.op("dve", lambda e, pq=pq, tile=tile: e.tensor_copy(out=PQ[:, tile, 512:1024], in_=pq[:, 512:1024]), reads=[pqb], writes=[PQb[tile]])
                NP = min(512, T)
                TG = min(4, nt)
                for pb in range(T // NP):
                    for tg in range(nt // TG):
                        ct, ctb = ctp.next()
                        nst, nstb = nstp.next()
                        r0 = tg * TG * 128
                        mk.dma("sp", ct[:, 0:TG, 0:NP], ctD[r0:r0 + TG * 128, pb * NP:(pb + 1) * NP].rearrange("(tc p) n -> p tc n", p=128), writes=[ctb])
                        mk.dma("sp", nst[:, 0:TG, 0:NP], nstD[r0:r0 + TG * 128, pb * NP:(pb + 1) * NP].rearrange("(tc p) n -> p tc n", p=128), writes=[nstb])
                        for g in range(4):
                            def mm(e, g=g, tg=tg, ct=ct, nst=nst):
                                for tc in range(TG):
                                    tile = tg * TG + tc
                                    first = (tg == 0 and tc == 0)
                                    last = (tg == nt // TG - 1 and tc == TG - 1)
                                    e.matmul(pf[g][:, 0:NP], lhsT=PQ[:, tile, g * 256:g * 256 + 128], rhs=ct[:, tc, 0:NP], start=first, stop=False)
                                    ins = e.matmul(pf[g][:, 0:NP], lhsT=PQ[:, tile, g * 256 + 128:g * 256 + 256], rhs=nst[:, tc, 0:NP], start=False, stop=last)
                                return ins
                            mk.op("pe", mm, reads=[ctb, nstb] + PQb[tg * TG:(tg + 1) * TG], writes=[pfb[g]])
                    fs, fsb = fst.next()
                    for g in range(4):
                        if g % 2 == 0:
                            mk.op("act", lambda e, g=g, fs=fs: e.activation(out=fs[:, g, 0:NP], in_=pf[g][:, 0:NP], func=AF.Copy), reads=[pfb[g]], writes=[fsb])
                        else:
                            mk.op("dve", lambda e, g=g, fs=fs: e.tensor_copy(out=fs[:, g, 0:NP], in_=pf[g][:, 0:NP]), reads=[pfb[g]], writes=[fsb])
                    mk.dma("sp", FTv[:, :, t0 + pb * NP:t0 + (pb + 1) * NP], fs[:, :, 0:NP], reads=[fsb])

    def phase_D(self, l):
        mk, dr = self.mk, self.dram
        with ExitStack() as st:
            tri = self.sb(st, "d_tri", [64, 6, 64], F32)
            cb = Buf()
            mk.dma("sp", tri[:], dr["k_tri"], writes=[cb])
            S32 = [self.sb(st, f"d_S32_{d}", [64, 4, 128], F32) for d in range(2)]
            Sb = [self.sb(st, f"d_Sb_{d}", [64, 4, 128], BF16) for d in range(2)]
            S32b = [Buf(), Buf()]
            Sbb = [Buf(), Buf()]
            Tmax = max(T for _, T in self.seqs)
            qTa = self.sb(st, "d_qT", [64, 4, Tmax], BF16)
            kTa = self.sb(st, "d_kT", [64, 4, Tmax], BF16)
            qkb = Buf()
            Lp = self.sbpool(st, "d_L", [64, 256], F32, 3)
            kp = self.sbpool(st, "d_k", [64, 256], BF16, 4)
            vp = self.sbpool(st, "d_v", [64, 512], BF16, 5)
            eqp = self.sbpool(st, "d_eq", [64, 4, 64], F32, 5)
            ekp = self.sbpool(st, "d_ek", [64, 4, 64], F32, 3)
            qsp = self.sbpool(st, "d_qs", [64, 4, 64], BF16, 4)
            ksp = self.sbpool(st, "d_ks", [64, 4, 64], BF16, 3)
            edp = self.sbpool(st, "d_ed", [64, 256], F32, 3)
            kdp = self.sbpool(st, "d_kd", [64, 256], BF16, 4)
            atp = self.sbpool(st, "d_at", [64, 4, 64], BF16, 3)
            osp = self.sbpool(st, "d_os", [64, 512], F32, 3)
            pA = self.pspool(st, "d_pA", [128, 512], F32, 2)
            pB = self.pspool(st, "d_pB", [128, 512], F32, 2)
            pS = self.pspool(st, "d_pS", [128, 512], F32, 2)
            pO = self.pspool(st, "d_pO", [128, 512], F32, 2)
            QGTv = dr["QGT"].rearrange("(h k) t -> k h t", k=64)
            KGTv = dr["KGT"].rearrange("(h k) t -> k h t", k=64)

            def chunk(t0, c, d):
                tok0 = t0 + c * 64
                Lc, Lb = Lp.next()
                kc_, kb = kp.next()
                vc, vb = vp.next()
                mk.dma("sp", Lc[:], dr["LG"][tok0:tok0 + 64, d * 256:(d + 1) * 256], writes=[Lb])
                mk.dma("sp", kc_[:], dr["KG"][tok0:tok0 + 64, :], writes=[kb])
                mk.dma("sp", vc[:], dr["VG"][tok0:tok0 + 64, :], writes=[vb])
                a, ab = pA.next()
                pc = a[0:64, 0:256].rearrange("p (h i) -> p h i", h=4)
                psuf = a[0:64, 256:512]

                def mm0(e):
                    for h in range(4):
                        e.matmul(pc[:, h, :], lhsT=Lc[:, h * 64:(h + 1) * 64], rhs=tri[:, d, :], start=True, stop=True)
                    return e.matmul(psuf, lhsT=tri[:, 2 + d, :], rhs=Lc[:], start=True, stop=True)
                mk.op("pe", mm0, reads=[Lb, cb], writes=[ab])
                yield
                eq, eqb = eqp.next()
                ek, ekb = ekp.next()
                qs, qsb = qsp.next()
                ks, ksb = ksp.next()
                ed, edb = edp.next()
                kd, kdb = kdp.next()
                mk.op("act", lambda e: e.activation(out=eq[:], in_=pc, func=AF.Exp, scale=-1.0), reads=[ab], writes=[eqb])
                mk.op("act", lambda e: e.activation(out=ek[:], in_=pc, func=AF.Exp), reads=[ab], writes=[ekb])
                mk.op("act", lambda e: e.activation(out=ed[:], in_=psuf, func=AF.Exp, scale=-1.0), reads=[ab], writes=[edb])
                mk.op("dve", lambda e: e.tensor_tensor(out=qs[:], in0=qTa[:, :, c * 64:(c + 1) * 64], in1=eq[:], op=ALU.mult), reads=[eqb, qkb], writes=[qsb])
                mk.op("pool", lambda e: e.tensor_tensor(out=ks[:], in0=kTa[:, :, c * 64:(c + 1) * 64], in1=ek[:], op=ALU.mult), reads=[ekb, qkb], writes=[ksb])
                mk.op("dve", lambda e: e.tensor_tensor(out=kd[:], in0=kc_[:], in1=ed[:], op=ALU.mult), reads=[edb, kb], writes=[kdb])
                b_, bb = pB.next()
                pa = b_[0:64, 0:256].rearrange("p (h i) -> p h i", h=4)

                def mm1(e):
                    for h in range(4):
                        ins = e.matmul(pa[:, h, :], lhsT=ks[:, h, :], rhs=qs[:, h, :], start=True, stop=True)
                    return ins
                mk.op("pe", mm1, reads=[ksb, qsb], writes=[bb])
                yield
                at, atb = atp.next()
                mk.op("dve", lambda e: e.tensor_tensor(out=at[:], in0=pa, in1=tri[:, 4 + d, :].unsqueeze(1).broadcast_to([64, 4, 64]), op=ALU.mult),
                      reads=[bb, cb], writes=[atb])
                o_, ob = pO.next()
                s_, sb_ = pS.next()
                ps_ = s_[0:64, :].rearrange("p (h v) -> p h v", h=4)

                def mm2(e):
                    for h in range(4):
                        e.matmul(o_[0:64, h * 128:(h + 1) * 128], lhsT=at[:, h, :], rhs=vc[:, h * 128:(h + 1) * 128], start=True, stop=False)
                        e.matmul(o_[0:64, h * 128:(h + 1) * 128], lhsT=qs[:, h, :], rhs=Sb[d][:, h, :], start=False, stop=True)
                    for h in range(4):
                        ins = e.matmul(ps_[:, h, :], lhsT=kd[:, h * 64:(h + 1) * 64], rhs=vc[:, h * 128:(h + 1) * 128], start=True, stop=True)
                    return ins
                mk.op("pe", mm2, reads=[atb, vb, qsb, Sbb[d], kdb], writes=[ob, sb_])
                yield
                os_, osb = osp.next()
                mk.op("act", lambda e: e.activation(out=os_[:], in_=o_[0:64, :], func=AF.Copy), reads=[ob], writes=[osb])
                mk.dma("sp", dr["OF" if d == 0 else "OB"][tok0:tok0 + 64, :], os_[:], reads=[osb])
                col = 63 if d == 0 else 0
                for h in range(4):
                    mk.op("dve", lambda e, h=h: e.scalar_tensor_tensor(out=S32[d][:, h, :], in0=S32[d][:, h, :], scalar=eq[:, h, col:col + 1], in1=ps_[:, h, :],
                                                                     op0=ALU.mult, op1=ALU.add), reads=[eqb, sb_, S32b[d]], writes=[S32b[d]])
                mk.op("act", lambda e: e.activation(out=Sb[d][:], in_=S32[d][:], func=AF.Copy), reads=[S32b[d]], writes=[Sbb[d]])

            for si, (t0, T) in enumerate(self.seqs):
                nch = T // 64
                mk.dma("sp", qTa[:, :, 0:T], QGTv[:, :, t0:t0 + T], writes=[qkb])
                mk.dma("sp", kTa[:, :, 0:T], KGTv[:, :, t0:t0 + T], writes=[qkb])
                for d in range(2):
                    if si < 2:
                        mk.op("pool", lambda e, d=d: e.memset(S32[d][:], 0.0), writes=[S32b[d]])
                    else:
                        mk.dma("sp", S32[d][:], dr["state"][l, d].rearrange("h k v -> k h v"), writes=[S32b[d]])
                    mk.op("act", lambda e, d=d: e.activation(out=Sb[d][:], in_=S32[d][:], func=AF.Copy), reads=[S32b[d]], writes=[Sbb[d]])
                gens = []
                for c in range(nch):
                    gens.append(chunk(t0, c, 0))
                    gens.append(chunk(t0, nch - 1 - c, 1))
                run_pipeline(gens, 4)
                if si < 2:
                    for d in range(2):
                        mk.dma("sp", dr["ns"][si, l, d].rearrange("h k v -> k h v"), S32[d][:], reads=[S32b[d]])
        mk.barrier()
        with ExitStack() as st:
            cb = Buf()
            gn = self.sb(st, "d_gn", [128, 128], F32)
            mk.dma("sp", gn[:], dr["gla_norm"][l].partition_broadcast(128), writes=[cb])
            obp = self.sbpool(st, "d_ob", [128, 512], F32, 3)
            rgp = self.sbpool(st, "d_rg", [128, 512], BF16, 3)
            self.post_norm_T(st, "d", dr["OGT"], gn, cb, 1.0,
                             loader=lambda t, o, ob_: (mk.dma("sp", o[:], dr["OF"][t * 128:(t + 1) * 128, :], writes=[ob_])),
                             extra=(obp, rgp, dr))

    def post_norm_T(self, st, pfx, dstT, gain, gb, gscale, loader, extra):
        mk = self.mk
        obp, rgp, dr = extra
        ofp = self.sbpool(st, pfx + "_pof", [128, 512], F32, 4)
        g2p = self.sbpool(st, pfx + "_pg2", [128, 512], F32, 4)
        jkp = self.sbpool(st, pfx + "_pjk", [128, 128], BF16, 2)
        smp = self.sbpool(st, pfx + "_psm", [128, 8], F32, 4)
        onp = self.sbpool(st, pfx + "_pon", [128, 512], BF16, 3)
        pTp = self.pspool(st, pfx + "_ppT", [128, 4, 128], BF16, 2)
        stT = self.sbpool(st, pfx + "_pst", [128, 4, 512], BF16, 2)
        v3 = lambda x: x[:].rearrange("p (h v) -> p h v", h=4)
        s4box = [None]

        def tile_gen(t):
            of, ofb = ofp.next()
            ob, obb = obp.next()
            rg, rgb = rgp.next()
            loader(t, of, ofb)
            mk.dma("sp", ob[:], dr["OB"][t * 128:(t + 1) * 128, :], writes=[obb])
            mk.dma("sp", rg[:], dr["RG"][t * 128:(t + 1) * 128, :], writes=[rgb])
            mk.op("dve", lambda e: e.tensor_tensor(out=of[:], in0=of[:], in1=ob[:], op=ALU.add), reads=[ofb, obb], writes=[ofb])
            g2, g2b = g2p.next()
            mk.op("pool", lambda e: e.tensor_tensor(out=v3(g2), in0=v3(rg), in1=gain[:].unsqueeze(1).broadcast_to([128, 4, 128]), op=ALU.mult),
                  reads=[rgb, gb], writes=[g2b])
            yield
            sm, smb = smp.next()
            for h in range(4):
                jk, jkb = jkp.next()
                mk.op("act", lambda e, jk=jk, h=h: e.activation(out=jk[:], in_=of[:, h * 128:(h + 1) * 128], func=AF.Square, accum_out=sm[:, h:h + 1]),
                      reads=[ofb], writes=[jkb, smb])
            mk.op("act", lambda e: e.activation(out=sm[:, 4:8], in_=sm[:, 0:4], func=AF.Ln, scale=1.0 / 128, bias=EPS), reads=[smb], writes=[smb])
            mk.op("act", lambda e: e.activation(out=sm[:, 4:8], in_=sm[:, 4:8], func=AF.Exp, scale=-0.5), reads=[smb], writes=[smb])
            yield
            on, onb = onp.next()
            for h in range(4):
                mk.op("dve", lambda e, h=h: e.scalar_tensor_tensor(out=on[:, h * 128:(h + 1) * 128], in0=of[:, h * 128:(h + 1) * 128],
                                                                 scalar=sm[:, 4 + h:5 + h], in1=g2[:, h * 128:(h + 1) * 128], op0=ALU.mult, op1=ALU.mult),
                      reads=[ofb, g2b, smb], writes=[onb])
            yield
            pT, pTb = pTp.next()

            def tr(e):
                for h in range(4):
                    ins = e.transpose(out=pT[:, h, :], in_=on[:, h * 128:(h + 1) * 128], identity=self.ident_b[:])
                return ins
            mk.op("pe", tr, reads=[onb], writes=[pTb])
            yield
            if t % 4 == 0:
                s4box[0] = stT.next()
            s4, s4b = s4box[0]
            mk.op("act", lambda e: e.activation(out=s4[:, :, (t % 4) * 128:(t % 4 + 1) * 128], in_=pT[:], func=AF.Copy), reads=[pTb], writes=[s4b])
            if t % 4 == 3:
                tb = t // 4
                mk.dma("sp", dstT.rearrange("(h p) t -> p h t", p=128)[:, :, tb * 512:(tb + 1) * 512], s4[:], reads=[s4b])
        run_pipeline((tile_gen(t) for t in range(self.NT)), 5)

    def phase_E(self, l):
        mk, dr = self.mk, self.dram
        lam_init = 0.8 - 0.6 * math.exp(-0.3 * l)
        TKmax = self.TS + PAST
        nkcmax = TKmax // 128
        with ExitStack() as st:
            KT = self.sb(st, "e_KT", [128, 4, TKmax], BF16)
            V = self.sb(st, "e_V", [128, nkcmax, 4, 132], BF16)
            npiece = (TKmax + 511) // 512
            KTb = [Buf() for _ in range(npiece)]
            Vb = [Buf() for _ in range(nkcmax)]
            zeros = self.sb(st, "e_zero", [1, 512], BF16)
            gdn = self.sb(st, "e_gdn", [128, 128], F32)
            cb = Buf()
            mk.op("pool", lambda e: e.memset(zeros[:], 0.0), writes=[cb])
            negh = self.sb(st, "e_negh", [128, 1], F32)
            mk.op("pool", lambda e: e.memset(negh[:], -0.5), writes=[cb])
            mk.op("pool", lambda e: e.memset(V[:, :, :, 128:132], 1.0), writes=[cb])
            mk.dma("sp", gdn[:], dr["diff_norm"][l].partition_broadcast(128), writes=[cb])
            mk.op("dve", lambda e: e.tensor_scalar(out=gdn[:], in0=gdn[:], scalar1=1.0 - lam_init, scalar2=None, op0=ALU.mult), reads=[cb], writes=[cb])
            mk.barrier()
            QTz = [self.sbpool(st, f"e_QT{m}", [128, 4, 512], BF16, 2) for m in range(2)]
            for m in range(2):
                for tz in QTz[m].tiles:
                    lo = 64 * (1 - m)
                    mk.op("pool", lambda e, tz=tz, lo=lo: e.memset(tz[lo:lo + 64, :, :], 0.0), writes=[cb])
            pS2 = self.pspool(st, "e_pS", [128, 1024], F32, 2)
            acc = [self.ps(st, f"e_acc{i}", [128, 512], F32) for i in range(3)]
            accb = [Buf() for _ in range(3)]
            pTo = self.pspool(st, "e_pTo", [128, 4, 128], BF16, 1)
            ptp = self.sbpool(st, "e_pt", [128, 1024], BF16, 4)
            ckf = self.sbpool(st, "e_ckf", [128, 512], F32, 2)
            ckb = self.sbpool(st, "e_ckb", [128, 512], BF16, 2)
            odp = self.sbpool(st, "e_od", [128, 4, 128], BF16, 8)
            o1p = self.sbpool(st, "e_o1", [128, 128], F32, 5)
            o2p = self.sbpool(st, "e_o2", [128, 128], F32, 5)
            jkp = self.sbpool(st, "e_jk", [128, 128], F32, 4)
            smp = self.sbpool(st, "e_sm", [128, 8], F32, 6)
            st4p = self.sbpool(st, "e_st4", [128, 4, 512], BF16, 1)
            accsp = self.sbpool(st, "e_accs", [128, 3, 512], F32, 2)
            KDTv = dr["KDT"].rearrange("(h p) t -> p h t", p=128)
            QDTv = dr["QDT"].rearrange("(h p) t -> p h t", p=128)
            ODTv = dr["ODT"].rearrange("(h p) t -> p h t", p=128)
            for si, (t0, T) in enumerate(self.seqs):
                TK = T + (PAST if si == 2 else 0)
                nkc = TK // 128
                for pc in range((T + 511) // 512):
                    w = min(512, T - pc * 512)
                    mk.dma("sp", KT[:, :, pc * 512:pc * 512 + w], KDTv[:, :, t0 + pc * 512:t0 + pc * 512 + w], writes=[KTb[pc]])
                    for tl in range(w // 128):
                        r0 = t0 + pc * 512 + tl * 128
                        mk.dma("sp", V[:, pc * 4 + tl, :, 0:128], dr["VD"][r0:r0 + 128, :].rearrange("p (h v) -> p h v", h=4), writes=[Vb[pc * 4 + tl]])
                if si == 2:
                    pcx = T // 512
                    for tl in range(PAST // 128):
                        kf, kfb = ckf.next()
                        kb_, kbb = ckb.next()
                        mk.dma("sp", kf[:], dr["cache_k"][l, tl * 128:(tl + 1) * 128, :], writes=[kfb])
                        mk.op("pool", lambda e, kf=kf, kb_=kb_: e.tensor_copy(out=kb_[:], in_=kf[:]), reads=[kfb], writes=[kbb])
                        pT, pTb = pTo.next()

                        def tr(e, kb_=kb_, pT=pT):
                            for h in range(4):
                                ins = e.transpose(out=pT[:, h, :], in_=kb_[:, h * 128:(h + 1) * 128], identity=self.ident_b[:])
                            return ins
                        mk.op("pe", tr, reads=[kbb], writes=[pTb])
                        mk.op("dve", lambda e, pT=pT, tl=tl: e.tensor_copy(out=KT[:, :, T + tl * 128:T + (tl + 1) * 128], in_=pT[:]), reads=[pTb], writes=[KTb[pcx]])
                        vf, vfb = ckf.next()
                        mk.dma("sp", vf[:], dr["cache_v"][l, tl * 128:(tl + 1) * 128, :], writes=[vfb])
                        mk.op("pool", lambda e, vf=vf, tl=tl: e.tensor_copy(out=V[:, T // 128 + tl, :, 0:128], in_=vf[:].rearrange("p (h v) -> p h v", h=4)),
                              reads=[vfb], writes=[Vb[T // 128 + tl]])
                QW = min(512, T)
                nqs = QW // 128
                nqb = T // QW
                LOOK = 2

                def load_Q(qb):
                    QTm = []
                    for m in range(2):
                        qt_, qtb_ = QTz[m].next()
                        mk.dma("sp", qt_[m * 64:(m + 1) * 64, :, 0:QW], QDTv[m * 64:(m + 1) * 64, :, t0 + qb * QW:t0 + (qb + 1) * QW], writes=[qtb_])
                        QTm.append((qt_, qtb_))
                    return QTm

                def emit_S(QTm, h, kc):
                    ps_, psb = pS2.next()

                    def mmS(e):
                        for m in range(2):
                            ins = e.matmul(ps_[:, m * 512:m * 512 + QW], lhsT=KT[:, h, kc * 128:(kc + 1) * 128], rhs=QTm[m][0][:, h, 0:QW], start=True, stop=True)
                        return ins
                    mk.op("pe", mmS, reads=[KTb[kc // 4], QTm[0][1], QTm[1][1], cb], writes=[psb])
                    pt, ptb = ptp.next()
                    if QW == 512:
                        mk.op("act", lambda e: e.activation(out=pt[:], in_=ps_[:], func=AF.Exp, scale=0.125), reads=[psb], writes=[ptb])
                    else:
                        mk.op("act", lambda e: e.activation(out=pt[:].rearrange("p (m q) -> p m q", m=2)[:, :, 0:QW], in_=ps_[:].rearrange("p (m q) -> p m q", m=2)[:, :, 0:QW],
                                                            func=AF.Exp, scale=0.125), reads=[psb], writes=[ptb])
                    return pt, ptb

                def emit_PV(h, kc, pt, ptb, zero_first=False):
                    for b_ in range(3):
                        accs_b = [(m, qs) for m in range(2) for qs in range(nqs) if (m * 4 + qs) // 3 == b_]
                        if not accs_b:
                            continue
                        if zero_first:
                            mk.op("pe", lambda e, b_=b_: e.matmul(acc[b_][:], lhsT=zeros[0:1, 0:128], rhs=zeros[0:1, 0:512], start=True, stop=False, skip_group_check=True),
                                  reads=[cb], writes=[accb[b_]])

                        def pv(e, accs_b=accs_b):
                            for m, qs in accs_b:
                                a = m * 4 + qs
                                c0 = (a % 3) * 129
                                ins = e.matmul(acc[a // 3][:, c0:c0 + 129], lhsT=pt[:, m * 512 + qs * 128:m * 512 + (qs + 1) * 128], rhs=V[:, kc, h, 0:129],
                                               start=False, stop=(kc == nkc - 1), skip_group_check=True)
                            return ins
                        mk.op("pe", pv, reads=[ptb, Vb[kc]], writes=[accb[b_]])

                def finalize(h, ods):
                    acs, acsb0 = accsp.next()
                    acsbs = [Buf(), Buf(), Buf()]
                    for b_ in sorted({(m * 4 + qs) // 3 for m in range(2) for qs in range(nqs)}):
                        mk.op("dve", lambda e, b_=b_: e.tensor_copy(out=acs[:, b_, :], in_=acc[b_][:]), reads=[accb[b_]], writes=[acsb0, acsbs[b_]])
                    acsb = acsb0

                    def fin(qs):
                        a1i, a2i = qs, 4 + qs
                        A1 = acs[:, a1i // 3, (a1i % 3) * 129:(a1i % 3) * 129 + 129]
                        A2 = acs[:, a2i // 3, (a2i % 3) * 129:(a2i % 3) * 129 + 129]
                        sm, smb = smp.next()
                        mk.op("dve", lambda e: e.reciprocal(out=sm[:, 0:1], in_=A1[:, 128:129]), reads=[acsb], writes=[smb])
                        mk.op("dve", lambda e: e.reciprocal(out=sm[:, 1:2], in_=A2[:, 128:129]), reads=[acsb], writes=[smb])
                        mk.op("dve", lambda e: e.tensor_tensor(out=sm[:, 2:3], in0=sm[:, 1:2], in1=self.lam[:, l, 1:2], op=ALU.mult), reads=[smb, self.cbuf], writes=[smb])
                        o1, o1b = o1p.next()
                        mk.op("dve", lambda e: e.tensor_scalar(out=o1[:], in0=A1[:, 0:128], scalar1=sm[:, 0:1], scalar2=None, op0=ALU.mult), reads=[acsb, smb], writes=[o1b])
                        o2, o2b = o2p.next()
                        mk.op("dve", lambda e: e.scalar_tensor_tensor(out=o2[:], in0=A2[:, 0:128], scalar=sm[:, 2:3], in1=o1[:], op0=ALU.mult, op1=ALU.add),
                              reads=[acsb, smb, o1b], writes=[o2b])
                        jk, jkb = jkp.next()
                        mk.op("dve", lambda e: e.tensor_tensor(out=jk[:], in0=o2[:], in1=o2[:], op=ALU.mult), reads=[o2b], writes=[jkb])
                        mk.op("dve", lambda e: e.reduce_sum(out=sm[:, 3:4], in_=jk[:], axis=AX.X), reads=[jkb], writes=[smb])
                        yield
                        mk.op("dve", lambda e: e.tensor_scalar(out=sm[:, 4:5], in0=sm[:, 3:4], scalar1=1.0 / 128, scalar2=EPS, op0=ALU.mult, op1=ALU.add), reads=[smb], writes=[smb])
                        mk.op("pool", lambda e: e.tensor_tensor(out=sm[:, 4:5], in0=sm[:, 4:5], in1=negh[:], op=ALU.pow), reads=[smb, cb], writes=[smb])
                        yield
                        od, odb = ods[qs]
                        mk.op("dve", lambda e: e.scalar_tensor_tensor(out=od[:, h, :], in0=o2[:], scalar=sm[:, 4:5], in1=gdn[:], op0=ALU.mult, op1=ALU.mult),
                              reads=[o2b, smb, cb], writes=[odb])
                    run_pipeline((fin(qs) for qs in range(nqs)), 4)

                def qblock_end(qb, ods):
                    s4, s4b = st4p.next()
                    for qs in range(nqs):
                        od, odb = ods[qs]
                        pT, pTb = pTo.next()

                        def tr(e, od=od, pT=pT):
                            for h in range(4):
                                ins = e.transpose(out=pT[:, h, :], in_=od[:, h, :], identity=self.ident_b[:])
                            return ins
                        mk.op("pe", tr, reads=[odb], writes=[pTb])
                        mk.op("dve", lambda e, s4=s4, pT=pT, qs=qs: e.tensor_copy(out=s4[:, :, qs * 128:(qs + 1) * 128], in_=pT[:]), reads=[pTb], writes=[s4b])
                    mk.dma("sp", ODTv[:, :, t0 + qb * QW:t0 + (qb + 1) * QW], s4[:, :, 0:QW], reads=[s4b])

                allsteps = [(qb, h, kc) for qb in range(nqb) for h in range(4) for kc in range(nkc)]
                Q = {}
                odss = {}
                pend = []
                for i in range(len(allsteps) + LOOK):
                    if i < len(allsteps):
                        qb, h, kc = allsteps[i]
                        if h == 0 and kc == 0:
                            if qb == 0:
                                Q[0] = load_Q(0)
                            if qb + 1 < nqb:
                                Q[qb + 1] = load_Q(qb + 1)
                        pend.append(emit_S(Q[qb], h, kc))
                    if i >= LOOK:
                        qb, h, kc = allsteps[i - LOOK]
                        if kc == 0:
                            if h == 0:
                                odss[qb] = [odp.next() for _ in range(nqs)]
                        emit_PV(h, kc, *pend[i - LOOK], zero_first=(kc == 0))
                        pend[i - LOOK] = None
                        if kc == nkc - 1:
                            finalize(h, odss[qb])
                            if h == 3:
                                qblock_end(qb, odss[qb])

    def phase_F1(self, l):
        mk, dr = self.mk, self.dram
        with ExitStack() as st:
            wbr = self.sb(st, "f_wbr", [128, 3, 4, D], BF16)
            wbbs = [Buf(), Buf(), Buf()]
            wst = self.sbpool(st, "f_wst", [128, D], F32, 3)
            for br, nm in enumerate(("w_fou", "w_gla_o", "w_diff_o")):
                for g in range(4):
                    ws, wsb = wst.next()
                    mk.dma("sp", ws[:], dr[nm][l, g * 128:(g + 1) * 128, :], writes=[wsb])
                    ce = ("pool", "dve", "act")[(br * 4 + g) % 3]
                    if ce == "act":
                        mk.op("act", lambda e, ws=ws, br=br, g=g: e.activation(out=wbr[:, br, g, :], in_=ws[:], func=AF.Copy), reads=[wsb], writes=[wbbs[(br * 4 + g) % 3]])
                    else:
                        mk.op(ce, lambda e, ws=ws, br=br, g=g: e.tensor_copy(out=wbr[:, br, g, :], in_=ws[:]), reads=[wsb], writes=[wbbs[(br * 4 + g) % 3]])
            f3p = self.sbpool(st, "f_f3", [128, 3, 4, 512], BF16, 2)
            gtp = self.sbpool(st, "f_gt", [128, 512], BF16, 12)
            pX = self.pspool(st, "f_pX", [128, 512], F32, 6)
            tp = self.sbpool(st, "f_t", [128, 512], BF16, 9)
            mp = self.sbpool(st, "f_m", [128, 512], BF16, 4)
            srcs = [dr[n].rearrange("(g p) t -> p g t", p=128) for n in ("FT", "OGT", "ODT")]
            def load_f3(tb):
                f3, f3b = f3p.next()
                for br in range(3):
                    mk.dma("sp", f3[:, br, :, :], srcs[br][:, :, tb * 512:(tb + 1) * 512], writes=[f3b])
                return f3, f3b
            nxt_f3 = load_f3(0)
            for tb in range(self.NB):
                f3, f3b = nxt_f3
                if tb + 1 < self.NB:
                    nxt_f3 = load_f3(tb + 1)
                gts = {}

                def load_gt(oc):
                    for br in range(3):
                        gt, gtb = gtp.next()
                        mk.dma("sp", gt[:], dr["GT"][(br * 8 + oc) * 128:(br * 8 + oc + 1) * 128, tb * 512:(tb + 1) * 512], writes=[gtb])
                        gts[(oc, br)] = (gt, gtb)
                load_gt(0)
                load_gt(1)
                def oc_gen(oc, tb=tb, f3=f3, f3b=f3b, gts=gts, load_gt=load_gt):
                    if oc + 2 < 8:
                        load_gt(oc + 2)
                    pxs = []
                    for br in range(3):
                        px, pxb = pX.next()

                        def mm(e, px=px, br=br):
                            for g in range(4):
                                ins = e.matmul(px[:], lhsT=wbr[:, br, g, oc * 128:(oc + 1) * 128], rhs=f3[:, br, g, :], start=(g == 0), stop=(g == 3))
                            return ins
                        mk.op("pe", mm, reads=wbbs + [f3b], writes=[pxb])
                        pxs.append((px, pxb))
                    yield
                    ts_ = []
                    for br in range(3):
                        gt, gtb = gts[(oc, br)]
                        px, pxb = pxs[br]
                        t_, tb_ = tp.next()
                        mk.op("dve", lambda e, t_=t_, px=px, gt=gt: e.tensor_tensor(out=t_[:], in0=px[:], in1=gt[:], op=ALU.mult), reads=[pxb, gtb], writes=[tb_])
                        ts_.append((t_, tb_))
                    m_, mb = mp.next()
                    mk.op("dve", lambda e: e.tensor_tensor(out=m_[:], in0=ts_[0][0][:], in1=ts_[1][0][:], op=ALU.add), reads=[ts_[0][1], ts_[1][1]], writes=[mb])
                    mk.op("dve", lambda e: e.tensor_tensor(out=m_[:], in0=m_[:], in1=ts_[2][0][:], op=ALU.add), reads=[ts_[2][1], mb], writes=[mb])
                    mk.dma("sp", dr["MT"][oc * 128:(oc + 1) * 128, tb * 512:(tb + 1) * 512], m_[:], reads=[mb])
                run_pipeline((oc_gen(oc) for oc in range(8)), 2)

    def resid_phase(self, l, pfx, act_name, nk, w_name, g_col, norm_l, norm_which, final):
        mk, dr = self.mk, self.dram
        with ExitStack() as st:
            self.norm_setup(st)
            W = self.sb(st, pfx + "_W", [128, nk, D], BF16)
            Wbs = [Buf(), Buf(), Buf()]
            xp = self.sbpool(st, pfx + "_x", [128, D], F32, 4)
            for j in range(nk):
                ws, wsb = xp.next()
                mk.dma("sp", ws[:], dr[w_name][l, j * 128:(j + 1) * 128, :], writes=[wsb])
                ce = ("pool", "dve", "act")[j % 3]
                if ce == "act":
                    mk.op("act", lambda e, ws=ws, j=j: e.activation(out=W[:, j, :], in_=ws[:], func=AF.Copy), reads=[wsb], writes=[Wbs[j % 3]])
                else:
                    mk.op(ce, lambda e, ws=ws, j=j: e.tensor_copy(out=W[:, j, :], in_=ws[:]), reads=[wsb], writes=[Wbs[j % 3]])
            gb = self.sb(st, pfx + "_g", [128, 2, D], F32)
            gbb = Buf()
            for r in range(2):
                mk.dma("sp", gb[:, r, :], dr["MOD"][l, r, g_col * D:(g_col + 1) * D].partition_broadcast(128), writes=[gbb])
            ap_ = self.sbpool(st, pfx + "_a", [128, nk, 512], BF16, 2)
            po = self.pspool(st, pfx + "_po", [128, 512], F32, 4)
            tp = self.sbpool(st, pfx + "_t", [128, 512], F32, 2)
            src = dr[act_name].rearrange("(j p) t -> p j t", p=128)
            self.filler_setup(st)
            nfill = 8 if nk <= 8 else 4
            blocks = {}

            def load_a(tb):
                a_, ab = ap_.next()
                mk.dma("sp", a_[:], src[:, :, tb * 512:(tb + 1) * 512], writes=[ab])
                blocks[tb] = (a_, ab)
            load_a(0)

            def tile_gen(t):
                tb, tl = t // 4, t % 4
                if tl == 1 and tb + 1 < self.NB:
                    load_a(tb + 1)
                a_, ab = blocks[tb]
                cond = self.cond_of_tile(t)
                xt, xb = xp.next()
                mk.dma("sp", xt[:], dr["X"][t * 128:(t + 1) * 128, :], writes=[xb])
                ps = []
                for half in range(2):
                    p_, pb = po.next()

                    def mm(e, p_=p_, half=half):
                        for j in range(nk):
                            ins = e.matmul(p_[:], lhsT=a_[:, j, tl * 128:(tl + 1) * 128], rhs=W[:, j, half * 512:(half + 1) * 512], start=(j == 0), stop=(j == nk - 1))
                        return ins
                    mk.op("pe", mm, reads=[ab] + Wbs, writes=[pb])
                    ps.append((p_, pb))
                self.filler(nfill)
                yield
                for half in range(2):
                    p_, pb = ps[half]
                    t_, tb_ = tp.next()
                    mk.op("dve", lambda e, t_=t_, p_=p_, half=half: e.tensor_tensor(out=t_[:], in0=p_[:], in1=gb[:, cond, half * 512:(half + 1) * 512], op=ALU.mult),
                          reads=[pb, gbb], writes=[tb_])
                    mk.op("dve", lambda e, t_=t_, half=half: e.tensor_tensor(out=xt[:, half * 512:(half + 1) * 512], in0=xt[:, half * 512:(half + 1) * 512], in1=t_[:], op=ALU.add),
                          reads=[tb_, xb], writes=[xb])
                if final:
                    if t < 4:
                        mk.dma("sp", dr["yp"][t * 128:(t + 1) * 128, :], xt[:], reads=[xb])
                    else:
                        mk.dma("sp", dr["ys"][(t - 4) * 128:(t - 3) * 128, :], xt[:], reads=[xb])
                else:
                    mk.dma("sp", dr["X"][t * 128:(t + 1) * 128, :], xt[:], reads=[xb])
                    yield from self.norm_gen(xt[:], xb, t, norm_l, norm_which)
            run_pipeline((tile_gen(t) for t in range(self.NT)), 5)

    def phase_F2(self, l):
        self.resid_phase(l, "f2", "MT", 8, "w_out", 2, l, 2, False)

    def phase_I(self, l):
        last = (l == self.L - 1)
        self.resid_phase(l, "i", "AT", NFF, "w_ff_down", 5, l + 1, 1, last)

    def phase_H(self, l):
        mk, dr = self.mk, self.dram
        hT, hTb, NB = self.hT, self.hT_b, self.NB
        with ExitStack() as st:
            wf = self.sbpool(st, "h_wf", [128, NKC, 256], F32, 3)
            wg = self.sbpool(st, "h_wg", [128, NKC, 512], BF16, 2)
            wu = self.sbpool(st, "h_wu", [128, NKC, 512], BF16, 2)
            pg = self.pspool(st, "h_pg", [128, 512], F32, 3)
            pu = self.pspool(st, "h_pu", [128, 512], F32, 3)
            sgp = self.sbpool(st, "h_sg", [128, 512], F32, 3)
            stp = self.sbpool(st, "h_st", [128, 512], BF16, 4)

            def load(pool, wname, c0, W):
                wt, wbuf = pool.next()
                for h0 in range(0, W, 256):
                    ft, fb = wf.next()
                    mk.dma("sp", ft[:], dr[wname][l][:, c0 + h0:c0 + h0 + 256].rearrange("(kc p) n -> p kc n", p=128), writes=[fb])
                    mk.op("pool", lambda e, ft=ft, h0=h0, wt=wt: e.tensor_copy(out=wt[:, :, h0:h0 + 256], in_=ft[:]), reads=[fb], writes=[wbuf])
                return wt, wbuf
            units = [(c0, min(512, DFF - c0)) for c0 in range(0, DFF, 512)]
            nxt = (load(wg, "w_ff_gate", *units[0]), load(wu, "w_ff_up", *units[0]))
            for ui, (c0, W) in enumerate(units):
                (wgt, wgb), (wut, wub) = nxt
                if ui + 1 < len(units):
                    nxt = (load(wg, "w_ff_gate", *units[ui + 1]), load(wu, "w_ff_up", *units[ui + 1]))
                for cc in range(W // 128):
                    for tb in range(NB):
                        g_, gb_ = pg.next()
                        u_, ub_ = pu.next()

                        def mm(e, g_=g_, u_=u_, cc=cc, tb=tb, wgt=wgt, wut=wut):
                            for kc in range(NKC):
                                e.matmul(g_[:], lhsT=wgt[:, kc, cc * 128:(cc + 1) * 128], rhs=hT[:, kc, tb * 512:(tb + 1) * 512], start=(kc == 0), stop=(kc == NKC - 1))
                            for kc in range(NKC):
                                ins = e.matmul(u_[:], lhsT=wut[:, kc, cc * 128:(cc + 1) * 128], rhs=hT[:, kc, tb * 512:(tb + 1) * 512], start=(kc == 0), stop=(kc == NKC - 1))
                            return ins
                        mk.op("pe", mm, reads=[wgb, wub] + hTb[tb * 4:(tb + 1) * 4], writes=[gb_, ub_])
                        sg, sgb = sgp.next()
                        mk.op("act", lambda e, sg=sg, g_=g_: e.activation(out=sg[:], in_=g_[:], func=AF.Silu), reads=[gb_], writes=[sgb])
                        s_, sb_ = stp.next()
                        mk.op("dve", lambda e, s_=s_, sg=sg, u_=u_: e.tensor_tensor(out=s_[:], in0=u_[:], in1=sg[:], op=ALU.mult), reads=[ub_, sgb], writes=[sb_])
                        mk.dma("sp", dr["AT"][c0 + cc * 128:c0 + (cc + 1) * 128, tb * 512:(tb + 1) * 512], s_[:], reads=[sb_])


def _bf(a):
    return np.ascontiguousarray(a.astype(ml_dtypes.bfloat16))


def host_consts(TS):
    k = {}
    k["k_ident"] = np.eye(128, dtype=np.float32)
    c = np.arange(128)
    ang = 2 * np.pi * np.outer(c, c) / 128.0
    k["k_csc"] = (np.concatenate([np.cos(ang), np.sin(ang)], axis=1) / np.sqrt(128.0)).astype(np.float32)
    for nm, T in (("P", SPL), ("S", TS)):
        t = np.arange(T, dtype=np.int64)
        ang = 2 * np.pi * ((np.outer(t, t) % T).astype(np.float64)) / T
        k["k_ct" + nm] = _bf(np.cos(ang) / np.sqrt(T))
        k["k_nst" + nm] = _bf(-np.sin(ang) / np.sqrt(T))
    j = np.arange(64)[:, None]
    i = np.arange(64)[None, :]
    tri = np.zeros((64, 6, 64), np.float32)
    tri[:, 0, :] = (j <= i) / 16.0
    tri[:, 1, :] = (j >= i) / 16.0
    tri[:, 2, :] = (j > i) / 16.0
    tri[:, 3, :] = (j < i) / 16.0
    tri[:, 4, :] = (j <= i)
    tri[:, 5, :] = (j >= i)
    k["k_tri"] = tri
    rows = TS // 64
    row = np.repeat(np.arange(rows), 64).astype(np.float32)
    col = np.tile(np.arange(64), rows).astype(np.float32)
    inv = (10000.0 ** (-np.arange(16, dtype=np.float32) * 2.0 / 32)).astype(np.float32)
    ar = row[:, None] * inv
    ac = col[:, None] * inv
    angm = np.concatenate([ar, ar, ac, ac], axis=-1)
    cos = np.cos(angm).astype(np.float32)
    sin = np.sin(angm).astype(np.float32)
    sgn = np.tile(np.concatenate([-np.ones(16), np.ones(16)]), 2).astype(np.float32)
    k["k_rope"] = np.ascontiguousarray(np.stack([cos, sin * sgn], axis=1)).astype(np.float32)
    return k


WNAMES = ["w_mod", "b_mod", "norm1", "norm2", "w_in", "w_gla_a2", "b_gla_a", "gla_norm", "diff_qk_norm", "diff_lambda",
          "diff_norm", "w_fou", "w_gla_o", "w_diff_o", "w_out", "w_ff_gate", "w_ff_up", "w_ff_down"]


def make_in_map(inp, core, L, TS, consts):
    f = lambda a: np.ascontiguousarray(np.asarray(a, dtype=np.float32))
    m = {}
    m["xp"] = f(inp["x_prompt"][2 * core:2 * core + 2]).reshape(2 * SPL, D)
    m["xs"] = f(inp["x_sample"][core]).reshape(TS, D)
    m["cvec"] = f(np.stack([np.asarray(inp["c_ctx"]), np.asarray(inp["c"][core])], axis=0))
    m["cache_k"] = f(inp["cache_diff_k"][core]).reshape(L, PAST, 512)
    m["cache_v"] = f(inp["cache_diff_v"][core]).reshape(L, PAST, 512)
    m["state"] = f(inp["state_gla"][core])
    for n in WNAMES:
        m[n] = f(inp[n])
    m.update(consts)
    return m


_CACHE = {}


def kernel(**inputs):
    L, TS, NCORE = 4, 4096, 8
    inp = {k: np.asarray(v) for k, v in inputs.items()}
    if "nc" not in _CACHE:
        _CACHE["nc"] = Builder(L=L, TS=TS).build()
        _CACHE["consts"] = host_consts(TS)
    nc = _CACHE["nc"]
    in_maps = [make_in_map(inp, c, L, TS, _CACHE["consts"]) for c in range(NCORE)]
    res = run_bass_kernel_spmd(nc, in_maps, core_ids=list(range(NCORE)))
    R = res.results
    f = lambda n: np.stack([np.asarray(R[c][n], dtype=np.float32) for c in range(NCORE)], axis=0)
    yp = f("yp").reshape(NCORE * 2, SPL, D)
    ys = f("ys").reshape(NCORE, TS, D)
    nk = f("nk").reshape(NCORE * 2, L, SPL, 4, 2, 64)
    nv = f("nv").reshape(NCORE * 2, L, SPL, 4, 128)
    ns = f("ns").reshape(NCORE * 2, L, 2, 4, 64, 128)
    return (yp, ys, nk, nv, ns)
```
